# Optimizing a Trainium2 kernel written in Bass

```python
import math
import jax
import jax.numpy as jnp
from jax import lax
import numpy as np

D_MODEL = 1024
BATCH = 32
SEQ = 2048
DEPTH = 1

CTX_LEN = 256
GRID_W = 64
SSD_D_INNER = 2 * D_MODEL
SSD_HEADDIM = 64
SSD_HEADS = SSD_D_INNER // SSD_HEADDIM
SSD_GROUPS = 8
SSD_HPG = SSD_HEADS // SSD_GROUPS
SSD_STATE = 128
SSD_CHUNK = 128
CONV_K = 4
CONV_LEFT = 2
LRU_WIDTH = D_MODEL
LRU_BLOCKS = 8
LRU_BLOCK_W = LRU_WIDTH // LRU_BLOCKS
LRU_C = 8.0
MLP_HIDDEN = 4 * D_MODEL
N_BRANCH = 2
N_MOD = 6
DEEPNORM_ALPHA = (2 * DEPTH) ** 0.25
DEEPNORM_BETA = (8 * DEPTH) ** -0.25
LN_EPS = 1e-6
RMS_EPS = 1e-5

SSD_BC_W = SSD_GROUPS * SSD_STATE
SSD_XB = SSD_D_INNER + SSD_BC_W
SSD_XBC = SSD_D_INNER + 2 * SSD_BC_W
SSD_DT = 2 * SSD_HEADS
O_DT = SSD_XB
O_LRU = O_DT + SSD_DT
STATE_COLS = O_LRU + LRU_WIDTH
O_C = STATE_COLS
O_Z = O_C + SSD_BC_W
O_LRU_GATE = O_Z + SSD_D_INNER
O_MERGE = O_LRU_GATE + LRU_WIDTH
IN_COLS = O_MERGE + N_BRANCH * D_MODEL

kernel_name = 'hybrid_ssd_rglru_dit_block'


def layer_norm(x, g=None, b=None):
    xf = x.astype(jnp.float32)
    mu = jnp.mean(xf, axis=-1, keepdims=True)
    var = jnp.mean(jnp.square(xf - mu), axis=-1, keepdims=True)
    y = (xf - mu) * lax.rsqrt(var + LN_EPS)
    if g is not None:
        y = y * g.astype(jnp.float32) + b.astype(jnp.float32)
    return y.astype(x.dtype)


def modulation(cvec, w_mod, b_mod, n_chunks):
    m = jax.nn.silu(cvec) @ w_mod[:, :n_chunks * D_MODEL] + b_mod[:n_chunks * D_MODEL]
    return jnp.split(m, n_chunks, axis=-1)


def modulate(x, shift, scale):
    return layer_norm(x) * (1.0 + scale) + shift


def short_conv(u, w, b, rows):
    bsz, t, ch = u.shape
    v = u if rows is None else u.reshape(bsz, rows, GRID_W, ch)
    n = v.shape[-2]
    pad = [(0, 0)] * (v.ndim - 2) + [(CONV_LEFT, CONV_K - 1 - CONV_LEFT), (0, 0)]
    vp = jnp.pad(v, pad)
    out = b
    for k in range(CONV_K):
        out = out + vp[..., k:k + n, :] * w[k]
    return out.reshape(bsz, t, ch)


def ssd_chunked(xh, dt, a_neg, bm, cm, h0):
    bsz, t = xh.shape[:2]
    nc = t // SSD_CHUNK
    xc = (xh.astype(jnp.float32) * dt[..., None]).reshape(bsz, nc, SSD_CHUNK, SSD_GROUPS, SSD_HPG, SSD_HEADDIM)
    cum = jnp.cumsum((dt * a_neg).reshape(bsz, nc, SSD_CHUNK, SSD_GROUPS, SSD_HPG), axis=2)
    bc = bm.astype(jnp.float32).reshape(bsz, nc, SSD_CHUNK, SSD_GROUPS, SSD_STATE)
    to_end = jnp.exp(cum[:, :, -1:] - cum)
    states = jnp.einsum('bcjgn,bcjgh,bcjghp->bcghpn', bc, to_end, xc)
    chunk_decay = jnp.exp(cum[:, :, -1])

    def step(h, inp):
        dec, st = inp
        return dec[..., None, None] * h + st, h

    h_fin, h_start = lax.scan(step, h0, (jnp.moveaxis(chunk_decay, 1, 0), jnp.moveaxis(states, 1, 0)))
    if cm is None:
        return None, h_fin
    h_start = jnp.moveaxis(h_start, 0, 1)
    cc = cm.astype(jnp.float32).reshape(bsz, nc, SSD_CHUNK, SSD_GROUPS, SSD_STATE)
    seg = cum[:, :, :, None] - cum[:, :, None, :]
    lower = jnp.tril(jnp.ones((SSD_CHUNK, SSD_CHUNK), dtype=bool))[:, :, None, None]
    decay = jnp.exp(jnp.where(lower, seg, -jnp.inf))
    cb = jnp.einsum('bcign,bcjgn->bcijg', cc, bc)
    y = (jnp.einsum('bcijg,bcijgh,bcjghp->bcighp', cb, decay, xc)
         + jnp.einsum('bcign,bcigh,bcghpn->bcighp', cc, jnp.exp(cum), h_start))
    return y.reshape(bsz, t, SSD_HEADS, SSD_HEADDIM), h_fin


def gated_rmsnorm(y, z, w):
    u = (y * jax.nn.silu(z)).astype(jnp.float32)
    ug = u.reshape(*u.shape[:-1], SSD_GROUPS, -1)
    ug = ug * lax.rsqrt(jnp.mean(jnp.square(ug), axis=-1, keepdims=True) + RMS_EPS)
    return (ug.reshape(u.shape) * w.astype(jnp.float32)).astype(y.dtype)


def ssd_branch(xb_raw, c_raw, dt_raw, p, h0_f, h0_b, rows):
    bsz, t, _ = xb_raw.shape
    xb = jax.nn.silu(short_conv(xb_raw, p['ssd_conv_w'][:, :SSD_XB], p['ssd_conv_b'][:SSD_XB], rows))
    xh = xb[..., :SSD_D_INNER].reshape(bsz, t, SSD_HEADS, SSD_HEADDIM)
    bm = xb[..., SSD_D_INNER:].reshape(bsz, t, SSD_GROUPS, SSD_STATE)
    cm = None
    if c_raw is not None:
        cm = jax.nn.silu(short_conv(c_raw, p['ssd_conv_w'][:, SSD_XB:], p['ssd_conv_b'][SSD_XB:], rows))
        cm = cm.reshape(bsz, t, SSD_GROUPS, SSD_STATE)
    dt = jax.nn.softplus(dt_raw.astype(jnp.float32).reshape(bsz, t, 2, SSD_HEADS) + p['ssd_dt_bias'])
    a_neg = -jnp.exp(p['ssd_a_log'].astype(jnp.float32))
    flip = lambda u: None if u is None else jnp.flip(u, axis=1)
    y_f, s_f = ssd_chunked(xh, dt[:, :, 0], a_neg[0], bm, cm, h0_f)
    y_b, s_b = ssd_chunked(flip(xh), flip(dt[:, :, 1]), a_neg[1], flip(bm), flip(cm), h0_b)
    if c_raw is None:
        return None, s_f, s_b
    y = y_f + flip(y_b) + p['ssd_d'][:, None] * xh
    return y.reshape(bsz, t, SSD_D_INNER).astype(xb_raw.dtype), s_f, s_b


def lru_combine(e1, e2):
    a1, b1 = e1
    a2, b2 = e2
    return a1 * a2, a2 * b1 + b2


def rglru(u, wa, ba, wi, bi, lam, h0, reverse):
    bsz, t, w = u.shape
    uf = u.astype(jnp.float32)
    ub = uf.reshape(bsz, t, LRU_BLOCKS, LRU_BLOCK_W)
    r = jax.nn.sigmoid(jnp.einsum('btkc,kcd->btkd', ub, wa).reshape(bsz, t, w) + ba)
    i = jax.nn.sigmoid(jnp.einsum('btkc,kcd->btkd', ub, wi).reshape(bsz, t, w) + bi)
    log_a = -LRU_C * r * jax.nn.softplus(-lam)
    a = jnp.exp(log_a)
    b_in = jnp.sqrt(-jnp.expm1(2.0 * log_a)) * (i * uf)
    edge = t - 1 if reverse else 0
    b_in = b_in.at[:, edge].add(a[:, edge] * h0)
    _, h = lax.associative_scan(lru_combine, (a, b_in), reverse=reverse, axis=1)
    return h, h[:, 0 if reverse else t - 1]


def lru_branch(u_raw, p, h0_f, h0_b, rows, need_y):
    u = short_conv(u_raw, p['lru_conv_w'], p['lru_conv_b'], rows)
    h_f, s_f = rglru(u, p['lru_wa'][0], p['lru_ba'][0], p['lru_wi'][0], p['lru_bi'][0], p['lru_lambda'][0], h0_f, False)
    h_b, s_b = rglru(u, p['lru_wa'][1], p['lru_ba'][1], p['lru_wi'][1], p['lru_bi'][1], p['lru_lambda'][1], h0_b, True)
    if not need_y:
        return None, s_f, s_b
    return (h_f + h_b).astype(u_raw.dtype), s_f, s_b


def token_mixer(h, p, init, rows, need_out):
    cols = IN_COLS if need_out else STATE_COLS
    proj = h @ p['w_in'][:, :cols]
    xb_raw = proj[..., :O_DT]
    dt_raw = proj[..., O_DT:O_LRU]
    lru_raw = proj[..., O_LRU:STATE_COLS]
    c_raw = proj[..., O_C:O_Z] if need_out else None
    y_ssd, s_f, s_b = ssd_branch(xb_raw, c_raw, dt_raw, p, init[0], init[1], rows)
    y_lru, l_f, l_b = lru_branch(lru_raw, p, init[2], init[3], rows, need_out)
    states = (s_f, s_b, l_f, l_b)
    if not need_out:
        return None, states
    z = proj[..., O_Z:O_LRU_GATE]
    lru_gate = proj[..., O_LRU_GATE:O_MERGE]
    gates = jax.nn.sigmoid(proj[..., O_MERGE:] + p['b_gate'])
    g_ssd, g_lru = jnp.split(gates, N_BRANCH, axis=-1)
    br_ssd = gated_rmsnorm(y_ssd, z, p['ssd_norm_w']) @ p['w_br_ssd']
    br_lru = (y_lru * jax.nn.gelu(lru_gate)) @ p['w_br_lru']
    return (g_ssd * br_ssd + g_lru * br_lru) @ p['w_out'], states


def sq_relu_mlp(h, p):
    return jnp.square(jax.nn.relu(h @ p['w_mlp1'] + p['b_mlp1'])) @ p['w_mlp2'] + p['b_mlp2']


def setup_inputs(seed: int = 0) -> dict:
    key = jax.random.key(seed)
    ks = jax.random.split(key, 40)
    f32 = jnp.float32

    def nrm(k, shape, fan_in, gain=1.0):
        return jax.random.normal(k, shape, f32) * (gain * fan_in ** -0.5)

    def small(k, shape):
        return 0.01 * jax.random.normal(k, shape, f32)

    dt0 = jnp.exp(jax.random.uniform(ks[8], (DEPTH, 2, SSD_HEADS), f32, minval=math.log(1e-3), maxval=math.log(1e-1)))
    a_pow = jax.random.uniform(ks[17], (DEPTH, 2, LRU_WIDTH), f32, minval=0.9, maxval=0.999)
    a_base = a_pow ** (1.0 / LRU_C)
    return {
        'x': jax.random.normal(ks[0], (BATCH, SEQ, D_MODEL), f32),
        'c': jax.random.normal(ks[1], (BATCH, D_MODEL), f32),
        'ctx': jax.random.normal(ks[2], (BATCH, CTX_LEN, D_MODEL), f32),
        'c_ctx': jax.random.normal(ks[3], (D_MODEL,), f32),
        'w_mod': nrm(ks[4], (DEPTH, D_MODEL, N_MOD * D_MODEL), D_MODEL),
        'b_mod': small(ks[5], (DEPTH, N_MOD * D_MODEL)),
        'w_in': nrm(ks[6], (DEPTH, D_MODEL, IN_COLS), D_MODEL),
        'b_gate': small(ks[7], (DEPTH, N_BRANCH * D_MODEL)),
        'ssd_conv_w': nrm(ks[9], (DEPTH, CONV_K, SSD_XBC), CONV_K),
        'ssd_conv_b': small(ks[10], (DEPTH, SSD_XBC)),
        'ssd_dt_bias': dt0 + jnp.log(-jnp.expm1(-dt0)),
        'ssd_a_log': jnp.log(jax.random.uniform(ks[11], (DEPTH, 2, SSD_HEADS), f32, minval=1.0, maxval=16.0)),
        'ssd_d': 1.0 + small(ks[12], (DEPTH, SSD_HEADS)),
        'ssd_norm_w': 1.0 + small(ks[13], (DEPTH, SSD_D_INNER)),
        'lru_conv_w': nrm(ks[14], (DEPTH, CONV_K, LRU_WIDTH), CONV_K),
        'lru_conv_b': small(ks[15], (DEPTH, LRU_WIDTH)),
        'lru_wa': nrm(ks[16], (DEPTH, 2, LRU_BLOCKS, LRU_BLOCK_W, LRU_BLOCK_W), LRU_BLOCK_W),
        'lru_ba': small(ks[18], (DEPTH, 2, LRU_WIDTH)),
        'lru_wi': nrm(ks[19], (DEPTH, 2, LRU_BLOCKS, LRU_BLOCK_W, LRU_BLOCK_W), LRU_BLOCK_W),
        'lru_bi': small(ks[20], (DEPTH, 2, LRU_WIDTH)),
        'lru_lambda': jnp.log(a_base) - jnp.log1p(-a_base),
        'w_br_ssd': nrm(ks[21], (DEPTH, SSD_D_INNER, D_MODEL), SSD_D_INNER, DEEPNORM_BETA),
        'w_br_lru': nrm(ks[22], (DEPTH, LRU_WIDTH, D_MODEL), LRU_WIDTH, DEEPNORM_BETA),
        'w_out': nrm(ks[23], (DEPTH, D_MODEL, D_MODEL), D_MODEL, DEEPNORM_BETA),
        'ln1_g': 1.0 + small(ks[24], (DEPTH, D_MODEL)),
        'ln1_b': small(ks[25], (DEPTH, D_MODEL)),
        'w_mlp1': nrm(ks[26], (DEPTH, D_MODEL, MLP_HIDDEN), D_MODEL),
        'b_mlp1': small(ks[27], (DEPTH, MLP_HIDDEN)),
        'w_mlp2': nrm(ks[28], (DEPTH, MLP_HIDDEN, D_MODEL), MLP_HIDDEN, DEEPNORM_BETA),
        'b_mlp2': small(ks[29], (DEPTH, D_MODEL)),
        'ln2_g': 1.0 + small(ks[30], (DEPTH, D_MODEL)),
        'ln2_b': small(ks[31], (DEPTH, D_MODEL)),
    }


def reference(x, c, ctx, c_ctx, w_mod, b_mod, w_in, b_gate, ssd_conv_w, ssd_conv_b, ssd_dt_bias,
              ssd_a_log, ssd_d, ssd_norm_w, lru_conv_w, lru_conv_b, lru_wa, lru_ba, lru_wi, lru_bi,
              lru_lambda, w_br_ssd, w_br_lru, w_out, ln1_g, ln1_b, w_mlp1, b_mlp1, w_mlp2, b_mlp2,
              ln2_g, ln2_b):
    bsz = x.shape[0]
    rows = x.shape[1] // GRID_W
    for l in range(DEPTH):
        p = dict(w_in=w_in[l], b_gate=b_gate[l], ssd_conv_w=ssd_conv_w[l], ssd_conv_b=ssd_conv_b[l],
                 ssd_dt_bias=ssd_dt_bias[l], ssd_a_log=ssd_a_log[l], ssd_d=ssd_d[l], ssd_norm_w=ssd_norm_w[l],
                 lru_conv_w=lru_conv_w[l], lru_conv_b=lru_conv_b[l], lru_wa=lru_wa[l], lru_ba=lru_ba[l],
                 lru_wi=lru_wi[l], lru_bi=lru_bi[l], lru_lambda=lru_lambda[l], w_br_ssd=w_br_ssd[l],
                 w_br_lru=w_br_lru[l], w_out=w_out[l], w_mlp1=w_mlp1[l], b_mlp1=b_mlp1[l],
                 w_mlp2=w_mlp2[l], b_mlp2=b_mlp2[l])
        last = l == DEPTH - 1
        zero_ssd = jnp.zeros((bsz, SSD_GROUPS, SSD_HPG, SSD_HEADDIM, SSD_STATE), jnp.float32)
        zero_lru = jnp.zeros((bsz, LRU_WIDTH), jnp.float32)
        mc = modulation(c_ctx, w_mod[l], b_mod[l], 2 if last else N_MOD)
        ctx_mix, ctx_states = token_mixer(modulate(ctx, mc[0], mc[1]), p,
                                          (zero_ssd, zero_ssd, zero_lru, zero_lru), None, not last)
        mx = [m[:, None, :] for m in modulation(c, w_mod[l], b_mod[l], N_MOD)]
        x_mix, _ = token_mixer(modulate(x, mx[0], mx[1]), p, ctx_states, rows, True)
        x = layer_norm(DEEPNORM_ALPHA * x + mx[2] * x_mix, ln1_g[l], ln1_b[l])
        x = layer_norm(DEEPNORM_ALPHA * x + mx[5] * sq_relu_mlp(modulate(x, mx[3], mx[4]), p), ln2_g[l], ln2_b[l])
        if not last:
            ctx = layer_norm(DEEPNORM_ALPHA * ctx + mc[2] * ctx_mix, ln1_g[l], ln1_b[l])
            ctx = layer_norm(DEEPNORM_ALPHA * ctx + mc[5] * sq_relu_mlp(modulate(ctx, mc[3], mc[4]), p),
                             ln2_g[l], ln2_b[l])
    return x
```

```python
import numpy as np
from concourse.bass_utils import run_bass_kernel_spmd
import concourse.bass as bass
import concourse.mybir as mybir

F32 = mybir.dt.float32
BF16 = mybir.dt.bfloat16
AF = mybir.ActivationFunctionType
ALU = mybir.AluOpType
AX = mybir.AxisListType


class TB:
    __slots__ = ("w", "rs", "name")

    def __init__(self, name=""):
        self.w = None
        self.rs = []
        self.name = name


class Op:
    __slots__ = ("eng", "fn", "deps", "idx", "eidx", "signal", "tok", "isdma")


COMPUTE = ("pe", "act", "dve", "pool")
ENGS = ("pe", "act", "dve", "pool", "sp")
EPOCH = 12000
NDMASEM = 12


class Prog:
    def __init__(self, nc, sem_stack):
        self.nc = nc
        self.ops = []
        self.ecount = {e: 0 for e in ENGS}
        self.last = {e: None for e in ENGS}
        self.sem_stack = sem_stack
        self.esems = {e: [] for e in COMPUTE}
        self.dsems = {}
        self.dma_count = {}
        self.dma_semval = {}
        self.dma_prev = {}
        self.seen = {e: {} for e in ENGS}
        self.tickets = {e: 0 for e in COMPUTE}
        self.emitted = 0
        self.out_tokens = []

    def _sem(self, name):
        return self.sem_stack.enter_context(self.nc.semaphore(name))

    def add(self, eng, fn, reads=(), writes=(), dma=False):
        op = Op()
        op.eng = eng
        op.fn = fn
        op.isdma = dma
        op.idx = len(self.ops)
        op.eidx = self.ecount[eng]
        self.ecount[eng] += 1
        op.signal = False
        op.tok = None
        deps = {}
        for r in reads:
            if r.w is not None:
                deps[r.w.idx] = r.w
        for w in writes:
            if w.w is not None:
                deps[w.w.idx] = w.w
            for o in w.rs:
                deps[o.idx] = o
        deps.pop(op.idx, None)
        best = {}
        out = []
        for d in deps.values():
            if d.isdma:
                out.append(d)
            else:
                b = best.get(d.eng)
                if b is None or d.idx > b.idx:
                    best[d.eng] = d
        for e, d in best.items():
            if e == eng and not dma:
                if eng == "pe":
                    continue
                if op.eidx - d.eidx > 1:
                    continue
            out.append(d)
        for d in out:
            d.signal = True
        op.deps = out
        for r in reads:
            r.rs.append(op)
        for w in writes:
            w.w = op
            w.rs = []
        self.ops.append(op)
        self.last[eng] = op
        return op

    def dma(self, eng, out, in_, reads=(), writes=(), is_output=False):
        op = self.add(eng, lambda e: e.dma_start(out=out, in_=in_), reads, writes, dma=True)
        op.signal = True
        if is_output:
            self.out_tokens.append(op)
        return op

    def barrier(self):
        lasts = [self.last[e] for e in ENGS if self.last[e] is not None]
        dmas = [o for o in self.ops[self.emitted:] if o.isdma]
        for e in ENGS:
            op = Op()
            op.eng = e
            op.fn = None
            op.isdma = False
            op.idx = len(self.ops)
            op.eidx = self.ecount[e]
            op.signal = False
            op.tok = None
            op.deps = []
            for d in lasts + dmas:
                if d.eng == e and not d.isdma:
                    continue
                if d not in op.deps:
                    d.signal = True
                    op.deps.append(d)
            self.ops.append(op)

    def finalize(self):
        nc = self.nc
        ops = self.ops[self.emitted:]
        self.emitted = len(self.ops)
        for op in ops:
            if not op.signal:
                continue
            if op.isdma:
                e = op.eng
                if e not in self.dsems:
                    self.dsems[e] = [self._sem(f"dq_{e}_{i}") for i in range(NDMASEM)]
                    self.dma_count[e] = 0
                    self.dma_semval[e] = [0] * NDMASEM
                    self.dma_prev[e] = [None] * NDMASEM
                k = self.dma_count[e] % NDMASEM
                self.dma_count[e] += 1
                self.dma_semval[e][k] += 16
                op.tok = (self.dsems[e][k], self.dma_semval[e][k], k)
            else:
                e = op.eng
                t = self.tickets[e]
                self.tickets[e] += 1
                ep, v = divmod(t, EPOCH)
                while len(self.esems[e]) <= ep:
                    self.esems[e].append(self._sem(f"es_{e}_{len(self.esems[e])}"))
                op.tok = (self.esems[e][ep], v + 1, None)
        per = {e: [o for o in ops if o.eng == e] for e in ENGS}
        outs = list(self.out_tokens)
        self.out_tokens = []

        def emit(ename, eng):
            seen = self.seen[ename]

            def wait(tok):
                sem, val = tok[0], tok[1]
                key = id(sem)
                if seen.get(key, 0) >= val:
                    return
                eng.wait_ge(sem, val)
                seen[key] = val

            for op in per[ename]:
                for d in op.deps:
                    if d.tok is not None:
                        wait(d.tok)
                if op.isdma:
                    sem, val, k = op.tok
                    if val > 16:
                        wait((sem, val - 16))
                if op.fn is None:
                    continue
                inst = op.fn(eng)
                if op.signal:
                    inst.then_inc(op.tok[0], 16 if op.isdma else 1)
            if ename == "sp":
                for o in outs:
                    wait(o.tok)

        with nc.Block() as block:
            if per["pe"]:
                @block.tensor
                def _(e):
                    emit("pe", e)
            if per["act"]:
                @block.scalar
                def _(e):
                    emit("act", e)
            if per["dve"]:
                @block.vector
                def _(e):
                    emit("dve", e)
            if per["pool"]:
                @block.gpsimd
                def _(e):
                    emit("pool", e)
            if per["sp"] or outs:
                @block.sync
                def _(e):
                    emit("sp", e)

import numpy as np
from contextlib import ExitStack

D = 1024
SEQ = 2048
CTX = 256
TT = 256
NTILE = SEQ // TT
ALPHA = 2.0 ** 0.25
O_XBC, O_Z, O_DT, O_LRU, O_LG, O_GS, O_GL, NIN = 0, 4096, 6144, 6208, 7232, 8256, 9280, 10304
W_IN, W_BS, W_BL, W_O, W_M1, W_M2, W_LR, W_TOT = 0, 82432, 98816, 107008, 115200, 147968, 180736, 184832
CBLK = 4864
NCBLK = W_TOT // CBLK
PP_CW, PP_CB, PP_LCW, PP_LCB, PP_BA, PP_BI, PP_LAM, PP_BG, PP_NW, PP_B1, PP_BM, NPP = 0, 128, 160, 192, 200, 216, 232, 248, 264, 280, 312, 360
RB_L1G, RB_L1B, RB_L2G, RB_L2B, RB_B2, RB_BM2, RB_BM5, RB_DTB, RB_ALOG, RB_D, NRB = 0, 1024, 2048, 3072, 4096, 5120, 6144, 7168, 7232, 7296, 7328
NCONST = 768


class _Stop(Exception):
    pass


def build_program(NB, debug=None):
    nc = bass.Bass("TRN2", target_bir_lowering=False)
    es = ExitStack()
    with es:
        return _build(nc, es, NB, debug)


def _build(nc, es, NB, debug):
    xin = nc.dram_tensor("xin", [NB, SEQ, D], F32, kind="ExternalInput").ap()
    ctxin = nc.dram_tensor("ctxin", [NB, CTX, D], F32, kind="ExternalInput").ap()
    ccT = nc.dram_tensor("ccT", [128, 8 * 5], F32, kind="ExternalInput").ap()
    wmod = nc.dram_tensor("wmod", [128, 8 * 6144], F32, kind="ExternalInput").ap()
    wall = nc.dram_tensor("wall", [128, W_TOT], F32, kind="ExternalInput").ap()
    ppd = nc.dram_tensor("pp", [128, NPP], F32, kind="ExternalInput").ap()
    rbd = nc.dram_tensor("rb", [NRB], F32, kind="ExternalInput").ap()
    cstd = nc.dram_tensor("consts", [128, NCONST], F32, kind="ExternalInput").ap()
    outd = nc.dram_tensor("out", [NB, SEQ, D], F32, kind="ExternalOutput").ap()
    wbf = nc.dram_tensor("wbf", [128, W_TOT], BF16, kind="Internal").ap()
    gsc = nc.dram_tensor("gsc", [NB * 2 * 128, 1024], F32, kind="Internal").ap()
    hbb = nc.dram_tensor("hbb", [NTILE * 128, 2048], F32, kind="Internal").ap()

    P = Prog(nc, es)
    dbg = {}
    if debug:
        dbg["modF"] = nc.dram_tensor("d_modF", [128, 160], F32, kind="ExternalOutput").ap()
        dbg["gsc"] = nc.dram_tensor("d_gsc", [NB * 2 * 128, 1024], F32, kind="ExternalOutput").ap()
        dbg["hTf"] = nc.dram_tensor("d_hTf", [128, 2048], F32, kind="ExternalOutput").ap()
        dbg["hTb"] = nc.dram_tensor("d_hTb", [128, 2048], F32, kind="ExternalOutput").ap()
        dbg["lruc"] = nc.dram_tensor("d_lruc", [128, 16], F32, kind="ExternalOutput").ap()
        dbg["hT"] = nc.dram_tensor("d_hT", [128, 8 * TT], BF16, kind="ExternalOutput").ap()
        dbg["xs"] = nc.dram_tensor("d_xs", [128, 2048], F32, kind="ExternalOutput").ap()
        dbg["unT"] = nc.dram_tensor("d_unT", [128, 16 * TT], BF16, kind="ExternalOutput").ap()
        dbg["ylg"] = nc.dram_tensor("d_ylg", [128, 8 * TT], BF16, kind="ExternalOutput").ap()
        dbg["merged"] = nc.dram_tensor("d_merged", [128, 8 * TT], BF16, kind="ExternalOutput").ap()
        dbg["xbc"] = nc.dram_tensor("d_xbc", [128, 4 * TT], BF16, kind="ExternalOutput").ap()
        dbg["dtt"] = nc.dram_tensor("d_dtt", [128, 128], F32, kind="ExternalOutput").ap()
        dbg["eall"] = nc.dram_tensor("d_eall", [128, 384], F32, kind="ExternalOutput").ap()
        dbg["xtok"] = nc.dram_tensor("d_xtok", [128, 512], BF16, kind="ExternalOutput").ap()
        dbg["btok"] = nc.dram_tensor("d_btok", [128, 256], BF16, kind="ExternalOutput").ap()

    def sb(name, shape, dt=F32):
        return es.enter_context(nc.sbuf_tensor(name, shape, dt))

    def act(out, in_, func, reads, writes, bias=None, scale=None, accum=None):
        kw = {}
        if bias is not None:
            kw["bias"] = bias
        if scale is not None:
            kw["scale"] = scale
        if accum is not None:
            kw["accum_out"] = accum
        return P.add("act", lambda e: e.activation(out=out, in_=in_, func=func, **kw), reads, writes)

    def tt(eng, out, in0, in1, op, reads, writes):
        return P.add(eng, lambda e: e.tensor_tensor(out=out, in0=in0, in1=in1, op=op), reads, writes)

    def ts(eng, out, in0, s1, s2, op0, op1, reads, writes):
        if op1 is None:
            return P.add(eng, lambda e: e.tensor_scalar(out=out, in0=in0, scalar1=s1, scalar2=None, op0=op0), reads, writes)
        return P.add(eng, lambda e: e.tensor_scalar(out=out, in0=in0, scalar1=s1, scalar2=s2, op0=op0, op1=op1), reads, writes)

    def stt(eng, out, in0, scalar, in1, op0, op1, reads, writes):
        return P.add(eng, lambda e: e.scalar_tensor_tensor(out=out, in0=in0, scalar=scalar, in1=in1, op0=op0, op1=op1), reads, writes)

    def stt_p(out, in0, scalar, in1, op0, op1, reads, writes):
        return stt("dve", out, in0, scalar, in1, op0, op1, reads, writes)

    def cp(eng, out, in_, reads, writes):
        if eng == "act":
            return P.add("act", lambda e: e.activation(out=out, in_=in_, func=AF.Identity), reads, writes)
        return P.add(eng, lambda e: e.tensor_copy(out=out, in_=in_), reads, writes)

    def affine(eng, out, in_, scale_ap, bias_ap, reads, writes):
        if eng == "act":
            return act(out, in_, AF.Identity, reads, writes, bias=bias_ap, scale=scale_ap)
        return ts(eng, out, in_, scale_ap, bias_ap, ALU.mult, ALU.add, reads, writes)

    def mm(out, lhsT, rhs, start, stop, reads, writes):
        return P.add("pe", lambda e: e.matmul(out, lhsT, rhs, start=start, stop=stop), reads, writes)

    def trp(out, in_, ident, reads, writes):
        return P.add("pe", lambda e: e.transpose(out, in_, ident), reads, writes)

    def memset(eng, ap, val, writes):
        return P.add(eng, lambda e: e.memset(ap, val), (), writes)

    psum_t = [es.enter_context(nc.psum_tensor(f"ps{i}", [128, 512], F32)) for i in range(8)]
    psum_tb = [TB(f"ps{i}") for i in range(8)]
    rot = {"i": 0, "n": 8}

    def psum():
        i = rot["i"] % rot["n"]
        rot["i"] += 1
        return psum_t[i], psum_tb[i]

    def psum_bf():
        t, b = psum()
        return t[:].bitcast(BF16), b

    pp = sb("pp_s", [128, NPP])
    pp_tb = TB("pp")
    der = sb("der", [128, 256])
    der_tb = TB("der")
    D_HCW, D_HCB, D_HBA, D_HBI, D_C1H, D_HBG = 0, 128, 160, 176, 192, 208
    modF = sb("modF", [128, 4 * 8 * 5])
    modF_tb = TB("modF")
    lnt = sb("lnt", [128, 4 * 1024])
    lnt_tb = TB("lnt")
    misc = sb("misc", [128, 160])
    misc_tb = TB("misc")
    cst = sb("cst", [128, NCONST])
    cst_tb = TB("cst")
    cstb = sb("cstb", [128, NCONST], BF16)
    cstb_tb = TB("cstb")
    LE, GT, GE, LT, ONES, IDN = [cst[:, i * 128:(i + 1) * 128] for i in range(6)]
    LEb, GTb, GEb, LTb, ONESb, IDNb = [cstb[:, i * 128:(i + 1) * 128] for i in range(6)]
    lruw = sb("lruw", [128, 32 * 128], BF16)
    lruw_tb = TB("lruw")
    b2row = sb("b2row", [1, 1024], BF16)
    b2row_tb = TB("b2row")

    P.dma("sp", pp[:], ppd[:, :], writes=[pp_tb])
    P.dma("sp", cst[:], cstd[:, :], writes=[cst_tb])
    for i in range(4):
        P.dma("sp", lnt[:, i * 1024:(i + 1) * 1024], rbd[i * 1024:(i + 1) * 1024].partition_broadcast(128), writes=[lnt_tb])
    P.dma("sp", misc[:, 0:160], rbd[RB_DTB:RB_DTB + 160].partition_broadcast(128), writes=[misc_tb])
    cp("dve", cstb[:], cst[:], [cst_tb], [cstb_tb])
    act(misc[:, 64:128], misc[:, 64:128], AF.Exp, [misc_tb], [misc_tb])
    ts("dve", misc[:, 64:128], misc[:, 64:128], -1.0, None, ALU.mult, None, [misc_tb], [misc_tb])
    ts("dve", der[:, D_HCW:D_HCW + 160], pp[:, PP_CW:PP_CW + 160], 0.5, None, ALU.mult, None, [pp_tb], [der_tb])
    ts("dve", der[:, D_HBA:D_HBA + 32], pp[:, PP_BA:PP_BA + 32], 0.5, None, ALU.mult, None, [pp_tb], [der_tb])
    ts("dve", der[:, D_HBG:D_HBG + 16], pp[:, PP_BG:PP_BG + 16], 0.5, None, ALU.mult, None, [pp_tb], [der_tb])
    act(der[:, D_C1H:D_C1H + 16], pp[:, PP_LAM:PP_LAM + 16], AF.Exp, [pp_tb], [der_tb], scale=-1.0)
    act(der[:, D_C1H:D_C1H + 16], der[:, D_C1H:D_C1H + 16], AF.Ln, [der_tb], [der_tb], bias=1.0)
    ts("dve", der[:, D_C1H:D_C1H + 16], der[:, D_C1H:D_C1H + 16], -4.0, None, ALU.mult, None, [der_tb], [der_tb])

    wbf_tbs = [TB(f"wbf{i}") for i in range(NCBLK)]
    with ExitStack() as es0:
        def sb0(name, shape, dt=F32):
            return es0.enter_context(nc.sbuf_tensor(name, shape, dt))
        wf = [sb0(f"wf{i}", [128, CBLK]) for i in range(2)]
        wb = [sb0(f"wb{i}", [128, CBLK], BF16) for i in range(2)]
        wf_tb = [TB() for _ in range(2)]
        wb_tb = [[TB() for _ in range(3)] for _ in range(2)]
        cuts = [0, 1664, 3264, 4864]
        for i in range(NCBLK):
            u = i % 2
            P.dma("sp", wf[u][:], wall[:, i * CBLK:(i + 1) * CBLK], writes=[wf_tb[u]])
            for j, eng in enumerate(("act", "dve", "pool")):
                a, b = cuts[j], cuts[j + 1]
                cp(eng, wb[u][:, a:b], wf[u][:, a:b], [wf_tb[u]], [wb_tb[u][j]])
            P.dma("pool", wbf[:, i * CBLK:(i + 1) * CBLK], wb[u][:], reads=wb_tb[u], writes=[wbf_tbs[i]])

        cc = sb0("cc", [128, 40])
        sct = sb0("sct", [128, 40])
        scT = sb0("scT", [128, 40])
        cc_tb, sct_tb, scT_tb = TB(), TB(), TB()
        P.dma("sp", cc[:], ccT[:, :], writes=[cc_tb])
        act(sct[:], cc[:], AF.Tanh, [cc_tb], [sct_tb], scale=0.5)
        stt("dve", sct[:], sct[:], 1.0, cc[:], ALU.add, ALU.mult, [sct_tb, cc_tb], [sct_tb])
        ts("dve", scT[:], sct[:], 0.5, None, ALU.mult, None, [sct_tb], [scT_tb])
        scT3 = scT[:].rearrange("p (k r) -> p k r", r=5)
        scB = sb0("scB", [128, 8 * NB * 128])
        scB_tb = TB()
        for kc in range(8):
            for j in range(NB):
                o = (kc * NB + j) * 128
                cp("pool", scB[:, o:o + 128], scT3[:, kc, j:j + 1].to_broadcast([128, 128]), [scT_tb], [scB_tb])
        wm = [sb0(f"wm{i}", [128, 8 * 1024]) for i in range(2)]
        wm_tb = [TB() for _ in range(2)]
        rbg = sb0("rbg", [128, 3 * 1024])
        rbg_tb = TB()
        P.dma("sp", rbg[:], rbd[RB_B2:RB_B2 + 3072].partition_broadcast(128), writes=[rbg_tb])
        b2f = sb0("b2f", [1, 1024])
        b2f_tb = TB()
        P.dma("sp", b2f[:], rbd[RB_B2:RB_B2 + 1024].partition_broadcast(1), writes=[b2f_tb])
        cp("dve", b2row[:], b2f[:], [b2f_tb], [b2row_tb])
        gst = sb0("gst", [128, 2 * 1024])
        gst_tb = [TB(), TB()]
        wmod3 = wmod.rearrange("p (k n) -> p k n", k=8)
        modF4 = modF[:].rearrange("p (m q r) -> p m q r", m=4, q=8)
        fm_index = {0: 0, 1: 1, 3: 2, 4: 3}
        for bi, mi in enumerate(range(6)):
            u = bi % 2
            w3 = wm[u][:].rearrange("p (k n) -> p k n", k=8)
            P.dma("sp", w3, wmod3[:, :, mi * 1024:(mi + 1) * 1024], writes=[wm_tb[u]])
            if mi in fm_index:
                m = fm_index[mi]
                for dq in range(8):
                    pt, ptb = psum()
                    for kc in range(8):
                        mm(pt[:, 0:5], w3[:, kc, dq * 128:(dq + 1) * 128], scT3[:, kc, :], kc == 0, kc == 7, [wm_tb[u], scT_tb], [ptb])
                    col = PP_BM + mi * 8 + dq
                    ts("dve", modF4[:, m, dq, :], pt[:, 0:5], pp[:, col:col + 1], 1.0 if mi in (1, 4) else 0.0,
                       ALU.add, ALU.add, [ptb, pp_tb], [modF_tb])
            else:
                gi = 0 if mi == 2 else 1
                for j in range(NB):
                    for half in range(2):
                        pt, ptb = psum()
                        for kc in range(8):
                            o = (kc * NB + j) * 128
                            mm(pt[:, 0:512], scB[:, o:o + 128], w3[:, kc, half * 512:(half + 1) * 512], kc == 0, kc == 7,
                               [wm_tb[u], scB_tb], [ptb])
                        gs = gst[:, gi * 1024 + half * 512: gi * 1024 + (half + 1) * 512]
                        bo = (1 + gi) * 1024 + half * 512
                        tt("dve", gs, pt[:, 0:512], rbg[:, bo:bo + 512], ALU.add, [ptb, rbg_tb], [gst_tb[gi]])
                    if gi == 0:
                        ts("dve", gst[:, 0:1024], gst[:, 0:1024], 0.5, None, ALU.mult, None, [gst_tb[0]], [gst_tb[0]])
                    r0 = (j * 2 + gi) * 128
                    P.dma("pool", gsc[r0:r0 + 128, :], gst[:, gi * 1024:(gi + 1) * 1024], reads=[gst_tb[gi]], writes=[TB()])
        if debug:
            P.dma("sp", dbg["modF"][:, :], modF[:], reads=[modF_tb], writes=[TB()], is_output=True)
        P.barrier()
        P.finalize()
    if debug:
        P.dma("sp", dbg["gsc"][:, :], gsc[:, :], writes=[TB()], is_output=True)
    if debug == "p0":
        P.finalize()
        return nc

    G = sb("G", [128, 2 * 1024])
    G_tb = TB("G")
    hTf = sb("hTf", [128, 2048])
    hTb = sb("hTb", [128, 2048])
    hTf_tb = [TB(f"hTf{g}") for g in range(8)]
    hTb_tb = [TB(f"hTb{g}") for g in range(8)]
    lruc = sb("lruc", [128, 16])
    lruc_tb = [TB(f"lruc{i}") for i in range(16)]
    lrub = sb("lrub", [128, NTILE * 8])
    lrub_tb = TB("lrub")
    junk = sb("junk", [128, 256])
    junk_tb = TB("junk")
    xs = sb("xs", [128, 2 * 1024])
    xs_tb = [TB("xs0"), TB("xs1")]
    hT = sb("hT", [128, 8 * TT], BF16)
    hT_tb = [TB("hT0"), TB("hT1")]
    h1T = sb("h1T", [128, 8 * TT], BF16)
    h1T_tb = [TB("h1T0"), TB("h1T1")]
    NRING = 4
    ring = [sb(f"ring{i}", [128, 4096], BF16) for i in range(NRING)]
    ring_tb = [TB(f"ring{i}") for i in range(NRING)]
    ringi = {"i": 0}
    st6 = sb("st6", [128, 2 * 12])
    mv = sb("mv", [128, 2 * 4])
    st_tb = [TB(), TB()]
    _xn = sb("xn0", [128, 1024], BF16)
    xn = [_xn, _xn]
    _xntb = TB()
    xn_tb = [_xntb, _xntb]
    dtv = sb("dtv", [128, 2 * 64])
    dta = sb("dta", [128, 2 * 64])
    dtt = sb("dtt", [128, 2 * 64])
    dtA = sb("dtA", [128, 2 * 64])
    dtw = sb("dtw", [128, 2 * 64])
    eall = sb("eall", [128, 2 * 192])
    dt_tb = [TB("dt0"), TB("dt1")]
    usb = [sb(f"usb{i}", [128, TT]) for i in range(4)]
    cacc = [sb(f"cacc{i}", [128, TT]) for i in range(4)]
    cth = [sb(f"cth{i}", [128, TT]) for i in range(4)]
    usb_tb = [TB() for _ in range(4)]
    cacc_tb = [TB() for _ in range(4)]
    cth_tb = [TB() for _ in range(4)]
    xbc = [[sb(f"xbc{r}_{q}", [128, TT], BF16) for q in range(4)] for r in range(2)]
    xbc_tb = [[TB() for q in range(4)] for r in range(2)]
    xtok = sb("xtok", [128, 2 * 256], BF16)
    btok = sb("btok", [128, 2 * 128], BF16)
    xtok_tb = [TB(), TB()]
    btok_tb = [TB(), TB()]
    NV = 2
    ver = [[sb(f"ver{r}_{v}", [128, 256], BF16) for v in range(5)] for r in range(NV)]
    ver_tb = [[TB() for v in range(5)] for r in range(NV)]
    hsb = sb("hsb", [128, 2 * 256], BF16)
    hsb_tb = [TB(), TB()]
    hsf = [sb(f"hsf{i}", [128, 256], BF16) for i in range(2)]
    hsf_tb = [TB(), TB()]
    Rt = [[sb(f"R{r}_{d}", [128, 512], BF16) for d in range(2)] for r in range(2)]
    Rt_tb = [[TB() for d in range(2)] for r in range(2)]
    Lt = [[sb(f"L{r}_{d}", [128, 512], BF16) for d in range(2)] for r in range(2)]
    Lt_tb = [[TB() for d in range(2)] for r in range(2)]
    Mt = [[sb(f"M{r}_{d}", [128, 512], BF16) for d in range(2)] for r in range(2)]
    Mt_tb = [[TB() for d in range(2)] for r in range(2)]
    CBm = [[sb(f"CB{r}_{d}", [128, 128], BF16) for d in range(2)] for r in range(2)]
    CBm_tb = [[TB() for d in range(2)] for r in range(2)]
    t12 = [[sb(f"t12_{r}_{d}", [128, 256]) for d in range(2)] for r in range(2)]
    t12_tb = [[TB() for d in range(2)] for r in range(2)]
    yy = [sb(f"yy{r}", [128, 256]) for r in range(2)]
    yy_tb = [TB(), TB()]
    tz = [sb(f"tz{r}", [128, 256]) for r in range(2)]
    tz_tb = [TB(), TB()]
    szt = [sb(f"sz{r}", [128, 256]) for r in range(2)]
    sz_tb = [TB(), TB()]
    ut = [sb(f"ut{r}", [128, 256]) for r in range(2)]
    ut_tb = [TB(), TB()]
    ssq = sb("ssq", [128, 4])
    ssq_tb = [TB(), TB()]
    unb = [sb(f"unb{r}", [128, 256], BF16) for r in range(2)]
    unb_tb = [TB(), TB()]
    unT = sb("unT", [128, 16 * TT], BF16)
    unT_tb = [[TB() for c in range(2)] for k in range(16)]
    ubf = [sb(f"ubf_{r}", [128, TT], BF16) for r in range(2)]
    ubf_tb = [TB(), TB()]
    ltmp = [sb(f"ltmp{i}", [128, TT]) for i in range(8)]
    ltmp_tb = [TB() for _ in range(8)]
    hfw = sb("hfw", [128, TT])
    hfw_tb = TB()
    ysum = sb("ysum", [128, TT])
    ysum_tb = TB()
    gtmp = [sb(f"gtmp{i}", [128, TT]) for i in range(4)]
    gtmp_tb = [TB() for _ in range(4)]
    ylg = sb("ylg", [128, 8 * TT], BF16)
    ylg_tb = [TB() for _ in range(8)]
    merged = sb("merged", [128, 8 * TT], BF16)
    merged_tb = [TB() for _ in range(8)]
    tg = [gtmp[0], gtmp[1]]
    tg_tb = [gtmp_tb[0], gtmp_tb[1]]
    m12 = [gtmp[2], gtmp[3]]
    m12_tb = [gtmp_tb[2], gtmp_tb[3]]
    _rt = sb("rtmp0", [128, 512])
    _rtb = TB()
    rtmp = [_rt, _rt]
    rtmp_tb = [_rtb, _rtb]
    hid = [sb(f"hid{i}", [128, 8 * TT], BF16) for i in range(2)]
    hid_tb = [[TB() for f in range(8)] for i in range(2)]
    rr = [sb(f"rr{i}", [128, TT], BF16) for i in range(2)]
    rr_tb = [TB(), TB()]

    print("SBUF bytes remaining:", nc.sbuf_bytes_remaining)
    wbf_in = wbf[:, W_IN:W_IN + 8 * NIN].rearrange("p (k n) -> p k n", k=8)
    wbf_bs = wbf[:, W_BS:W_BS + 16 * 1024].rearrange("p (k n) -> p k n", k=16)
    wbf_bl = wbf[:, W_BL:W_BL + 8 * 1024].rearrange("p (k n) -> p k n", k=8)
    wbf_o = wbf[:, W_O:W_O + 8 * 1024].rearrange("p (k n) -> p k n", k=8)
    wbf_m1 = wbf[:, W_M1:W_M1 + 8 * 4096].rearrange("p (k n) -> p k n", k=8)
    wbf_m2 = wbf[:, W_M2:W_M2 + 32 * 1024].rearrange("p (k n) -> p k n", k=32)

    def wload(dram3, k0, k1, c0, c1):
        i = ringi["i"] % NRING
        ringi["i"] += 1
        nk, ncol = k1 - k0, c1 - c0
        assert nk * ncol <= 4096
        v = ring[i][:, 0:nk * ncol].rearrange("p (k n) -> p k n", k=nk)
        P.dma("sp", v, dram3[:, k0:k1, c0:c1], reads=wbf_tbs, writes=[ring_tb[i]])
        return v, ring_tb[i]

    def wload_multi(specs):
        i = ringi["i"] % NRING
        ringi["i"] += 1
        off = 0
        views = []
        for (dram3, k0, k1, c0, c1) in specs:
            nk, ncol = k1 - k0, c1 - c0
            v = ring[i][:, off:off + nk * ncol].rearrange("p (k n) -> p k n", k=nk)
            off += nk * ncol
            assert off <= 4096
            P.dma("sp", v, dram3[:, k0:k1, c0:c1], reads=wbf_tbs, writes=[ring_tb[i]])
            views.append(v)
        return views, ring_tb[i]

    P.dma("sp", lruw[:], wbf[:, W_LR:W_LR + 4096], reads=wbf_tbs, writes=[lruw_tb])
    lruw3 = lruw[:].rearrange("p (i d) -> p i d", d=128)
    modF4 = modF[:].rearrange("p (m q r) -> p m q r", m=4, q=8)
    hT3 = hT[:].rearrange("p (k t) -> p k t", k=8)
    h1T3 = h1T[:].rearrange("p (k t) -> p k t", k=8)
    xs3 = xs[:].rearrange("p (c d) -> p c d", c=2)
    unT3 = unT[:].rearrange("p (k t) -> p k t", k=16)
    ylg3 = ylg[:].rearrange("p (k t) -> p k t", k=8)
    merged3 = merged[:].rearrange("p (k t) -> p k t", k=8)
    xtok3 = xtok[:].rearrange("p (c n) -> p c n", c=2)
    btok3 = btok[:].rearrange("p (c n) -> p c n", c=2)
    hsb3 = hsb[:].rearrange("p (c n) -> p c n", c=2)
    EPS = 1e-6

    def ln_stats(c, src_ap, reads):
        s3 = st6[:, 12 * c:12 * c + 12]
        P.add("dve", lambda e: e.bn_stats(out=s3[:, 0:6], in_=src_ap[:, 0:512]), reads, [st_tb[c]])
        P.add("dve", lambda e: e.bn_stats(out=s3[:, 6:12], in_=src_ap[:, 512:1024]), reads, [st_tb[c]])
        P.add("dve", lambda e: e.bn_aggr(out=mv[:, 4 * c:4 * c + 2], in_=s3), [st_tb[c]], [st_tb[c]])
        ts("dve", mv[:, 4 * c + 3:4 * c + 4], mv[:, 4 * c + 1:4 * c + 2], EPS, None, ALU.add, None, [st_tb[c]], [st_tb[c]])
        act(mv[:, 4 * c + 3:4 * c + 4], mv[:, 4 * c + 3:4 * c + 4], AF.Ln, [st_tb[c]], [st_tb[c]])
        act(mv[:, 4 * c + 2:4 * c + 3], mv[:, 4 * c + 3:4 * c + 4], AF.Exp, [st_tb[c]], [st_tb[c]], scale=-0.5)

    def to_hT(c, src_ap, src_reads, dst3, dst_tb, m_shift, m_scale, r):
        ln_stats(c, src_ap, src_reads)
        ts("dve", xn[c][:], src_ap, mv[:, 4 * c:4 * c + 1], mv[:, 4 * c + 2:4 * c + 3], ALU.subtract, ALU.mult,
           src_reads + [st_tb[c]], [xn_tb[c]])
        pt, ptb = psum_bf()
        for kc in range(8):
            trp(pt[:, kc * 128:(kc + 1) * 128], xn[c][:, kc * 128:(kc + 1) * 128], IDNb, [xn_tb[c], cstb_tb], [ptb])
        for kc in range(8):
            eng = "act" if kc % 2 == 0 else "dve"
            affine(eng, dst3[:, kc, c * 128:(c + 1) * 128], pt[:, kc * 128:(kc + 1) * 128],
                   modF4[:, m_scale, kc, r:r + 1], modF4[:, m_shift, kc, r:r + 1], [ptb, modF_tb], [dst_tb[c]])

    def conv(eng, i, u_ap, wcol, bcol, L, reads):
        a3 = cacc[i][:].rearrange("p (r l) -> p r l", l=L)
        u3 = u_ap.rearrange("p (r l) -> p r l", l=L)
        t3 = cth[i][:].rearrange("p (r l) -> p r l", l=L)
        ts(eng, cacc[i][:], u_ap, wcol(2), bcol, ALU.mult, ALU.add, reads, [cacc_tb[i]])
        for (k, dst, srcs) in ((1, slice(1, L), slice(0, L - 1)), (0, slice(2, L), slice(0, L - 2)), (3, slice(0, L - 1), slice(1, L))):
            if eng == "dve":
                stt(eng, a3[:, :, dst], u3[:, :, srcs], wcol(k), a3[:, :, dst], ALU.mult, ALU.add, reads + [cacc_tb[i]], [cacc_tb[i]])
            else:
                ts(eng, t3[:, :, dst], u3[:, :, srcs], wcol(k), None, ALU.mult, None, reads, [cth_tb[i]])
                tt(eng, a3[:, :, dst], a3[:, :, dst], t3[:, :, dst], ALU.add, [cacc_tb[i], cth_tb[i]], [cacc_tb[i]])

    cnt = {"g": 0, "v": 0, "c": 0, "k": 0}

    def tile(b, kind, ti):
        full = kind == "full"
        isctx = kind == "ctx"
        rot["n"] = 8

        def chk(stage):
            if debug == f"{kind}{ti}_{stage}":
                raise _Stop()
        L = 256 if isctx else 64
        r = 4 if isctx else b
        src = ctxin[b] if isctx else xin[b, ti * TT:(ti + 1) * TT, :]
        if kind == "bwd":
            P.dma("pool", hbb[ti * 128:(ti + 1) * 128, :], hTb[:], reads=hTb_tb, writes=[hbb_tbs[ti]])
            cp("pool", lrub[:, ti * 8:(ti + 1) * 8], lruc[:, 8:16], lruc_tb[8:16], [lrub_tb])
        if full:
            P.dma("pool", hTb[:], hbb[ti * 128:(ti + 1) * 128, :], reads=[hbb_tbs[ti]], writes=hTb_tb)
        for c in range(2):
            P.dma("pool", xs3[:, c, :], src[c * 128:(c + 1) * 128, :], writes=[xs_tb[c]])
            to_hT(c, xs3[:, c, :], [xs_tb[c]], hT3, hT_tb, 0, 1, r)
        if debug == "s1":
            raise _Stop()
        wdt, wdt_tb = wload(wbf_in, 0, 8, O_DT, O_DT + 64)
        for c in range(2):
            pt, ptb = psum()
            for kc in range(8):
                mm(pt[:, 0:64], hT3[:, kc, c * 128:(c + 1) * 128], wdt[:, kc, :], kc == 0, kc == 7, [hT_tb[c], wdt_tb], [ptb])
            s = slice(c * 64, (c + 1) * 64)
            tt("dve", dtv[:, s], pt[:, 0:64], misc[:, 0:64], ALU.add, [ptb, misc_tb], [dt_tb[c]])
            stt("dve", dta[:, s], dtv[:, s], -1.0, dtv[:, s], ALU.mult, ALU.max, [dt_tb[c]], [dt_tb[c]])
            act(dta[:, s], dta[:, s], AF.Exp, [dt_tb[c]], [dt_tb[c]], scale=-1.0)
            act(dta[:, s], dta[:, s], AF.Ln, [dt_tb[c]], [dt_tb[c]], bias=1.0)
            stt("dve", dtt[:, s], dtv[:, s], 0.0, dta[:, s], ALU.max, ALU.add, [dt_tb[c]], [dt_tb[c]])
            tt("dve", dtA[:, s], dtt[:, s], misc[:, 64:128], ALU.mult, [dt_tb[c], misc_tb], [dt_tb[c]])
            p2, p2b = psum()
            o = c * 64
            mm(p2[:, 0:32], LE, dtA[:, o:o + 32], True, True, [dt_tb[c], cst_tb], [p2b])
            mm(p2[:, 32:64], GE, dtA[:, o + 32:o + 64], True, True, [dt_tb[c], cst_tb], [p2b])
            mm(p2[:, 64:96], GT, dtA[:, o:o + 32], True, True, [dt_tb[c], cst_tb], [p2b])
            mm(p2[:, 96:128], LT, dtA[:, o + 32:o + 64], True, True, [dt_tb[c], cst_tb], [p2b])
            mm(p2[:, 128:192], ONES, dtA[:, o:o + 64], True, True, [dt_tb[c], cst_tb], [p2b])
            act(eall[:, c * 192:(c + 1) * 192], p2[:, 0:192], AF.Exp, [p2b], [dt_tb[c]])
            tt("dve", dtw[:, s], dtt[:, s], eall[:, c * 192 + 64:c * 192 + 128], ALU.mult, [dt_tb[c]], [dt_tb[c]])

        if debug == "dt":
            raise _Stop()

        def bc4(ap2, col0):
            return ap2[:, col0:col0 + 4].unsqueeze(2).to_broadcast([128, 4, 64])

        for g in range(8):
            rb_ = cnt["g"] % 2
            cnt["g"] += 1
            nq = 4 if full else 3
            wx, wx_tb = wload(wbf_in, 0, 8, O_XBC + g * 512, O_XBC + g * 512 + nq * 128)
            if full:
                if g % 2 == 0:
                    wz, wz_tb = wload(wbf_in, 0, 8, O_Z + g * 256, O_Z + g * 256 + 512)
            for q in range(nq):
                pt, ptb = psum()
                for kc in range(8):
                    mm(pt[:, 0:TT], wx[:, kc, q * 128:(q + 1) * 128], hT3[:, kc, :], kc == 0, kc == 7, hT_tb + [wx_tb], [ptb])
                cp("act", usb[q][:], pt[:, 0:TT], [ptb], [usb_tb[q]])
            for q in range(nq):
                ccn = 4 * g + q
                eng = "pool" if q == nq - 1 else "dve"
                conv(eng, q, usb[q][:], lambda k, ccn=ccn: der[:, D_HCW + ccn * 4 + k:D_HCW + ccn * 4 + k + 1],
                     der[:, D_HCB + ccn:D_HCB + ccn + 1], L, [usb_tb[q], der_tb])
            for q in range(nq):
                eng = "pool" if q == nq - 1 else "dve"
                act(cth[q][:], cacc[q][:], AF.Tanh, [cacc_tb[q]], [cth_tb[q]])
                if eng == "dve":
                    stt(eng, xbc[rb_][q][:], cth[q][:], 1.0, cacc[q][:], ALU.add, ALU.mult, [cth_tb[q], cacc_tb[q]], [xbc_tb[rb_][q]])
                else:
                    ts(eng, cth[q][:], cth[q][:], 1.0, None, ALU.add, None, [cth_tb[q]], [cth_tb[q]])
                    tt(eng, xbc[rb_][q][:], cth[q][:], cacc[q][:], ALU.mult, [cth_tb[q], cacc_tb[q]], [xbc_tb[rb_][q]])
            if debug == "ssd_a":
                raise _Stop()
            for c in range(2):
                pt, ptb = psum_bf()
                for q in range(3):
                    trp(pt[:, q * 128:(q + 1) * 128], xbc[rb_][q][:, c * 128:(c + 1) * 128], IDNb, [xbc_tb[rb_][q], cstb_tb], [ptb])
                if debug != "ssd_b1":
                    ts("dve", xtok3[:, c, :], pt[:, 0:256], 1.0, None, ALU.mult, None, [ptb], [xtok_tb[c]])
                if debug not in ("ssd_b1", "ssd_b2"):
                    ts("dve", btok3[:, c, :], pt[:, 256:384], 1.0, None, ALU.mult, None, [ptb], [btok_tb[c]])
            if debug in ("ssd_b", "ssd_b1", "ssd_b2"):
                raise _Stop()
            hgf = hTf[:, g * 256:(g + 1) * 256]
            hgb = hTb[:, g * 256:(g + 1) * 256]
            hgf3 = hgf.rearrange("p (h q) -> p h q", q=64)
            hgb3 = hgb.rearrange("p (h q) -> p h q", q=64)

            def versions(c, which):
                rv = cnt["v"] % NV
                cnt["v"] += 1
                x3 = xtok3[:, c, :].rearrange("p (h q) -> p h q", q=64)
                srcs = {0: (dtt, c * 64 + 4 * g), 1: (dtt, c * 64 + 32 + 4 * g), 2: (dtw, c * 64 + 4 * g),
                        3: (dtw, c * 64 + 32 + 4 * g), 4: (misc, 128 + 4 * g)}
                for v in which:
                    t_, col = srcs[v]
                    eng = "dve"
                    tt(eng, ver[rv][v][:].rearrange("p (h q) -> p h q", q=64), x3, bc4(t_, col), ALU.mult,
                       [xtok_tb[c], dt_tb[c], misc_tb], [ver_tb[rv][v]])
                return rv

            def state_update(c, d, rv):
                pt, ptb = psum()
                mm(pt[:, 0:256], btok3[:, c, :], ver[rv][2 + d][:], True, True, [btok_tb[c], ver_tb[rv][2 + d]], [ptb])
                h3 = hgf3 if d == 0 else hgb3
                h2 = hgf if d == 0 else hgb
                htb = hTf_tb[g] if d == 0 else hTb_tb[g]
                dec = bc4(eall, c * 192 + 128 + d * 32 + 4 * g)
                tt("dve", h3, h3, dec, ALU.mult, [htb, dt_tb[c]], [htb])
                tt("dve", h2, h2, pt[:, 0:256], ALU.add, [htb, ptb], [htb])

            if not full:
                if debug == "ssd_c":
                    versions(0, [2])
                    raise _Stop()
                if isctx:
                    for c in range(2):
                        rv = versions(c, [2])
                        state_update(c, 0, rv)
                for c in (1, 0):
                    rv = versions(c, [3])
                    state_update(c, 1, rv)
                continue

            for c in (1, 0):
                cp("act", hsb3[:, c, :], hgb, [hTb_tb[g]], [hsb_tb[c]])
                if c == 1:
                    rvb = versions(c, [3])
                    state_update(c, 1, rvb)
            chk("fa")
            for c in range(2):
                rc = cnt["c"] % 2
                cnt["c"] += 1
                rv = versions(c, [0, 1, 2, 4])
                cp("act", hsf[rc][:], hgf, [hTf_tb[g]], [hsf_tb[rc]])
                pz, pzb = psum()
                zc0 = (g % 2) * 256
                for kc in range(8):
                    mm(pz[:, 0:256], hT3[:, kc, c * 128:(c + 1) * 128], wz[:, kc, zc0:zc0 + 256], kc == 0, kc == 7,
                       [hT_tb[c], wz_tb], [pzb])
                pc, pcb = psum()
                mm(pc[:, 0:128], xbc[rb_][2][:, c * 128:(c + 1) * 128], xbc[rb_][3][:, c * 128:(c + 1) * 128], True, True,
                   [xbc_tb[rb_][2], xbc_tb[rb_][3]], [pcb])
                pi_, pib = psum()
                ct = xbc[rb_][3][:, c * 128:(c + 1) * 128]
                mm(pi_[:, 0:256], ct, hsf[rc][:], True, True, [xbc_tb[rb_][3], hsf_tb[rc]], [pib])
                mm(pi_[:, 256:512], ct, hsb3[:, c, :], True, True, [xbc_tb[rb_][3], hsb_tb[c]], [pib])
                segs = []
                for d in range(2):
                    msk = LEb if d == 0 else GEb
                    lh = GTb if d == 0 else LTb
                    col = c * 64 + d * 32 + 4 * g
                    tt("dve", Rt[rc][d][:].rearrange("p (h i) -> p h i", h=4),
                       dtA[:, col:col + 4].unsqueeze(2).to_broadcast([128, 4, 128]),
                       msk.unsqueeze(1).to_broadcast([128, 4, 128]), ALU.mult, [dt_tb[c], cstb_tb], [Rt_tb[rc][d]])
                    pt, ptb = psum()
                    mm(pt[:, 0:512], lh, Rt[rc][d][:], True, True, [Rt_tb[rc][d], cstb_tb], [ptb])
                    segs.append((pt, ptb))
                act(tz[rc][:], pz[:, 0:256], AF.Tanh, [pzb], [tz_tb[rc]], scale=0.5)
                for d in range(2):
                    act(Lt[rc][d][:], segs[d][0][:, 0:512], AF.Exp, [segs[d][1]], [Lt_tb[rc][d]])
                tt("dve", CBm[rc][0][:], pc[:, 0:128], LEb, ALU.mult, [pcb, cstb_tb], [CBm_tb[rc][0]])
                tt("dve", CBm[rc][1][:], pc[:, 0:128], GEb, ALU.mult, [pcb, cstb_tb], [CBm_tb[rc][1]])
                stt("dve", szt[rc][:], tz[rc][:], 1.0, pz[:, 0:256], ALU.add, ALU.mult, [tz_tb[rc], pzb], [sz_tb[rc]])
                for d in range(2):
                    tt("dve", t12[rc][d][:].rearrange("p (h q) -> p h q", q=64),
                       pi_[:, d * 256:(d + 1) * 256].rearrange("p (h q) -> p h q", q=64),
                       bc4(eall, c * 192 + d * 32 + 4 * g), ALU.mult, [pib, dt_tb[c]], [t12_tb[rc][d]])
                tt("dve", t12[rc][0][:], t12[rc][0][:], t12[rc][1][:], ALU.add, [t12_tb[rc][0], t12_tb[rc][1]], [t12_tb[rc][0]])
                for d in range(2):
                    tt("dve", Mt[rc][d][:].rearrange("p (h i) -> p h i", h=4), Lt[rc][d][:].rearrange("p (h i) -> p h i", h=4),
                       CBm[rc][d][:].unsqueeze(1).to_broadcast([128, 4, 128]), ALU.mult,
                       [Lt_tb[rc][d], CBm_tb[rc][d]], [Mt_tb[rc][d]])
                py, pyb = psum()
                for h in range(4):
                    hs = slice(h * 64, (h + 1) * 64)
                    ms = slice(h * 128, (h + 1) * 128)
                    mm(py[:, hs], Mt[rc][0][:, ms], ver[rv][0][:, hs], True, False, [Mt_tb[rc][0], ver_tb[rv][0]], [pyb])
                    mm(py[:, hs], Mt[rc][1][:, ms], ver[rv][1][:, hs], False, False, [Mt_tb[rc][1], ver_tb[rv][1]], [pyb])
                    mm(py[:, hs], IDNb, ver[rv][4][:, hs], False, True, [cstb_tb, ver_tb[rv][4]], [pyb])
                state_update(c, 0, rv)
                tt("dve", yy[rc][:], py[:, 0:256], t12[rc][0][:], ALU.add, [pyb, t12_tb[rc][0]], [yy_tb[rc]])
                tt("dve", ut[rc][:], yy[rc][:], szt[rc][:], ALU.mult, [yy_tb[rc], sz_tb[rc]], [ut_tb[rc]])
                act(junk[:, 0:256], ut[rc][:], AF.Square, [ut_tb[rc]], [junk_tb, ssq_tb[rc]], accum=ssq[:, 2 * rc:2 * rc + 1])
                ts("dve", ssq[:, 2 * rc + 1:2 * rc + 2], ssq[:, 2 * rc:2 * rc + 1], 1024.0 * 1e-5, None, ALU.add, None,
                   [ssq_tb[rc]], [ssq_tb[rc]])
                act(ssq[:, 2 * rc + 1:2 * rc + 2], ssq[:, 2 * rc + 1:2 * rc + 2], AF.Ln, [ssq_tb[rc]], [ssq_tb[rc]])
                act(ssq[:, 2 * rc + 1:2 * rc + 2], ssq[:, 2 * rc + 1:2 * rc + 2], AF.Exp, [ssq_tb[rc]], [ssq_tb[rc]], scale=-0.5)
                ts("dve", unb[rc][:], ut[rc][:], ssq[:, 2 * rc + 1:2 * rc + 2], 16.0, ALU.mult, ALU.mult,
                   [ut_tb[rc], ssq_tb[rc]], [unb_tb[rc]])
                pt, ptb = psum_bf()
                for m in range(2):
                    trp(pt[:, m * 128:(m + 1) * 128], unb[rc][:, m * 128:(m + 1) * 128], IDNb, [unb_tb[rc], cstb_tb], [ptb])
                for m in range(2):
                    k = 2 * g + m
                    ts("dve", unT3[:, k, c * 128:(c + 1) * 128], pt[:, m * 128:(m + 1) * 128],
                       pp[:, PP_NW + k:PP_NW + k + 1], None, ALU.mult, None, [ptb, pp_tb], [unT_tb[k][c]])
                chk("fj")
                if g == 0 and c == 1:
                    chk("fk")
                if g == 1 and c == 1:
                    chk("fl")
                if g == 3 and c == 1:
                    chk("fm")

        if debug == "ssd":
            raise _Stop()
        chk("ssd")
        dirs = [1] if kind == "bwd" else [0, 1]
        for k in range(8):
            ru = cnt["k"] % 2
            cnt["k"] += 1
            if k % 4 == 0:
                wl, wl_tb = wload(wbf_in, 0, 8, O_LRU + k * 128, O_LRU + k * 128 + 512)
                if full:
                    wg, wg_tb = wload(wbf_in, 0, 8, O_LG + k * 128, O_LG + k * 128 + 512)
            kk = (k % 4) * 128
            pt, ptb = psum()
            for kc in range(8):
                mm(pt[:, 0:TT], wl[:, kc, kk:kk + 128], hT3[:, kc, :], kc == 0, kc == 7, hT_tb + [wl_tb], [ptb])
            cp("act", usb[ru][:], pt[:, 0:TT], [ptb], [usb_tb[ru]])
            conv("dve", ru, usb[ru][:], lambda t_, k=k: pp[:, PP_LCW + k * 4 + t_:PP_LCW + k * 4 + t_ + 1],
                 pp[:, PP_LCB + k:PP_LCB + k + 1], L, [usb_tb[ru], pp_tb])
            uu = cacc[ru]
            uu_tb = cacc_tb[ru]
            cp("act", ubf[ru][:], uu[:], [uu_tb], [ubf_tb[ru]])
            for d in dirs:
                TR, A_, TI, NA2, S_, IU, B_, H_ = range(8)
                pr, prb = psum()
                mm(pr[:, 0:TT], lruw3[:, (d * 2 + 0) * 8 + k, :], ubf[ru][:], True, True, [lruw_tb, ubf_tb[ru]], [prb])
                mm(pr[:, 256:256 + TT], lruw3[:, (d * 2 + 1) * 8 + k, :], ubf[ru][:], True, True, [lruw_tb, ubf_tb[ru]], [prb])
                idx = d * 8 + k
                act(ltmp[TR][:], pr[:, 0:TT], AF.Tanh, [prb, der_tb], [ltmp_tb[TR]], scale=0.5, bias=der[:, D_HBA + idx:D_HBA + idx + 1])
                aout = ltmp[A_][:] if d == 0 else ltmp[A_][:, ::-1]
                act(aout, ltmp[TR][:], AF.Exp, [ltmp_tb[TR], der_tb], [ltmp_tb[A_]],
                    scale=der[:, D_C1H + idx:D_C1H + idx + 1], bias=der[:, D_C1H + idx:D_C1H + idx + 1])
                act(ltmp[TI][:], pr[:, 256:256 + TT], AF.Tanh, [prb, der_tb], [ltmp_tb[TI]], scale=0.5,
                    bias=der[:, D_HBI + idx:D_HBI + idx + 1])
                stt("dve", ltmp[NA2][:], ltmp[A_][:], -1.0, ltmp[A_][:], ALU.mult, ALU.mult, [ltmp_tb[A_]], [ltmp_tb[NA2]])
                ts("dve", ltmp[S_][:], ltmp[NA2][:], 1.0, 1e-30, ALU.add, ALU.max, [ltmp_tb[NA2]], [ltmp_tb[S_]])
                act(ltmp[S_][:], ltmp[S_][:], AF.Ln, [ltmp_tb[S_]], [ltmp_tb[S_]])
                act(ltmp[S_][:], ltmp[S_][:], AF.Exp, [ltmp_tb[S_]], [ltmp_tb[S_]], scale=0.5)
                stt_p(ltmp[IU][:], ltmp[TI][:], 1.0, uu[:], ALU.add, ALU.mult, [ltmp_tb[TI], uu_tb], [ltmp_tb[IU]])
                iu_in = ltmp[IU][:] if d == 0 else ltmp[IU][:, ::-1]
                stt("dve", ltmp[B_][:], ltmp[S_][:], 0.5, iu_in, ALU.mult, ALU.mult, [ltmp_tb[S_], ltmp_tb[IU]], [ltmp_tb[B_]])
                if d == 0 or not full:
                    init = lruc[:, idx:idx + 1]
                    init_tb = lruc_tb[idx]
                else:
                    init = lrub[:, ti * 8 + k:ti * 8 + k + 1]
                    init_tb = lrub_tb
                hout = hfw if (d == 0 and full) else ltmp[H_]
                hout_tb = hfw_tb if (d == 0 and full) else ltmp_tb[H_]
                P.add("dve", lambda e, hout=hout, init=init, A_=A_, B_=B_: e.tensor_tensor_scan(
                    out=hout[:], data0=ltmp[A_][:], data1=ltmp[B_][:], initial=init, op0=ALU.mult, op1=ALU.add),
                    [ltmp_tb[A_], ltmp_tb[B_], init_tb], [hout_tb])
                if d == 0 or not full:
                    cp("pool", lruc[:, idx:idx + 1], hout[:, TT - 1:TT], [hout_tb], [lruc_tb[idx]])
                if full and d == 1:
                    tt("dve", ysum[:], hfw[:], ltmp[H_][:, ::-1], ALU.add, [hfw_tb, ltmp_tb[H_]], [ysum_tb])
            if full:
                pg, pgb = psum()
                for kc in range(8):
                    mm(pg[:, 0:TT], wg[:, kc, kk:kk + 128], hT3[:, kc, :], kc == 0, kc == 7, hT_tb + [wg_tb], [pgb])
                X, X2, IN, TG = range(4)
                cp("act", gtmp[X][:], pg[:, 0:TT], [pgb], [gtmp_tb[X]])
                act(gtmp[X2][:], pg[:, 0:TT], AF.Square, [pgb], [gtmp_tb[X2]])
                ts("dve", gtmp[X2][:], gtmp[X2][:], 0.044715, 1.0, ALU.mult, ALU.add, [gtmp_tb[X2]], [gtmp_tb[X2]])
                tt("dve", gtmp[IN][:], gtmp[X2][:], gtmp[X][:], ALU.mult, [gtmp_tb[X2], gtmp_tb[X]], [gtmp_tb[IN]])
                act(gtmp[TG][:], gtmp[IN][:], AF.Tanh, [gtmp_tb[IN]], [gtmp_tb[TG]], scale=0.7978845608028654)
                stt_p(gtmp[IN][:], gtmp[TG][:], 1.0, gtmp[X][:], ALU.add, ALU.mult, [gtmp_tb[TG], gtmp_tb[X]], [gtmp_tb[IN]])
                stt("dve", ylg3[:, k, :], ysum[:], 0.5, gtmp[IN][:], ALU.mult, ALU.mult, [ysum_tb, gtmp_tb[IN]], [ylg_tb[k]])
        chk("lru")
        if not full:
            return

        all_unT = [unT_tb[k][c] for k in range(16) for c in range(2)]
        for dq in range(8):
            if dq % 2 == 0:
                wbs, wbs_tb = wload(wbf_bs, 0, 16, dq * 128, dq * 128 + 256)
                (wgs, wgl), wgs_tb = wload_multi([(wbf_in, 0, 8, O_GS + dq * 128, O_GS + dq * 128 + 256),
                                                  (wbf_in, 0, 8, O_GL + dq * 128, O_GL + dq * 128 + 256)])
                wgl_tb = wgs_tb
                wbl, wbl_tb = wload(wbf_bl, 0, 8, dq * 128, dq * 128 + 256)
            pb, pbb = psum()
            o2 = (dq % 2) * 128
            o4 = (dq % 4) * 128
            for kc in range(16):
                mm(pb[:, 0:TT], wbs[:, kc, o2:o2 + 128], unT3[:, kc, :], kc == 0, kc == 15, all_unT + [wbs_tb], [pbb])
            for kc in range(8):
                mm(pb[:, 256:256 + TT], wbl[:, kc, o2:o2 + 128], ylg3[:, kc, :], kc == 0, kc == 7, ylg_tb + [wbl_tb], [pbb])
            pgt, pgtb = psum()
            for kc in range(8):
                mm(pgt[:, 0:TT], wgs[:, kc, o2:o2 + 128], hT3[:, kc, :], kc == 0, kc == 7, hT_tb + [wgs_tb], [pgtb])
            for kc in range(8):
                mm(pgt[:, 256:256 + TT], wgl[:, kc, o2:o2 + 128], hT3[:, kc, :], kc == 0, kc == 7, hT_tb + [wgl_tb], [pgtb])
            act(tg[0][:], pgt[:, 0:TT], AF.Tanh, [pgtb, der_tb], [tg_tb[0]], scale=0.5, bias=der[:, D_HBG + dq:D_HBG + dq + 1])
            act(tg[1][:], pgt[:, 256:256 + TT], AF.Tanh, [pgtb, der_tb], [tg_tb[1]], scale=0.5,
                bias=der[:, D_HBG + 8 + dq:D_HBG + 8 + dq + 1])
            stt("dve", m12[0][:], tg[0][:], 1.0, pb[:, 0:TT], ALU.add, ALU.mult, [tg_tb[0], pbb], [m12_tb[0]])
            stt("dve", m12[1][:], tg[1][:], 1.0, pb[:, 256:256 + TT], ALU.add, ALU.mult, [tg_tb[1], pbb], [m12_tb[1]])
            tt("dve", merged3[:, dq, :], m12[0][:], m12[1][:], ALU.add, [m12_tb[0], m12_tb[1]], [merged_tb[dq]])

        chk("s8")
        rot["n"] = 4
        wo = [wload(wbf_o, 0, 8, half * 512, (half + 1) * 512) for half in range(2)]
        for c in range(2):
            for half in range(2):
                pm, pmb = psum_t[4 + half], psum_tb[4 + half]
                for kc in range(8):
                    mm(pm[:, 0:512], merged3[:, kc, c * 128:(c + 1) * 128], wo[half][0][:, kc, :], kc == 0, kc == 7,
                       merged_tb + [wo[half][1]], [pmb])
                xsl = xs3[:, c, half * 512:(half + 1) * 512]
                tt("dve", rtmp[half][:], pm[:, 0:512], G[:, half * 512:(half + 1) * 512], ALU.mult, [pmb, G_tb], [rtmp_tb[half]])
                stt_p(xsl, xsl, ALPHA, rtmp[half][:], ALU.mult, ALU.add, [xs_tb[c], rtmp_tb[half]], [xs_tb[c]])
            x2 = xs3[:, c, :]
            ln_stats(c, x2, [xs_tb[c]])
            ts("dve", x2, x2, mv[:, 4 * c:4 * c + 1], mv[:, 4 * c + 2:4 * c + 3], ALU.subtract, ALU.mult, [xs_tb[c], st_tb[c]], [xs_tb[c]])
            tt("dve", x2, x2, lnt[:, 0:1024], ALU.mult, [xs_tb[c], lnt_tb], [xs_tb[c]])
            tt("dve", x2, x2, lnt[:, 1024:2048], ALU.add, [xs_tb[c], lnt_tb], [xs_tb[c]])
            to_hT(c, x2, [xs_tb[c]], h1T3, h1T_tb, 2, 3, r)

        chk("s9")
        for q in range(4):
            hq = hid[q % 2]
            hq3 = hq[:].rearrange("p (f t) -> p f t", f=8)
            for fq in range(8):
                f = 8 * q + fq
                if f % 4 == 0:
                    w1, w1_tb = wload(wbf_m1, 0, 8, f * 128, f * 128 + 512)
                o4 = (f % 4) * 128
                ph, phb = psum()
                for kc in range(8):
                    mm(ph[:, 0:TT], w1[:, kc, o4:o4 + 128], h1T3[:, kc, :], kc == 0, kc == 7, h1T_tb + [w1_tb], [phb])
                act(rr[f % 2][:], ph[:, 0:TT], AF.Relu, [phb, pp_tb], [rr_tb[f % 2]], bias=pp[:, PP_B1 + f:PP_B1 + f + 1])
                if f % 2 == 0:
                    act(hq3[:, fq, :], rr[f % 2][:], AF.Square, [rr_tb[f % 2]], [hid_tb[q % 2][fq]])
                else:
                    tt("dve", hq3[:, fq, :], rr[f % 2][:], rr[f % 2][:], ALU.mult, [rr_tb[f % 2]], [hid_tb[q % 2][fq]])
            for half in range(2):
                w2, w2_tb = wload(wbf_m2, 8 * q, 8 * q + 8, half * 512, (half + 1) * 512)
                for c in range(2):
                    pa, pab = psum_t[4 + 2 * c + half], psum_tb[4 + 2 * c + half]
                    for fq in range(8):
                        mm(pa[:, 0:512], hq3[:, fq, c * 128:(c + 1) * 128], w2[:, fq, :], q == 0 and fq == 0, False,
                           [hid_tb[q % 2][fq], w2_tb], [pab])
        for c in range(2):
            for half in range(2):
                pa, pab = psum_t[4 + 2 * c + half], psum_tb[4 + 2 * c + half]
                mm(pa[:, 0:512], ONESb[0:1, :], b2row[0:1, half * 512:(half + 1) * 512], False, True, [cstb_tb, b2row_tb], [pab])
                xsl = xs3[:, c, half * 512:(half + 1) * 512]
                tt("dve", rtmp[half][:], pa[:, 0:512], G[:, 1024 + half * 512:1024 + (half + 1) * 512], ALU.mult,
                   [pab, G_tb], [rtmp_tb[half]])
                stt_p(xsl, xsl, ALPHA, rtmp[half][:], ALU.mult, ALU.add, [xs_tb[c], rtmp_tb[half]], [xs_tb[c]])
            x2 = xs3[:, c, :]
            ln_stats(c, x2, [xs_tb[c]])
            ts("dve", x2, x2, mv[:, 4 * c:4 * c + 1], mv[:, 4 * c + 2:4 * c + 3], ALU.subtract, ALU.mult, [xs_tb[c], st_tb[c]], [xs_tb[c]])
            tt("dve", x2, x2, lnt[:, 2048:3072], ALU.mult, [xs_tb[c], lnt_tb], [xs_tb[c]])
            tt("dve", x2, x2, lnt[:, 3072:4096], ALU.add, [xs_tb[c], lnt_tb], [xs_tb[c]])
            P.dma("pool", outd[b, ti * TT + c * 128: ti * TT + (c + 1) * 128, :], x2, reads=[xs_tb[c]], writes=[TB()], is_output=True)

    hbb_tbs = [TB(f"hbb{i}") for i in range(NTILE)]
    try:
        for b in range(NB):
            for gi in range(2):
                r0 = (b * 2 + gi) * 128
                P.dma("sp", G[:, gi * 1024:(gi + 1) * 1024], gsc[r0:r0 + 128, :], writes=[G_tb])
            memset("pool", hTf[:], 0.0, hTf_tb)
            memset("pool", hTb[:], 0.0, hTb_tb)
            memset("pool", lruc[:], 0.0, lruc_tb)
            tile(b, "ctx", 0)
            if debug in ("ctx", "s1", "dt", "ssd", "ssd_a", "ssd_b", "ssd_c", "ssd_b1", "ssd_b2"):
                raise _Stop()
            for ti in reversed(range(NTILE)):
                tile(b, "bwd", ti)
            if debug == "bwdall":
                raise _Stop()
            for ti in range(NTILE):
                tile(b, "full", ti)
    except _Stop:
        pass
    if debug:
        P.dma("sp", dbg["hTf"][:, :], hTf[:], reads=hTf_tb, writes=[TB()], is_output=True)
        P.dma("sp", dbg["hTb"][:, :], hTb[:], reads=hTb_tb, writes=[TB()], is_output=True)
        P.dma("sp", dbg["lruc"][:, :], lruc[:], reads=lruc_tb, writes=[TB()], is_output=True)
        P.dma("sp", dbg["hT"][:, :], hT[:], reads=hT_tb, writes=[TB()], is_output=True)
        P.dma("sp", dbg["xs"][:, :], xs[:], reads=xs_tb, writes=[TB()], is_output=True)
        P.dma("sp", dbg["unT"][:, :], unT[:], reads=[unT_tb[k][c] for k in range(16) for c in range(2)], writes=[TB()], is_output=True)
        P.dma("sp", dbg["ylg"][:, :], ylg[:], reads=ylg_tb, writes=[TB()], is_output=True)
        P.dma("sp", dbg["merged"][:, :], merged[:], reads=merged_tb, writes=[TB()], is_output=True)
        for q in range(4):
            P.dma("sp", dbg["xbc"][:, q * TT:(q + 1) * TT], xbc[1][q][:], reads=[xbc_tb[1][q]], writes=[TB()], is_output=True)
        P.dma("sp", dbg["dtt"][:, :], dtt[:], reads=dt_tb, writes=[TB()], is_output=True)
        P.dma("sp", dbg["eall"][:, :], eall[:], reads=dt_tb, writes=[TB()], is_output=True)
        P.dma("sp", dbg["xtok"][:, :], xtok[:], reads=xtok_tb, writes=[TB()], is_output=True)
        P.dma("sp", dbg["btok"][:, :], btok[:], reads=btok_tb, writes=[TB()], is_output=True)
    P.finalize()
    return nc

_CACHE = {}


def _consts():
    k = np.arange(128)[:, None]
    i = np.arange(128)[None, :]
    mats = [(k <= i), (k > i), (k >= i), (k < i), np.ones((128, 128), bool), (k == i)]
    return np.concatenate([m.astype(np.float32) for m in mats], axis=1)


def _kmajor(w):
    K, N = w.shape
    return np.ascontiguousarray(w.reshape(K // 128, 128, N).transpose(1, 0, 2).reshape(128, (K // 128) * N))


def _cols128(v):
    return np.ascontiguousarray(v.reshape(-1, 128).T)


def prep_shared(inp):
    f32 = np.float32
    w_in = np.asarray(inp["w_in"][0], f32)
    cols = []
    for g in range(8):
        cols += list(range(256 * g, 256 * g + 256))
        cols += list(range(2048 + 128 * g, 2048 + 128 * g + 128))
        cols += list(range(4160 + 128 * g, 4160 + 128 * g + 128))
    cols += list(range(5184, 7232)) + list(range(3072, 3136)) + list(range(3136, 4160))
    cols += list(range(7232, 8256)) + list(range(8256, 10304))
    cols = np.asarray(cols)
    assert cols.shape[0] == 10304 and np.unique(cols).shape[0] == 10304
    lw = np.stack([np.asarray(inp["lru_wa"][0], f32), np.asarray(inp["lru_wi"][0], f32)], axis=1)
    lw = np.ascontiguousarray(lw.transpose(3, 0, 1, 2, 4).reshape(128, 32 * 128))
    wall = np.concatenate([
        _kmajor(w_in[:, cols]), _kmajor(np.asarray(inp["w_br_ssd"][0], f32)), _kmajor(np.asarray(inp["w_br_lru"][0], f32)),
        _kmajor(np.asarray(inp["w_out"][0], f32)), _kmajor(np.asarray(inp["w_mlp1"][0], f32)),
        _kmajor(np.asarray(inp["w_mlp2"][0], f32)), lw], axis=1)
    assert wall.shape == (128, W_TOT)
    cw = np.asarray(inp["ssd_conv_w"][0], f32)
    cb = np.asarray(inp["ssd_conv_b"][0], f32)
    pp = np.zeros((128, NPP), f32)
    for g in range(8):
        for q, ch0 in enumerate([256 * g, 256 * g + 128, 2048 + 128 * g, 3072 + 128 * g]):
            ccn = 4 * g + q
            pp[:, PP_CW + ccn * 4:PP_CW + ccn * 4 + 4] = cw[:, ch0:ch0 + 128].T
            pp[:, PP_CB + ccn] = cb[ch0:ch0 + 128]
    lcw = np.asarray(inp["lru_conv_w"][0], f32)
    for k in range(8):
        pp[:, PP_LCW + k * 4:PP_LCW + k * 4 + 4] = lcw[:, 128 * k:128 * k + 128].T
    pp[:, PP_LCB:PP_LCB + 8] = _cols128(np.asarray(inp["lru_conv_b"][0], f32))
    pp[:, PP_BA:PP_BA + 16] = _cols128(np.asarray(inp["lru_ba"][0], f32).reshape(-1))
    pp[:, PP_BI:PP_BI + 16] = _cols128(np.asarray(inp["lru_bi"][0], f32).reshape(-1))
    pp[:, PP_LAM:PP_LAM + 16] = _cols128(np.asarray(inp["lru_lambda"][0], f32).reshape(-1))
    pp[:, PP_BG:PP_BG + 16] = _cols128(np.asarray(inp["b_gate"][0], f32))
    pp[:, PP_NW:PP_NW + 16] = _cols128(np.asarray(inp["ssd_norm_w"][0], f32))
    pp[:, PP_B1:PP_B1 + 32] = _cols128(np.asarray(inp["b_mlp1"][0], f32))
    bm = np.asarray(inp["b_mod"][0], f32)
    pp[:, PP_BM:PP_BM + 48] = _cols128(bm)
    rb = np.concatenate([np.asarray(inp[k][0], f32).reshape(-1) for k in ("ln1_g", "ln1_b", "ln2_g", "ln2_b", "b_mlp2")]
                        + [bm[2048:3072], bm[5120:6144], np.asarray(inp["ssd_dt_bias"][0], f32).reshape(-1),
                           np.asarray(inp["ssd_a_log"][0], f32).reshape(-1), np.asarray(inp["ssd_d"][0], f32).reshape(-1)])
    assert rb.shape[0] == NRB
    wmod = _kmajor(np.asarray(inp["w_mod"][0], f32))
    return dict(wall=wall, pp=pp, rb=np.ascontiguousarray(rb), wmod=wmod, consts=_consts())


def core_inputs(inp, shared, b0, NB):
    f32 = np.float32
    cc = np.zeros((5, 1024), f32)
    cc[:NB] = np.asarray(inp["c"], f32)[b0:b0 + NB]
    cc[4] = np.asarray(inp["c_ctx"], f32)
    ccT = np.ascontiguousarray(cc.reshape(5, 8, 128).transpose(2, 1, 0).reshape(128, 40))
    d = dict(shared)
    d["xin"] = np.ascontiguousarray(np.asarray(inp["x"], f32)[b0:b0 + NB])
    d["ctxin"] = np.ascontiguousarray(np.asarray(inp["ctx"], f32)[b0:b0 + NB])
    d["ccT"] = ccT
    return d


def kernel(**inputs):
    NB = 4
    if "nc" not in _CACHE:
        _CACHE["nc"] = build_program(NB)
    nc = _CACHE["nc"]
    shared = prep_shared(inputs)
    in_maps = [core_inputs(inputs, shared, 4 * i, NB) for i in range(8)]
    res = run_bass_kernel_spmd(nc, in_maps, core_ids=list(range(8)))
    return np.concatenate([r["out"] for r in res.results], axis=0)
```

```python
import numpy as np
from concourse.bass_utils import run_bass_kernel_spmd
import concourse.bass as bass
import concourse.mybir as mybir

F32 = mybir.dt.float32
BF16 = mybir.dt.bfloat16
AF = mybir.ActivationFunctionType
ALU = mybir.AluOpType
AX = mybir.AxisListType


class TB:
    __slots__ = ("w", "rs", "name")

    def __init__(self, name=""):
        self.w = None
        self.rs = []
        self.name = name


class Op:
    __slots__ = ("eng", "fn", "deps", "idx", "eidx", "signal", "tok", "isdma")


COMPUTE = ("pe", "act", "dve", "pool")
ENGS = ("pe", "act", "dve", "pool", "sp")
EPOCH = 12000
NDMASEM = 12


class Prog:
    def __init__(self, nc, sem_stack):
        self.nc = nc
        self.ops = []
        self.ecount = {e: 0 for e in ENGS}
        self.last = {e: None for e in ENGS}
        self.sem_stack = sem_stack
        self.esems = {e: [] for e in COMPUTE}
        self.dsems = {}
        self.dma_count = {}
        self.dma_semval = {}
        self.dma_prev = {}
        self.seen = {e: {} for e in ENGS}
        self.tickets = {e: 0 for e in COMPUTE}
        self.emitted = 0
        self.out_tokens = []

    def _sem(self, name):
        return self.sem_stack.enter_context(self.nc.semaphore(name))

    def add(self, eng, fn, reads=(), writes=(), dma=False):
        op = Op()
        op.eng = eng
        op.fn = fn
        op.isdma = dma
        op.idx = len(self.ops)
        op.eidx = self.ecount[eng]
        self.ecount[eng] += 1
        op.signal = False
        op.tok = None
        deps = {}
        for r in reads:
            if r.w is not None:
                deps[r.w.idx] = r.w
        for w in writes:
            if w.w is not None:
                deps[w.w.idx] = w.w
            for o in w.rs:
                deps[o.idx] = o
        deps.pop(op.idx, None)
        best = {}
        out = []
        for d in deps.values():
            if d.isdma:
                out.append(d)
            else:
                b = best.get(d.eng)
                if b is None or d.idx > b.idx:
                    best[d.eng] = d
        for e, d in best.items():
            if e == eng and not dma:
                if eng == "pe":
                    continue
                if op.eidx - d.eidx > 1:
                    continue
            out.append(d)
        for d in out:
            d.signal = True
        op.deps = out
        for r in reads:
            r.rs.append(op)
        for w in writes:
            w.w = op
            w.rs = []
        self.ops.append(op)
        self.last[eng] = op
        return op

    def dma(self, eng, out, in_, reads=(), writes=(), is_output=False):
        op = self.add(eng, lambda e: e.dma_start(out=out, in_=in_), reads, writes, dma=True)
        op.signal = True
        if is_output:
            self.out_tokens.append(op)
        return op

    def barrier(self):
        lasts = [self.last[e] for e in ENGS if self.last[e] is not None]
        dmas = [o for o in self.ops[self.emitted:] if o.isdma]
        for e in ENGS:
            op = Op()
            op.eng = e
            op.fn = None
            op.isdma = False
            op.idx = len(self.ops)
            op.eidx = self.ecount[e]
            op.signal = False
            op.tok = None
            op.deps = []
            for d in lasts + dmas:
                if d.eng == e and not d.isdma:
                    continue
                if d not in op.deps:
                    d.signal = True
                    op.deps.append(d)
            self.ops.append(op)

    def finalize(self):
        nc = self.nc
        ops = self.ops[self.emitted:]
        self.emitted = len(self.ops)
        for op in ops:
            if not op.signal:
                continue
            if op.isdma:
                e = op.eng
                if e not in self.dsems:
                    self.dsems[e] = [self._sem(f"dq_{e}_{i}") for i in range(NDMASEM)]
                    self.dma_count[e] = 0
                    self.dma_semval[e] = [0] * NDMASEM
                    self.dma_prev[e] = [None] * NDMASEM
                k = self.dma_count[e] % NDMASEM
                self.dma_count[e] += 1
                self.dma_semval[e][k] += 16
                op.tok = (self.dsems[e][k], self.dma_semval[e][k], k)
            else:
                e = op.eng
                t = self.tickets[e]
                self.tickets[e] += 1
                ep, v = divmod(t, EPOCH)
                while len(self.esems[e]) <= ep:
                    self.esems[e].append(self._sem(f"es_{e}_{len(self.esems[e])}"))
                op.tok = (self.esems[e][ep], v + 1, None)
        per = {e: [o for o in ops if o.eng == e] for e in ENGS}
        outs = list(self.out_tokens)
        self.out_tokens = []

        def emit(ename, eng):
            seen = self.seen[ename]

            def wait(tok):
                sem, val = tok[0], tok[1]
                key = id(sem)
                if seen.get(key, 0) >= val:
                    return
                eng.wait_ge(sem, val)
                seen[key] = val

            for op in per[ename]:
                for d in op.deps:
                    if d.tok is not None:
                        wait(d.tok)
                if op.isdma:
                    sem, val, k = op.tok
                    if val > 16:
                        wait((sem, val - 16))
                if op.fn is None:
                    continue
                inst = op.fn(eng)
                if op.signal:
                    inst.then_inc(op.tok[0], 16 if op.isdma else 1)
            if ename == "sp":
                for o in outs:
                    wait(o.tok)

        with nc.Block() as block:
            if per["pe"]:
                @block.tensor
                def _(e):
                    emit("pe", e)
            if per["act"]:
                @block.scalar
                def _(e):
                    emit("act", e)
            if per["dve"]:
                @block.vector
                def _(e):
                    emit("dve", e)
            if per["pool"]:
                @block.gpsimd
                def _(e):
                    emit("pool", e)
            if per["sp"] or outs:
                @block.sync
                def _(e):
                    emit("sp", e)

import numpy as np
from contextlib import ExitStack

D = 1024
SEQ = 2048
CTX = 256
TT = 256
NTILE = SEQ // TT
ALPHA = 2.0 ** 0.25
O_XBC, O_Z, O_DT, O_LRU, O_LG, O_GS, O_GL, NIN = 0, 4096, 6144, 6208, 7232, 8256, 9280, 10304
W_IN, W_BS, W_BL, W_O, W_M1, W_M2, W_LR, W_TOT = 0, 82432, 98816, 107008, 115200, 147968, 180736, 184832
CBLK = 4864
NCBLK = W_TOT // CBLK
PP_CW, PP_CB, PP_LCW, PP_LCB, PP_BA, PP_BI, PP_LAM, PP_BG, PP_NW, PP_B1, PP_BM, NPP = 0, 128, 160, 192, 200, 216, 232, 248, 264, 280, 312, 360
RB_L1G, RB_L1B, RB_L2G, RB_L2B, RB_B2, RB_BM2, RB_BM5, RB_DTB, RB_ALOG, RB_D, NRB = 0, 1024, 2048, 3072, 4096, 5120, 6144, 7168, 7232, 7296, 7328
NCONST = 768


class _Stop(Exception):
    pass


def build_program(NB, debug=None):
    nc = bass.Bass("TRN2", target_bir_lowering=False)
    es = ExitStack()
    with es:
        return _build(nc, es, NB, debug)


def _build(nc, es, NB, debug):
    xin = nc.dram_tensor("xin", [NB, SEQ, D], F32, kind="ExternalInput").ap()
    ctxin = nc.dram_tensor("ctxin", [NB, CTX, D], F32, kind="ExternalInput").ap()
    ccT = nc.dram_tensor("ccT", [128, 8 * 5], F32, kind="ExternalInput").ap()
    wmod = nc.dram_tensor("wmod", [128, 8 * 6144], F32, kind="ExternalInput").ap()
    wall = nc.dram_tensor("wall", [128, W_TOT], F32, kind="ExternalInput").ap()
    ppd = nc.dram_tensor("pp", [128, NPP], F32, kind="ExternalInput").ap()
    rbd = nc.dram_tensor("rb", [NRB], F32, kind="ExternalInput").ap()
    cstd = nc.dram_tensor("consts", [128, NCONST], F32, kind="ExternalInput").ap()
    outd = nc.dram_tensor("out", [NB, SEQ, D], F32, kind="ExternalOutput").ap()
    wbf = nc.dram_tensor("wbf", [128, W_TOT], BF16, kind="Internal").ap()
    gsc = nc.dram_tensor("gsc", [NB * 2 * 128, 1024], F32, kind="Internal").ap()
    hbb = nc.dram_tensor("hbb", [NTILE * 128, 2048], F32, kind="Internal").ap()

    P = Prog(nc, es)
    dbg = {}
    if debug:
        dbg["modF"] = nc.dram_tensor("d_modF", [128, 160], F32, kind="ExternalOutput").ap()
        dbg["gsc"] = nc.dram_tensor("d_gsc", [NB * 2 * 128, 1024], F32, kind="ExternalOutput").ap()
        dbg["hTf"] = nc.dram_tensor("d_hTf", [128, 2048], F32, kind="ExternalOutput").ap()
        dbg["hTb"] = nc.dram_tensor("d_hTb", [128, 2048], F32, kind="ExternalOutput").ap()
        dbg["lruc"] = nc.dram_tensor("d_lruc", [128, 16], F32, kind="ExternalOutput").ap()
        dbg["hT"] = nc.dram_tensor("d_hT", [128, 8 * TT], BF16, kind="ExternalOutput").ap()
        dbg["xs"] = nc.dram_tensor("d_xs", [128, 2048], F32, kind="ExternalOutput").ap()
        dbg["unT"] = nc.dram_tensor("d_unT", [128, 16 * TT], BF16, kind="ExternalOutput").ap()
        dbg["ylg"] = nc.dram_tensor("d_ylg", [128, 8 * TT], BF16, kind="ExternalOutput").ap()
        dbg["merged"] = nc.dram_tensor("d_merged", [128, 8 * TT], BF16, kind="ExternalOutput").ap()
        dbg["xbc"] = nc.dram_tensor("d_xbc", [128, 4 * TT], BF16, kind="ExternalOutput").ap()
        dbg["dtt"] = nc.dram_tensor("d_dtt", [128, 128], F32, kind="ExternalOutput").ap()
        dbg["eall"] = nc.dram_tensor("d_eall", [128, 384], F32, kind="ExternalOutput").ap()
        dbg["xtok"] = nc.dram_tensor("d_xtok", [128, 512], BF16, kind="ExternalOutput").ap()
        dbg["btok"] = nc.dram_tensor("d_btok", [128, 256], BF16, kind="ExternalOutput").ap()

    def sb(name, shape, dt=F32):
        return es.enter_context(nc.sbuf_tensor(name, shape, dt))

    def act(out, in_, func, reads, writes, bias=None, scale=None, accum=None):
        kw = {}
        if bias is not None:
            kw["bias"] = bias
        if scale is not None:
            kw["scale"] = scale
        if accum is not None:
            kw["accum_out"] = accum
        return P.add("act", lambda e: e.activation(out=out, in_=in_, func=func, **kw), reads, writes)

    def tt(eng, out, in0, in1, op, reads, writes):
        return P.add(eng, lambda e: e.tensor_tensor(out=out, in0=in0, in1=in1, op=op), reads, writes)

    def ts(eng, out, in0, s1, s2, op0, op1, reads, writes):
        if op1 is None:
            return P.add(eng, lambda e: e.tensor_scalar(out=out, in0=in0, scalar1=s1, scalar2=None, op0=op0), reads, writes)
        return P.add(eng, lambda e: e.tensor_scalar(out=out, in0=in0, scalar1=s1, scalar2=s2, op0=op0, op1=op1), reads, writes)

    def stt(eng, out, in0, scalar, in1, op0, op1, reads, writes):
        return P.add(eng, lambda e: e.scalar_tensor_tensor(out=out, in0=in0, scalar=scalar, in1=in1, op0=op0, op1=op1), reads, writes)

    def stt_p(out, in0, scalar, in1, op0, op1, reads, writes):
        return stt("dve", out, in0, scalar, in1, op0, op1, reads, writes)

    def cp(eng, out, in_, reads, writes):
        if eng == "act":
            return P.add("act", lambda e: e.activation(out=out, in_=in_, func=AF.Identity), reads, writes)
        return P.add(eng, lambda e: e.tensor_copy(out=out, in_=in_), reads, writes)

    def affine(eng, out, in_, scale_ap, bias_ap, reads, writes):
        if eng == "act":
            return act(out, in_, AF.Identity, reads, writes, bias=bias_ap, scale=scale_ap)
        return ts(eng, out, in_, scale_ap, bias_ap, ALU.mult, ALU.add, reads, writes)

    def mm(out, lhsT, rhs, start, stop, reads, writes):
        return P.add("pe", lambda e: e.matmul(out, lhsT, rhs, start=start, stop=stop), reads, writes)

    def trp(out, in_, ident, reads, writes):
        return P.add("pe", lambda e: e.transpose(out, in_, ident), reads, writes)

    def memset(eng, ap, val, writes):
        return P.add(eng, lambda e: e.memset(ap, val), (), writes)

    psum_t = [es.enter_context(nc.psum_tensor(f"ps{i}", [128, 512], F32)) for i in range(8)]
    psum_tb = [TB(f"ps{i}") for i in range(8)]
    rot = {"i": 0, "n": 8}

    def psum():
        i = rot["i"] % rot["n"]
        rot["i"] += 1
        return psum_t[i], psum_tb[i]

    def psum_bf():
        t, b = psum()
        return t[:].bitcast(BF16), b

    pp = sb("pp_s", [128, NPP])
    pp_tb = TB("pp")
    der = sb("der", [128, 256])
    der_tb = TB("der")
    D_HCW, D_HCB, D_HBA, D_HBI, D_C1H, D_HBG = 0, 128, 160, 176, 192, 208
    modF = sb("modF", [128, 4 * 8 * 5])
    modF_tb = TB("modF")
    lnt = sb("lnt", [128, 4 * 1024])
    lnt_tb = TB("lnt")
    misc = sb("misc", [128, 160])
    misc_tb = TB("misc")
    cst = sb("cst", [128, NCONST])
    cst_tb = TB("cst")
    cstb = sb("cstb", [128, NCONST], BF16)
    cstb_tb = TB("cstb")
    LE, GT, GE, LT, ONES, IDN = [cst[:, i * 128:(i + 1) * 128] for i in range(6)]
    LEb, GTb, GEb, LTb, ONESb, IDNb = [cstb[:, i * 128:(i + 1) * 128] for i in range(6)]
    lruw = sb("lruw", [128, 32 * 128], BF16)
    lruw_tb = TB("lruw")
    b2row = sb("b2row", [1, 1024], BF16)
    b2row_tb = TB("b2row")

    P.dma("sp", pp[:], ppd[:, :], writes=[pp_tb])
    P.dma("sp", cst[:], cstd[:, :], writes=[cst_tb])
    for i in range(4):
        P.dma("sp", lnt[:, i * 1024:(i + 1) * 1024], rbd[i * 1024:(i + 1) * 1024].partition_broadcast(128), writes=[lnt_tb])
    P.dma("sp", misc[:, 0:160], rbd[RB_DTB:RB_DTB + 160].partition_broadcast(128), writes=[misc_tb])
    cp("dve", cstb[:], cst[:], [cst_tb], [cstb_tb])
    act(misc[:, 64:128], misc[:, 64:128], AF.Exp, [misc_tb], [misc_tb])
    ts("dve", misc[:, 64:128], misc[:, 64:128], -1.0, None, ALU.mult, None, [misc_tb], [misc_tb])
    ts("dve", der[:, D_HCW:D_HCW + 160], pp[:, PP_CW:PP_CW + 160], 0.5, None, ALU.mult, None, [pp_tb], [der_tb])
    ts("dve", der[:, D_HBA:D_HBA + 32], pp[:, PP_BA:PP_BA + 32], 0.5, None, ALU.mult, None, [pp_tb], [der_tb])
    ts("dve", der[:, D_HBG:D_HBG + 16], pp[:, PP_BG:PP_BG + 16], 0.5, None, ALU.mult, None, [pp_tb], [der_tb])
    act(der[:, D_C1H:D_C1H + 16], pp[:, PP_LAM:PP_LAM + 16], AF.Exp, [pp_tb], [der_tb], scale=-1.0)
    act(der[:, D_C1H:D_C1H + 16], der[:, D_C1H:D_C1H + 16], AF.Ln, [der_tb], [der_tb], bias=1.0)
    ts("dve", der[:, D_C1H:D_C1H + 16], der[:, D_C1H:D_C1H + 16], -4.0, None, ALU.mult, None, [der_tb], [der_tb])

    wbf_tbs = [TB(f"wbf{i}") for i in range(NCBLK)]
    with ExitStack() as es0:
        def sb0(name, shape, dt=F32):
            return es0.enter_context(nc.sbuf_tensor(name, shape, dt))
        wf = [sb0(f"wf{i}", [128, CBLK]) for i in range(2)]
        wb = [sb0(f"wb{i}", [128, CBLK], BF16) for i in range(2)]
        wf_tb = [TB() for _ in range(2)]
        wb_tb = [[TB() for _ in range(3)] for _ in range(2)]
        cuts = [0, 1664, 3264, 4864]
        for i in range(NCBLK):
            u = i % 2
            P.dma("sp", wf[u][:], wall[:, i * CBLK:(i + 1) * CBLK], writes=[wf_tb[u]])
            for j, eng in enumerate(("act", "dve", "pool")):
                a, b = cuts[j], cuts[j + 1]
                cp(eng, wb[u][:, a:b], wf[u][:, a:b], [wf_tb[u]], [wb_tb[u][j]])
            P.dma("pool", wbf[:, i * CBLK:(i + 1) * CBLK], wb[u][:], reads=wb_tb[u], writes=[wbf_tbs[i]])

        cc = sb0("cc", [128, 40])
        sct = sb0("sct", [128, 40])
        scT = sb0("scT", [128, 40])
        cc_tb, sct_tb, scT_tb = TB(), TB(), TB()
        P.dma("sp", cc[:], ccT[:, :], writes=[cc_tb])
        act(sct[:], cc[:], AF.Tanh, [cc_tb], [sct_tb], scale=0.5)
        stt("dve", sct[:], sct[:], 1.0, cc[:], ALU.add, ALU.mult, [sct_tb, cc_tb], [sct_tb])
        ts("dve", scT[:], sct[:], 0.5, None, ALU.mult, None, [sct_tb], [scT_tb])
        scT3 = scT[:].rearrange("p (k r) -> p k r", r=5)
        scB = sb0("scB", [128, 8 * NB * 128])
        scB_tb = TB()
        for kc in range(8):
            for j in range(NB):
                o = (kc * NB + j) * 128
                cp("pool", scB[:, o:o + 128], scT3[:, kc, j:j + 1].to_broadcast([128, 128]), [scT_tb], [scB_tb])
        wm = [sb0(f"wm{i}", [128, 8 * 1024]) for i in range(2)]
        wm_tb = [TB() for _ in range(2)]
        rbg = sb0("rbg", [128, 3 * 1024])
        rbg_tb = TB()
        P.dma("sp", rbg[:], rbd[RB_B2:RB_B2 + 3072].partition_broadcast(128), writes=[rbg_tb])
        b2f = sb0("b2f", [1, 1024])
        b2f_tb = TB()
        P.dma("sp", b2f[:], rbd[RB_B2:RB_B2 + 1024].partition_broadcast(1), writes=[b2f_tb])
        cp("dve", b2row[:], b2f[:], [b2f_tb], [b2row_tb])
        gst = sb0("gst", [128, 2 * 1024])
        gst_tb = [TB(), TB()]
        wmod3 = wmod.rearrange("p (k n) -> p k n", k=8)
        modF4 = modF[:].rearrange("p (m q r) -> p m q r", m=4, q=8)
        fm_index = {0: 0, 1: 1, 3: 2, 4: 3}
        for bi, mi in enumerate(range(6)):
            u = bi % 2
            w3 = wm[u][:].rearrange("p (k n) -> p k n", k=8)
            P.dma("sp", w3, wmod3[:, :, mi * 1024:(mi + 1) * 1024], writes=[wm_tb[u]])
            if mi in fm_index:
                m = fm_index[mi]
                for dq in range(8):
                    pt, ptb = psum()
                    for kc in range(8):
                        mm(pt[:, 0:5], w3[:, kc, dq * 128:(dq + 1) * 128], scT3[:, kc, :], kc == 0, kc == 7, [wm_tb[u], scT_tb], [ptb])
                    col = PP_BM + mi * 8 + dq
                    ts("dve", modF4[:, m, dq, :], pt[:, 0:5], pp[:, col:col + 1], 1.0 if mi in (1, 4) else 0.0,
                       ALU.add, ALU.add, [ptb, pp_tb], [modF_tb])
            else:
                gi = 0 if mi == 2 else 1
                for j in range(NB):
                    for half in range(2):
                        pt, ptb = psum()
                        for kc in range(8):
                            o = (kc * NB + j) * 128
                            mm(pt[:, 0:512], scB[:, o:o + 128], w3[:, kc, half * 512:(half + 1) * 512], kc == 0, kc == 7,
                               [wm_tb[u], scB_tb], [ptb])
                        gs = gst[:, gi * 1024 + half * 512: gi * 1024 + (half + 1) * 512]
                        bo = (1 + gi) * 1024 + half * 512
                        tt("dve", gs, pt[:, 0:512], rbg[:, bo:bo + 512], ALU.add, [ptb, rbg_tb], [gst_tb[gi]])
                    if gi == 0:
                        ts("dve", gst[:, 0:1024], gst[:, 0:1024], 0.5, None, ALU.mult, None, [gst_tb[0]], [gst_tb[0]])
                    r0 = (j * 2 + gi) * 128
                    P.dma("pool", gsc[r0:r0 + 128, :], gst[:, gi * 1024:(gi + 1) * 1024], reads=[gst_tb[gi]], writes=[TB()])
        if debug:
            P.dma("sp", dbg["modF"][:, :], modF[:], reads=[modF_tb], writes=[TB()], is_output=True)
        P.barrier()
        P.finalize()
    if debug:
        P.dma("sp", dbg["gsc"][:, :], gsc[:, :], writes=[TB()], is_output=True)
    if debug == "p0":
        P.finalize()
        return nc

    G = sb("G", [128, 2 * 1024])
    G_tb = TB("G")
    hTf = sb("hTf", [128, 2048])
    hTb = sb("hTb", [128, 2048])
    hTf_tb = [TB(f"hTf{g}") for g in range(8)]
    hTb_tb = [TB(f"hTb{g}") for g in range(8)]
    lruc = sb("lruc", [128, 16])
    lruc_tb = [TB(f"lruc{i}") for i in range(16)]
    lrub = sb("lrub", [128, NTILE * 8])
    lrub_tb = TB("lrub")
    junk = sb("junk", [128, 256])
    junk_tb = TB("junk")
    xs = sb("xs", [128, 2 * 1024])
    xs_tb = [TB("xs0"), TB("xs1")]
    hT = sb("hT", [128, 8 * TT], BF16)
    hT_tb = [TB("hT0"), TB("hT1")]
    h1T = sb("h1T", [128, 8 * TT], BF16)
    h1T_tb = [TB("h1T0"), TB("h1T1")]
    NRING = 4
    ring = [sb(f"ring{i}", [128, 4096], BF16) for i in range(NRING)]
    ring_tb = [TB(f"ring{i}") for i in range(NRING)]
    ringi = {"i": 0}
    st6 = sb("st6", [128, 2 * 12])
    mv = sb("mv", [128, 2 * 4])
    st_tb = [TB(), TB()]
    _xn = sb("xn0", [128, 1024], BF16)
    xn = [_xn, _xn]
    _xntb = TB()
    xn_tb = [_xntb, _xntb]
    dtv = sb("dtv", [128, 2 * 64])
    dta = sb("dta", [128, 2 * 64])
    dtt = sb("dtt", [128, 2 * 64])
    dtA = sb("dtA", [128, 2 * 64])
    dtw = sb("dtw", [128, 2 * 64])
    eall = sb("eall", [128, 2 * 192])
    dt_tb = [TB("dt0"), TB("dt1")]
    usb = [sb(f"usb{i}", [128, TT]) for i in range(4)]
    cacc = [sb(f"cacc{i}", [128, TT]) for i in range(4)]
    cth = [sb(f"cth{i}", [128, TT]) for i in range(4)]
    usb_tb = [TB() for _ in range(4)]
    cacc_tb = [TB() for _ in range(4)]
    cth_tb = [TB() for _ in range(4)]
    xbc = [[sb(f"xbc{r}_{q}", [128, TT], BF16) for q in range(4)] for r in range(2)]
    xbc_tb = [[TB() for q in range(4)] for r in range(2)]
    xtok = sb("xtok", [128, 2 * 256], BF16)
    btok = sb("btok", [128, 2 * 128], BF16)
    xtok_tb = [TB(), TB()]
    btok_tb = [TB(), TB()]
    NV = 2
    ver = [[sb(f"ver{r}_{v}", [128, 256], BF16) for v in range(5)] for r in range(NV)]
    ver_tb = [[TB() for v in range(5)] for r in range(NV)]
    hsb = sb("hsb", [128, 2 * 256], BF16)
    hsb_tb = [TB(), TB()]
    hsf = [sb(f"hsf{i}", [128, 256], BF16) for i in range(2)]
    hsf_tb = [TB(), TB()]
    Rt = [[sb(f"R{r}_{d}", [128, 512], BF16) for d in range(2)] for r in range(2)]
    Rt_tb = [[TB() for d in range(2)] for r in range(2)]
    Lt = [[sb(f"L{r}_{d}", [128, 512], BF16) for d in range(2)] for r in range(2)]
    Lt_tb = [[TB() for d in range(2)] for r in range(2)]
    Mt = [[sb(f"M{r}_{d}", [128, 512], BF16) for d in range(2)] for r in range(2)]
    Mt_tb = [[TB() for d in range(2)] for r in range(2)]
    CBm = [[sb(f"CB{r}_{d}", [128, 128], BF16) for d in range(2)] for r in range(2)]
    CBm_tb = [[TB() for d in range(2)] for r in range(2)]
    t12 = [[sb(f"t12_{r}_{d}", [128, 256]) for d in range(2)] for r in range(2)]
    t12_tb = [[TB() for d in range(2)] for r in range(2)]
    yy = [sb(f"yy{r}", [128, 256]) for r in range(2)]
    yy_tb = [TB(), TB()]
    tz = [sb(f"tz{r}", [128, 256]) for r in range(2)]
    tz_tb = [TB(), TB()]
    szt = [sb(f"sz{r}", [128, 256]) for r in range(2)]
    sz_tb = [TB(), TB()]
    ut = [sb(f"ut{r}", [128, 256]) for r in range(2)]
    ut_tb = [TB(), TB()]
    ssq = sb("ssq", [128, 4])
    ssq_tb = [TB(), TB()]
    unb = [sb(f"unb{r}", [128, 256], BF16) for r in range(2)]
    unb_tb = [TB(), TB()]
    unT = sb("unT", [128, 16 * TT], BF16)
    unT_tb = [[TB() for c in range(2)] for k in range(16)]
    ubf = [sb(f"ubf_{r}", [128, TT], BF16) for r in range(2)]
    ubf_tb = [TB(), TB()]
    ltmp = [sb(f"ltmp{i}", [128, TT]) for i in range(8)]
    ltmp_tb = [TB() for _ in range(8)]
    hfw = sb("hfw", [128, TT])
    hfw_tb = TB()
    ysum = sb("ysum", [128, TT])
    ysum_tb = TB()
    gtmp = [sb(f"gtmp{i}", [128, TT]) for i in range(4)]
    gtmp_tb = [TB() for _ in range(4)]
    ylg = sb("ylg", [128, 8 * TT], BF16)
    ylg_tb = [TB() for _ in range(8)]
    merged = sb("merged", [128, 8 * TT], BF16)
    merged_tb = [TB() for _ in range(8)]
    tg = [gtmp[0], gtmp[1]]
    tg_tb = [gtmp_tb[0], gtmp_tb[1]]
    m12 = [gtmp[2], gtmp[3]]
    m12_tb = [gtmp_tb[2], gtmp_tb[3]]
    _rt = sb("rtmp0", [128, 512])
    _rtb = TB()
    rtmp = [_rt, _rt]
    rtmp_tb = [_rtb, _rtb]
    hid = [sb(f"hid{i}", [128, 8 * TT], BF16) for i in range(2)]
    hid_tb = [[TB() for f in range(8)] for i in range(2)]
    rr = [sb(f"rr{i}", [128, TT], BF16) for i in range(2)]
    rr_tb = [TB(), TB()]

    print("SBUF bytes remaining:", nc.sbuf_bytes_remaining)
    wbf_in = wbf[:, W_IN:W_IN + 8 * NIN].rearrange("p (k n) -> p k n", k=8)
    wbf_bs = wbf[:, W_BS:W_BS + 16 * 1024].rearrange("p (k n) -> p k n", k=16)
    wbf_bl = wbf[:, W_BL:W_BL + 8 * 1024].rearrange("p (k n) -> p k n", k=8)
    wbf_o = wbf[:, W_O:W_O + 8 * 1024].rearrange("p (k n) -> p k n", k=8)
    wbf_m1 = wbf[:, W_M1:W_M1 + 8 * 4096].rearrange("p (k n) -> p k n", k=8)
    wbf_m2 = wbf[:, W_M2:W_M2 + 32 * 1024].rearrange("p (k n) -> p k n", k=32)

    def wload(dram3, k0, k1, c0, c1):
        i = ringi["i"] % NRING
        ringi["i"] += 1
        nk, ncol = k1 - k0, c1 - c0
        assert nk * ncol <= 4096
        v = ring[i][:, 0:nk * ncol].rearrange("p (k n) -> p k n", k=nk)
        P.dma("sp", v, dram3[:, k0:k1, c0:c1], reads=wbf_tbs, writes=[ring_tb[i]])
        return v, ring_tb[i]

    def wload_multi(specs):
        i = ringi["i"] % NRING
        ringi["i"] += 1
        off = 0
        views = []
        for (dram3, k0, k1, c0, c1) in specs:
            nk, ncol = k1 - k0, c1 - c0
            v = ring[i][:, off:off + nk * ncol].rearrange("p (k n) -> p k n", k=nk)
            off += nk * ncol
            assert off <= 4096
            P.dma("sp", v, dram3[:, k0:k1, c0:c1], reads=wbf_tbs, writes=[ring_tb[i]])
            views.append(v)
        return views, ring_tb[i]

    P.dma("sp", lruw[:], wbf[:, W_LR:W_LR + 4096], reads=wbf_tbs, writes=[lruw_tb])
    lruw3 = lruw[:].rearrange("p (i d) -> p i d", d=128)
    modF4 = modF[:].rearrange("p (m q r) -> p m q r", m=4, q=8)
    hT3 = hT[:].rearrange("p (k t) -> p k t", k=8)
    h1T3 = h1T[:].rearrange("p (k t) -> p k t", k=8)
    xs3 = xs[:].rearrange("p (c d) -> p c d", c=2)
    unT3 = unT[:].rearrange("p (k t) -> p k t", k=16)
    ylg3 = ylg[:].rearrange("p (k t) -> p k t", k=8)
    merged3 = merged[:].rearrange("p (k t) -> p k t", k=8)
    xtok3 = xtok[:].rearrange("p (c n) -> p c n", c=2)
    btok3 = btok[:].rearrange("p (c n) -> p c n", c=2)
    hsb3 = hsb[:].rearrange("p (c n) -> p c n", c=2)
    EPS = 1e-6

    def ln_stats(c, src_ap, reads):
        s3 = st6[:, 12 * c:12 * c + 12]
        P.add("dve", lambda e: e.bn_stats(out=s3[:, 0:6], in_=src_ap[:, 0:512]), reads, [st_tb[c]])
        P.add("dve", lambda e: e.bn_stats(out=s3[:, 6:12], in_=src_ap[:, 512:1024]), reads, [st_tb[c]])
        P.add("dve", lambda e: e.bn_aggr(out=mv[:, 4 * c:4 * c + 2], in_=s3), [st_tb[c]], [st_tb[c]])
        ts("dve", mv[:, 4 * c + 3:4 * c + 4], mv[:, 4 * c + 1:4 * c + 2], EPS, None, ALU.add, None, [st_tb[c]], [st_tb[c]])
        act(mv[:, 4 * c + 3:4 * c + 4], mv[:, 4 * c + 3:4 * c + 4], AF.Ln, [st_tb[c]], [st_tb[c]])
        act(mv[:, 4 * c + 2:4 * c + 3], mv[:, 4 * c + 3:4 * c + 4], AF.Exp, [st_tb[c]], [st_tb[c]], scale=-0.5)

    def to_hT(c, src_ap, src_reads, dst3, dst_tb, m_shift, m_scale, r):
        ln_stats(c, src_ap, src_reads)
        ts("dve", xn[c][:], src_ap, mv[:, 4 * c:4 * c + 1], mv[:, 4 * c + 2:4 * c + 3], ALU.subtract, ALU.mult,
           src_reads + [st_tb[c]], [xn_tb[c]])
        pt, ptb = psum_bf()
        for kc in range(8):
            trp(pt[:, kc * 128:(kc + 1) * 128], xn[c][:, kc * 128:(kc + 1) * 128], IDNb, [xn_tb[c], cstb_tb], [ptb])
        for kc in range(8):
            eng = "act" if kc % 2 == 0 else "dve"
            affine(eng, dst3[:, kc, c * 128:(c + 1) * 128], pt[:, kc * 128:(kc + 1) * 128],
                   modF4[:, m_scale, kc, r:r + 1], modF4[:, m_shift, kc, r:r + 1], [ptb, modF_tb], [dst_tb[c]])

    def conv(eng, i, u_ap, wcol, bcol, L, reads):
        a3 = cacc[i][:].rearrange("p (r l) -> p r l", l=L)
        u3 = u_ap.rearrange("p (r l) -> p r l", l=L)
        t3 = cth[i][:].rearrange("p (r l) -> p r l", l=L)
        ts(eng, cacc[i][:], u_ap, wcol(2), bcol, ALU.mult, ALU.add, reads, [cacc_tb[i]])
        for (k, dst, srcs) in ((1, slice(1, L), slice(0, L - 1)), (0, slice(2, L), slice(0, L - 2)), (3, slice(0, L - 1), slice(1, L))):
            if eng == "dve":
                stt(eng, a3[:, :, dst], u3[:, :, srcs], wcol(k), a3[:, :, dst], ALU.mult, ALU.add, reads + [cacc_tb[i]], [cacc_tb[i]])
            else:
                ts(eng, t3[:, :, dst], u3[:, :, srcs], wcol(k), None, ALU.mult, None, reads, [cth_tb[i]])
                tt(eng, a3[:, :, dst], a3[:, :, dst], t3[:, :, dst], ALU.add, [cacc_tb[i], cth_tb[i]], [cacc_tb[i]])

    cnt = {"g": 0, "v": 0, "c": 0, "k": 0}

    def tile(b, kind, ti):
        full = kind == "full"
        isctx = kind == "ctx"
        rot["n"] = 8

        def chk(stage):
            if debug == f"{kind}{ti}_{stage}":
                raise _Stop()
        L = 256 if isctx else 64
        r = 4 if isctx else b
        src = ctxin[b] if isctx else xin[b, ti * TT:(ti + 1) * TT, :]
        if kind == "bwd":
            P.dma("pool", hbb[ti * 128:(ti + 1) * 128, :], hTb[:], reads=hTb_tb, writes=[hbb_tbs[ti]])
            cp("pool", lrub[:, ti * 8:(ti + 1) * 8], lruc[:, 8:16], lruc_tb[8:16], [lrub_tb])
        if full:
            P.dma("pool", hTb[:], hbb[ti * 128:(ti + 1) * 128, :], reads=[hbb_tbs[ti]], writes=hTb_tb)
        for c in range(2):
            P.dma("pool", xs3[:, c, :], src[c * 128:(c + 1) * 128, :], writes=[xs_tb[c]])
            to_hT(c, xs3[:, c, :], [xs_tb[c]], hT3, hT_tb, 0, 1, r)
        if debug == "s1":
            raise _Stop()
        wdt, wdt_tb = wload(wbf_in, 0, 8, O_DT, O_DT + 64)
        for c in range(2):
            pt, ptb = psum()
            for kc in range(8):
                mm(pt[:, 0:64], hT3[:, kc, c * 128:(c + 1) * 128], wdt[:, kc, :], kc == 0, kc == 7, [hT_tb[c], wdt_tb], [ptb])
            s = slice(c * 64, (c + 1) * 64)
            tt("dve", dtv[:, s], pt[:, 0:64], misc[:, 0:64], ALU.add, [ptb, misc_tb], [dt_tb[c]])
            stt("dve", dta[:, s], dtv[:, s], -1.0, dtv[:, s], ALU.mult, ALU.max, [dt_tb[c]], [dt_tb[c]])
            act(dta[:, s], dta[:, s], AF.Exp, [dt_tb[c]], [dt_tb[c]], scale=-1.0)
            act(dta[:, s], dta[:, s], AF.Ln, [dt_tb[c]], [dt_tb[c]], bias=1.0)
            stt("dve", dtt[:, s], dtv[:, s], 0.0, dta[:, s], ALU.max, ALU.add, [dt_tb[c]], [dt_tb[c]])
            tt("dve", dtA[:, s], dtt[:, s], misc[:, 64:128], ALU.mult, [dt_tb[c], misc_tb], [dt_tb[c]])
            p2, p2b = psum()
            o = c * 64
            mm(p2[:, 0:32], LE, dtA[:, o:o + 32], True, True, [dt_tb[c], cst_tb], [p2b])
            mm(p2[:, 32:64], GE, dtA[:, o + 32:o + 64], True, True, [dt_tb[c], cst_tb], [p2b])
            mm(p2[:, 64:96], GT, dtA[:, o:o + 32], True, True, [dt_tb[c], cst_tb], [p2b])
            mm(p2[:, 96:128], LT, dtA[:, o + 32:o + 64], True, True, [dt_tb[c], cst_tb], [p2b])
            mm(p2[:, 128:192], ONES, dtA[:, o:o + 64], True, True, [dt_tb[c], cst_tb], [p2b])
            act(eall[:, c * 192:(c + 1) * 192], p2[:, 0:192], AF.Exp, [p2b], [dt_tb[c]])
            tt("dve", dtw[:, s], dtt[:, s], eall[:, c * 192 + 64:c * 192 + 128], ALU.mult, [dt_tb[c]], [dt_tb[c]])

        if debug == "dt":
            raise _Stop()

        def bc4(ap2, col0):
            return ap2[:, col0:col0 + 4].unsqueeze(2).to_broadcast([128, 4, 64])

        for g in range(8):
            rb_ = cnt["g"] % 2
            cnt["g"] += 1
            nq = 4 if full else 3
            wx, wx_tb = wload(wbf_in, 0, 8, O_XBC + g * 512, O_XBC + g * 512 + nq * 128)
            if full:
                if g % 2 == 0:
                    wz, wz_tb = wload(wbf_in, 0, 8, O_Z + g * 256, O_Z + g * 256 + 512)
            for q in range(nq):
                pt, ptb = psum()
                for kc in range(8):
                    mm(pt[:, 0:TT], wx[:, kc, q * 128:(q + 1) * 128], hT3[:, kc, :], kc == 0, kc == 7, hT_tb + [wx_tb], [ptb])
                cp("act", usb[q][:], pt[:, 0:TT], [ptb], [usb_tb[q]])
            for q in range(nq):
                ccn = 4 * g + q
                eng = "dve"
                conv(eng, q, usb[q][:], lambda k, ccn=ccn: der[:, D_HCW + ccn * 4 + k:D_HCW + ccn * 4 + k + 1],
                     der[:, D_HCB + ccn:D_HCB + ccn + 1], L, [usb_tb[q], der_tb])
            for q in range(nq):
                eng = "dve"
                act(cth[q][:], cacc[q][:], AF.Tanh, [cacc_tb[q]], [cth_tb[q]])
                if eng == "dve":
                    stt(eng, xbc[rb_][q][:], cth[q][:], 1.0, cacc[q][:], ALU.add, ALU.mult, [cth_tb[q], cacc_tb[q]], [xbc_tb[rb_][q]])
                else:
                    ts(eng, cth[q][:], cth[q][:], 1.0, None, ALU.add, None, [cth_tb[q]], [cth_tb[q]])
                    tt(eng, xbc[rb_][q][:], cth[q][:], cacc[q][:], ALU.mult, [cth_tb[q], cacc_tb[q]], [xbc_tb[rb_][q]])
            if debug == "ssd_a":
                raise _Stop()
            for c in range(2):
                pt, ptb = psum_bf()
                for q in range(3):
                    trp(pt[:, q * 128:(q + 1) * 128], xbc[rb_][q][:, c * 128:(c + 1) * 128], IDNb, [xbc_tb[rb_][q], cstb_tb], [ptb])
                if debug != "ssd_b1":
                    ts("dve", xtok3[:, c, :], pt[:, 0:256], 1.0, None, ALU.mult, None, [ptb], [xtok_tb[c]])
                if debug not in ("ssd_b1", "ssd_b2"):
                    ts("dve", btok3[:, c, :], pt[:, 256:384], 1.0, None, ALU.mult, None, [ptb], [btok_tb[c]])
            if debug in ("ssd_b", "ssd_b1", "ssd_b2"):
                raise _Stop()
            hgf = hTf[:, g * 256:(g + 1) * 256]
            hgb = hTb[:, g * 256:(g + 1) * 256]
            hgf3 = hgf.rearrange("p (h q) -> p h q", q=64)
            hgb3 = hgb.rearrange("p (h q) -> p h q", q=64)

            def versions(c, which):
                rv = cnt["v"] % NV
                cnt["v"] += 1
                x3 = xtok3[:, c, :].rearrange("p (h q) -> p h q", q=64)
                srcs = {0: (dtt, c * 64 + 4 * g), 1: (dtt, c * 64 + 32 + 4 * g), 2: (dtw, c * 64 + 4 * g),
                        3: (dtw, c * 64 + 32 + 4 * g), 4: (misc, 128 + 4 * g)}
                for v in which:
                    t_, col = srcs[v]
                    eng = "dve"
                    tt(eng, ver[rv][v][:].rearrange("p (h q) -> p h q", q=64), x3, bc4(t_, col), ALU.mult,
                       [xtok_tb[c], dt_tb[c], misc_tb], [ver_tb[rv][v]])
                return rv

            def state_update(c, d, rv):
                pt, ptb = psum()
                mm(pt[:, 0:256], btok3[:, c, :], ver[rv][2 + d][:], True, True, [btok_tb[c], ver_tb[rv][2 + d]], [ptb])
                h3 = hgf3 if d == 0 else hgb3
                h2 = hgf if d == 0 else hgb
                htb = hTf_tb[g] if d == 0 else hTb_tb[g]
                dec = bc4(eall, c * 192 + 128 + d * 32 + 4 * g)
                tt("dve", h3, h3, dec, ALU.mult, [htb, dt_tb[c]], [htb])
                tt("dve", h2, h2, pt[:, 0:256], ALU.add, [htb, ptb], [htb])

            if not full:
                if debug == "ssd_c":
                    versions(0, [2])
                    raise _Stop()
                if isctx:
                    for c in range(2):
                        rv = versions(c, [2])
                        state_update(c, 0, rv)
                for c in (1, 0):
                    rv = versions(c, [3])
                    state_update(c, 1, rv)
                continue

            for c in (1, 0):
                cp("act", hsb3[:, c, :], hgb, [hTb_tb[g]], [hsb_tb[c]])
                if c == 1:
                    rvb = versions(c, [3])
                    state_update(c, 1, rvb)
            chk("fa")
            for c in range(2):
                rc = cnt["c"] % 2
                cnt["c"] += 1
                rv = versions(c, [0, 1, 2, 4])
                cp("act", hsf[rc][:], hgf, [hTf_tb[g]], [hsf_tb[rc]])
                pz, pzb = psum()
                zc0 = (g % 2) * 256
                for kc in range(8):
                    mm(pz[:, 0:256], hT3[:, kc, c * 128:(c + 1) * 128], wz[:, kc, zc0:zc0 + 256], kc == 0, kc == 7,
                       [hT_tb[c], wz_tb], [pzb])
                pc, pcb = psum()
                mm(pc[:, 0:128], xbc[rb_][2][:, c * 128:(c + 1) * 128], xbc[rb_][3][:, c * 128:(c + 1) * 128], True, True,
                   [xbc_tb[rb_][2], xbc_tb[rb_][3]], [pcb])
                pi_, pib = psum()
                ct = xbc[rb_][3][:, c * 128:(c + 1) * 128]
                mm(pi_[:, 0:256], ct, hsf[rc][:], True, True, [xbc_tb[rb_][3], hsf_tb[rc]], [pib])
                mm(pi_[:, 256:512], ct, hsb3[:, c, :], True, True, [xbc_tb[rb_][3], hsb_tb[c]], [pib])
                segs = []
                for d in range(2):
                    msk = LEb if d == 0 else GEb
                    lh = GTb if d == 0 else LTb
                    col = c * 64 + d * 32 + 4 * g
                    tt("dve", Rt[rc][d][:].rearrange("p (h i) -> p h i", h=4),
                       dtA[:, col:col + 4].unsqueeze(2).to_broadcast([128, 4, 128]),
                       msk.unsqueeze(1).to_broadcast([128, 4, 128]), ALU.mult, [dt_tb[c], cstb_tb], [Rt_tb[rc][d]])
                    pt, ptb = psum()
                    mm(pt[:, 0:512], lh, Rt[rc][d][:], True, True, [Rt_tb[rc][d], cstb_tb], [ptb])
                    segs.append((pt, ptb))
                act(tz[rc][:], pz[:, 0:256], AF.Tanh, [pzb], [tz_tb[rc]], scale=0.5)
                for d in range(2):
                    act(Lt[rc][d][:], segs[d][0][:, 0:512], AF.Exp, [segs[d][1]], [Lt_tb[rc][d]])
                tt("dve", CBm[rc][0][:], pc[:, 0:128], LEb, ALU.mult, [pcb, cstb_tb], [CBm_tb[rc][0]])
                tt("dve", CBm[rc][1][:], pc[:, 0:128], GEb, ALU.mult, [pcb, cstb_tb], [CBm_tb[rc][1]])
                stt("dve", szt[rc][:], tz[rc][:], 1.0, pz[:, 0:256], ALU.add, ALU.mult, [tz_tb[rc], pzb], [sz_tb[rc]])
                for d in range(2):
                    tt("dve", t12[rc][d][:].rearrange("p (h q) -> p h q", q=64),
                       pi_[:, d * 256:(d + 1) * 256].rearrange("p (h q) -> p h q", q=64),
                       bc4(eall, c * 192 + d * 32 + 4 * g), ALU.mult, [pib, dt_tb[c]], [t12_tb[rc][d]])
                tt("dve", t12[rc][0][:], t12[rc][0][:], t12[rc][1][:], ALU.add, [t12_tb[rc][0], t12_tb[rc][1]], [t12_tb[rc][0]])
                for d in range(2):
                    tt("dve", Mt[rc][d][:].rearrange("p (h i) -> p h i", h=4), Lt[rc][d][:].rearrange("p (h i) -> p h i", h=4),
                       CBm[rc][d][:].unsqueeze(1).to_broadcast([128, 4, 128]), ALU.mult,
                       [Lt_tb[rc][d], CBm_tb[rc][d]], [Mt_tb[rc][d]])
                py, pyb = psum()
                for h in range(4):
                    hs = slice(h * 64, (h + 1) * 64)
                    ms = slice(h * 128, (h + 1) * 128)
                    mm(py[:, hs], Mt[rc][0][:, ms], ver[rv][0][:, hs], True, False, [Mt_tb[rc][0], ver_tb[rv][0]], [pyb])
                    mm(py[:, hs], Mt[rc][1][:, ms], ver[rv][1][:, hs], False, False, [Mt_tb[rc][1], ver_tb[rv][1]], [pyb])
                    mm(py[:, hs], IDNb, ver[rv][4][:, hs], False, True, [cstb_tb, ver_tb[rv][4]], [pyb])
                state_update(c, 0, rv)
                tt("dve", yy[rc][:], py[:, 0:256], t12[rc][0][:], ALU.add, [pyb, t12_tb[rc][0]], [yy_tb[rc]])
                tt("dve", ut[rc][:], yy[rc][:], szt[rc][:], ALU.mult, [yy_tb[rc], sz_tb[rc]], [ut_tb[rc]])
                act(junk[:, 0:256], ut[rc][:], AF.Square, [ut_tb[rc]], [junk_tb, ssq_tb[rc]], accum=ssq[:, 2 * rc:2 * rc + 1])
                ts("dve", ssq[:, 2 * rc + 1:2 * rc + 2], ssq[:, 2 * rc:2 * rc + 1], 1024.0 * 1e-5, None, ALU.add, None,
                   [ssq_tb[rc]], [ssq_tb[rc]])
                act(ssq[:, 2 * rc + 1:2 * rc + 2], ssq[:, 2 * rc + 1:2 * rc + 2], AF.Ln, [ssq_tb[rc]], [ssq_tb[rc]])
                act(ssq[:, 2 * rc + 1:2 * rc + 2], ssq[:, 2 * rc + 1:2 * rc + 2], AF.Exp, [ssq_tb[rc]], [ssq_tb[rc]], scale=-0.5)
                ts("dve", unb[rc][:], ut[rc][:], ssq[:, 2 * rc + 1:2 * rc + 2], 16.0, ALU.mult, ALU.mult,
                   [ut_tb[rc], ssq_tb[rc]], [unb_tb[rc]])
                pt, ptb = psum_bf()
                for m in range(2):
                    trp(pt[:, m * 128:(m + 1) * 128], unb[rc][:, m * 128:(m + 1) * 128], IDNb, [unb_tb[rc], cstb_tb], [ptb])
                for m in range(2):
                    k = 2 * g + m
                    ts("dve", unT3[:, k, c * 128:(c + 1) * 128], pt[:, m * 128:(m + 1) * 128],
                       pp[:, PP_NW + k:PP_NW + k + 1], None, ALU.mult, None, [ptb, pp_tb], [unT_tb[k][c]])
                chk("fj")
                if g == 0 and c == 1:
                    chk("fk")
                if g == 1 and c == 1:
                    chk("fl")
                if g == 3 and c == 1:
                    chk("fm")

        if debug == "ssd":
            raise _Stop()
        chk("ssd")
        dirs = [1] if kind == "bwd" else [0, 1]
        lst = {}
        X, X2, IN, TG = range(4)
        TR, A_, TI, NA2, S_, IU, B_, H_ = range(8)

        def stageA(k):
            ru = cnt["k"] % 2
            cnt["k"] += 1
            lst[k] = ru
            if k % 4 == 0:
                lst["wl"] = wload(wbf_in, 0, 8, O_LRU + k * 128, O_LRU + k * 128 + 512)
                if full:
                    lst["wg"] = wload(wbf_in, 0, 8, O_LG + k * 128, O_LG + k * 128 + 512)
            wl, wl_tb = lst["wl"]
            kk = (k % 4) * 128
            pt, ptb = psum()
            for kc in range(8):
                mm(pt[:, 0:TT], wl[:, kc, kk:kk + 128], hT3[:, kc, :], kc == 0, kc == 7, hT_tb + [wl_tb], [ptb])
            cp("act", usb[ru][:], pt[:, 0:TT], [ptb], [usb_tb[ru]])
            conv("dve", ru, usb[ru][:], lambda t_, k=k: pp[:, PP_LCW + k * 4 + t_:PP_LCW + k * 4 + t_ + 1],
                 pp[:, PP_LCB + k:PP_LCB + k + 1], L, [usb_tb[ru], pp_tb])
            cp("act", ubf[ru][:], cacc[ru][:], [cacc_tb[ru]], [ubf_tb[ru]])
            if full:
                wg, wg_tb = lst["wg"]
                pg, pgb = psum()
                for kc in range(8):
                    mm(pg[:, 0:TT], wg[:, kc, kk:kk + 128], hT3[:, kc, :], kc == 0, kc == 7, hT_tb + [wg_tb], [pgb])
                cp("act", gtmp[X][:], pg[:, 0:TT], [pgb], [gtmp_tb[X]])
                act(gtmp[X2][:], pg[:, 0:TT], AF.Square, [pgb], [gtmp_tb[X2]])
                ts("dve", gtmp[X2][:], gtmp[X2][:], 0.044715, 1.0, ALU.mult, ALU.add, [gtmp_tb[X2]], [gtmp_tb[X2]])
                tt("dve", gtmp[IN][:], gtmp[X2][:], gtmp[X][:], ALU.mult, [gtmp_tb[X2], gtmp_tb[X]], [gtmp_tb[IN]])
                act(gtmp[TG][:], gtmp[IN][:], AF.Tanh, [gtmp_tb[IN]], [gtmp_tb[TG]], scale=0.7978845608028654)
                stt("dve", ylg3[:, k, :], gtmp[TG][:], 1.0, gtmp[X][:], ALU.add, ALU.mult, [gtmp_tb[TG], gtmp_tb[X]], [ylg_tb[k]])

        def stageB(k):
            ru = lst[k]
            uu = cacc[ru]
            uu_tb = cacc_tb[ru]
            for d in dirs:
                pr, prb = psum()
                mm(pr[:, 0:TT], lruw3[:, (d * 2 + 0) * 8 + k, :], ubf[ru][:], True, True, [lruw_tb, ubf_tb[ru]], [prb])
                mm(pr[:, 256:256 + TT], lruw3[:, (d * 2 + 1) * 8 + k, :], ubf[ru][:], True, True, [lruw_tb, ubf_tb[ru]], [prb])
                idx = d * 8 + k
                act(ltmp[TR][:], pr[:, 0:TT], AF.Tanh, [prb, der_tb], [ltmp_tb[TR]], scale=0.5, bias=der[:, D_HBA + idx:D_HBA + idx + 1])
                aout = ltmp[A_][:] if d == 0 else ltmp[A_][:, ::-1]
                act(aout, ltmp[TR][:], AF.Exp, [ltmp_tb[TR], der_tb], [ltmp_tb[A_]],
                    scale=der[:, D_C1H + idx:D_C1H + idx + 1], bias=der[:, D_C1H + idx:D_C1H + idx + 1])
                act(ltmp[TI][:], pr[:, 256:256 + TT], AF.Tanh, [prb, der_tb], [ltmp_tb[TI]], scale=0.5,
                    bias=der[:, D_HBI + idx:D_HBI + idx + 1])
                stt("dve", ltmp[NA2][:], ltmp[A_][:], -1.0, ltmp[A_][:], ALU.mult, ALU.mult, [ltmp_tb[A_]], [ltmp_tb[NA2]])
                ts("dve", ltmp[S_][:], ltmp[NA2][:], 1.0, 1e-30, ALU.add, ALU.max, [ltmp_tb[NA2]], [ltmp_tb[S_]])
                act(ltmp[S_][:], ltmp[S_][:], AF.Ln, [ltmp_tb[S_]], [ltmp_tb[S_]])
                act(ltmp[S_][:], ltmp[S_][:], AF.Exp, [ltmp_tb[S_]], [ltmp_tb[S_]], scale=0.5)
                stt_p(ltmp[IU][:], ltmp[TI][:], 1.0, uu[:], ALU.add, ALU.mult, [ltmp_tb[TI], uu_tb], [ltmp_tb[IU]])
                iu_in = ltmp[IU][:] if d == 0 else ltmp[IU][:, ::-1]
                stt("dve", ltmp[B_][:], ltmp[S_][:], 0.5, iu_in, ALU.mult, ALU.mult, [ltmp_tb[S_], ltmp_tb[IU]], [ltmp_tb[B_]])
                if d == 0 or not full:
                    init = lruc[:, idx:idx + 1]
                    init_tb = lruc_tb[idx]
                else:
                    init = lrub[:, ti * 8 + k:ti * 8 + k + 1]
                    init_tb = lrub_tb
                hout = hfw if (d == 0 and full) else ltmp[H_]
                hout_tb = hfw_tb if (d == 0 and full) else ltmp_tb[H_]
                P.add("dve", lambda e, hout=hout, init=init: e.tensor_tensor_scan(
                    out=hout[:], data0=ltmp[A_][:], data1=ltmp[B_][:], initial=init, op0=ALU.mult, op1=ALU.add),
                    [ltmp_tb[A_], ltmp_tb[B_], init_tb], [hout_tb])
                if d == 0 or not full:
                    cp("pool", lruc[:, idx:idx + 1], hout[:, TT - 1:TT], [hout_tb], [lruc_tb[idx]])
                if full and d == 1:
                    tt("dve", ysum[:], hfw[:], ltmp[H_][:, ::-1], ALU.add, [hfw_tb, ltmp_tb[H_]], [ysum_tb])
            if full:
                stt("dve", ylg3[:, k, :], ysum[:], 0.5, ylg3[:, k, :], ALU.mult, ALU.mult, [ysum_tb, ylg_tb[k]], [ylg_tb[k]])

        stageA(0)
        for k in range(8):
            if k < 7:
                stageA(k + 1)
            stageB(k)
        chk("lru")
        if not full:
            return

        all_unT = [unT_tb[k][c] for k in range(16) for c in range(2)]
        for dq in range(8):
            if dq % 2 == 0:
                wbs, wbs_tb = wload(wbf_bs, 0, 16, dq * 128, dq * 128 + 256)
                (wgs, wgl), wgs_tb = wload_multi([(wbf_in, 0, 8, O_GS + dq * 128, O_GS + dq * 128 + 256),
                                                  (wbf_in, 0, 8, O_GL + dq * 128, O_GL + dq * 128 + 256)])
                wgl_tb = wgs_tb
                wbl, wbl_tb = wload(wbf_bl, 0, 8, dq * 128, dq * 128 + 256)
            pb, pbb = psum()
            o2 = (dq % 2) * 128
            o4 = (dq % 4) * 128
            for kc in range(16):
                mm(pb[:, 0:TT], wbs[:, kc, o2:o2 + 128], unT3[:, kc, :], kc == 0, kc == 15, all_unT + [wbs_tb], [pbb])
            for kc in range(8):
                mm(pb[:, 256:256 + TT], wbl[:, kc, o2:o2 + 128], ylg3[:, kc, :], kc == 0, kc == 7, ylg_tb + [wbl_tb], [pbb])
            pgt, pgtb = psum()
            for kc in range(8):
                mm(pgt[:, 0:TT], wgs[:, kc, o2:o2 + 128], hT3[:, kc, :], kc == 0, kc == 7, hT_tb + [wgs_tb], [pgtb])
            for kc in range(8):
                mm(pgt[:, 256:256 + TT], wgl[:, kc, o2:o2 + 128], hT3[:, kc, :], kc == 0, kc == 7, hT_tb + [wgl_tb], [pgtb])
            act(tg[0][:], pgt[:, 0:TT], AF.Tanh, [pgtb, der_tb], [tg_tb[0]], scale=0.5, bias=der[:, D_HBG + dq:D_HBG + dq + 1])
            act(tg[1][:], pgt[:, 256:256 + TT], AF.Tanh, [pgtb, der_tb], [tg_tb[1]], scale=0.5,
                bias=der[:, D_HBG + 8 + dq:D_HBG + 8 + dq + 1])
            stt("dve", m12[0][:], tg[0][:], 1.0, pb[:, 0:TT], ALU.add, ALU.mult, [tg_tb[0], pbb], [m12_tb[0]])
            stt("dve", m12[1][:], tg[1][:], 1.0, pb[:, 256:256 + TT], ALU.add, ALU.mult, [tg_tb[1], pbb], [m12_tb[1]])
            tt("dve", merged3[:, dq, :], m12[0][:], m12[1][:], ALU.add, [m12_tb[0], m12_tb[1]], [merged_tb[dq]])

        chk("s8")
        rot["n"] = 4
        wo = [wload(wbf_o, 0, 8, half * 512, (half + 1) * 512) for half in range(2)]
        for c in range(2):
            for half in range(2):
                pm, pmb = psum_t[4 + half], psum_tb[4 + half]
                for kc in range(8):
                    mm(pm[:, 0:512], merged3[:, kc, c * 128:(c + 1) * 128], wo[half][0][:, kc, :], kc == 0, kc == 7,
                       merged_tb + [wo[half][1]], [pmb])
                xsl = xs3[:, c, half * 512:(half + 1) * 512]
                tt("dve", rtmp[half][:], pm[:, 0:512], G[:, half * 512:(half + 1) * 512], ALU.mult, [pmb, G_tb], [rtmp_tb[half]])
                stt_p(xsl, xsl, ALPHA, rtmp[half][:], ALU.mult, ALU.add, [xs_tb[c], rtmp_tb[half]], [xs_tb[c]])
            x2 = xs3[:, c, :]
            ln_stats(c, x2, [xs_tb[c]])
            ts("dve", x2, x2, mv[:, 4 * c:4 * c + 1], mv[:, 4 * c + 2:4 * c + 3], ALU.subtract, ALU.mult, [xs_tb[c], st_tb[c]], [xs_tb[c]])
            tt("dve", x2, x2, lnt[:, 0:1024], ALU.mult, [xs_tb[c], lnt_tb], [xs_tb[c]])
            tt("dve", x2, x2, lnt[:, 1024:2048], ALU.add, [xs_tb[c], lnt_tb], [xs_tb[c]])
            to_hT(c, x2, [xs_tb[c]], h1T3, h1T_tb, 2, 3, r)

        chk("s9")
        for q in range(4):
            hq = hid[q % 2]
            hq3 = hq[:].rearrange("p (f t) -> p f t", f=8)
            for fq in range(8):
                f = 8 * q + fq
                if f % 4 == 0:
                    w1, w1_tb = wload(wbf_m1, 0, 8, f * 128, f * 128 + 512)
                o4 = (f % 4) * 128
                ph, phb = psum()
                for kc in range(8):
                    mm(ph[:, 0:TT], w1[:, kc, o4:o4 + 128], h1T3[:, kc, :], kc == 0, kc == 7, h1T_tb + [w1_tb], [phb])
                act(rr[f % 2][:], ph[:, 0:TT], AF.Relu, [phb, pp_tb], [rr_tb[f % 2]], bias=pp[:, PP_B1 + f:PP_B1 + f + 1])
                if f % 2 == 0:
                    act(hq3[:, fq, :], rr[f % 2][:], AF.Square, [rr_tb[f % 2]], [hid_tb[q % 2][fq]])
                else:
                    tt("dve", hq3[:, fq, :], rr[f % 2][:], rr[f % 2][:], ALU.mult, [rr_tb[f % 2]], [hid_tb[q % 2][fq]])
            for half in range(2):
                w2, w2_tb = wload(wbf_m2, 8 * q, 8 * q + 8, half * 512, (half + 1) * 512)
                for c in range(2):
                    pa, pab = psum_t[4 + 2 * c + half], psum_tb[4 + 2 * c + half]
                    for fq in range(8):
                        mm(pa[:, 0:512], hq3[:, fq, c * 128:(c + 1) * 128], w2[:, fq, :], q == 0 and fq == 0, False,
                           [hid_tb[q % 2][fq], w2_tb], [pab])
        for c in range(2):
            for half in range(2):
                pa, pab = psum_t[4 + 2 * c + half], psum_tb[4 + 2 * c + half]
                mm(pa[:, 0:512], ONESb[0:1, :], b2row[0:1, half * 512:(half + 1) * 512], False, True, [cstb_tb, b2row_tb], [pab])
                xsl = xs3[:, c, half * 512:(half + 1) * 512]
                tt("dve", rtmp[half][:], pa[:, 0:512], G[:, 1024 + half * 512:1024 + (half + 1) * 512], ALU.mult,
                   [pab, G_tb], [rtmp_tb[half]])
                stt_p(xsl, xsl, ALPHA, rtmp[half][:], ALU.mult, ALU.add, [xs_tb[c], rtmp_tb[half]], [xs_tb[c]])
            x2 = xs3[:, c, :]
            ln_stats(c, x2, [xs_tb[c]])
            ts("dve", x2, x2, mv[:, 4 * c:4 * c + 1], mv[:, 4 * c + 2:4 * c + 3], ALU.subtract, ALU.mult, [xs_tb[c], st_tb[c]], [xs_tb[c]])
            tt("dve", x2, x2, lnt[:, 2048:3072], ALU.mult, [xs_tb[c], lnt_tb], [xs_tb[c]])
            tt("dve", x2, x2, lnt[:, 3072:4096], ALU.add, [xs_tb[c], lnt_tb], [xs_tb[c]])
            P.dma("pool", outd[b, ti * TT + c * 128: ti * TT + (c + 1) * 128, :], x2, reads=[xs_tb[c]], writes=[TB()], is_output=True)

    hbb_tbs = [TB(f"hbb{i}") for i in range(NTILE)]
    try:
        for b in range(NB):
            for gi in range(2):
                r0 = (b * 2 + gi) * 128
                P.dma("sp", G[:, gi * 1024:(gi + 1) * 1024], gsc[r0:r0 + 128, :], writes=[G_tb])
            memset("pool", hTf[:], 0.0, hTf_tb)
            memset("pool", hTb[:], 0.0, hTb_tb)
            memset("pool", lruc[:], 0.0, lruc_tb)
            tile(b, "ctx", 0)
            if debug in ("ctx", "s1", "dt", "ssd", "ssd_a", "ssd_b", "ssd_c", "ssd_b1", "ssd_b2"):
                raise _Stop()
            for ti in reversed(range(NTILE)):
                tile(b, "bwd", ti)
            if debug == "bwdall":
                raise _Stop()
            for ti in range(NTILE):
                tile(b, "full", ti)
    except _Stop:
        pass
    if debug:
        P.dma("sp", dbg["hTf"][:, :], hTf[:], reads=hTf_tb, writes=[TB()], is_output=True)
        P.dma("sp", dbg["hTb"][:, :], hTb[:], reads=hTb_tb, writes=[TB()], is_output=True)
        P.dma("sp", dbg["lruc"][:, :], lruc[:], reads=lruc_tb, writes=[TB()], is_output=True)
        P.dma("sp", dbg["hT"][:, :], hT[:], reads=hT_tb, writes=[TB()], is_output=True)
        P.dma("sp", dbg["xs"][:, :], xs[:], reads=xs_tb, writes=[TB()], is_output=True)
        P.dma("sp", dbg["unT"][:, :], unT[:], reads=[unT_tb[k][c] for k in range(16) for c in range(2)], writes=[TB()], is_output=True)
        P.dma("sp", dbg["ylg"][:, :], ylg[:], reads=ylg_tb, writes=[TB()], is_output=True)
        P.dma("sp", dbg["merged"][:, :], merged[:], reads=merged_tb, writes=[TB()], is_output=True)
        for q in range(4):
            P.dma("sp", dbg["xbc"][:, q * TT:(q + 1) * TT], xbc[1][q][:], reads=[xbc_tb[1][q]], writes=[TB()], is_output=True)
        P.dma("sp", dbg["dtt"][:, :], dtt[:], reads=dt_tb, writes=[TB()], is_output=True)
        P.dma("sp", dbg["eall"][:, :], eall[:], reads=dt_tb, writes=[TB()], is_output=True)
        P.dma("sp", dbg["xtok"][:, :], xtok[:], reads=xtok_tb, writes=[TB()], is_output=True)
        P.dma("sp", dbg["btok"][:, :], btok[:], reads=btok_tb, writes=[TB()], is_output=True)
    P.finalize()
    return nc

_CACHE = {}


def _consts():
    k = np.arange(128)[:, None]
    i = np.arange(128)[None, :]
    mats = [(k <= i), (k > i), (k >= i), (k < i), np.ones((128, 128), bool), (k == i)]
    return np.concatenate([m.astype(np.float32) for m in mats], axis=1)


def _kmajor(w):
    K, N = w.shape
    return np.ascontiguousarray(w.reshape(K // 128, 128, N).transpose(1, 0, 2).reshape(128, (K // 128) * N))


def _cols128(v):
    return np.ascontiguousarray(v.reshape(-1, 128).T)


def prep_shared(inp):
    f32 = np.float32
    w_in = np.asarray(inp["w_in"][0], f32)
    cols = []
    for g in range(8):
        cols += list(range(256 * g, 256 * g + 256))
        cols += list(range(2048 + 128 * g, 2048 + 128 * g + 128))
        cols += list(range(4160 + 128 * g, 4160 + 128 * g + 128))
    cols += list(range(5184, 7232)) + list(range(3072, 3136)) + list(range(3136, 4160))
    cols += list(range(7232, 8256)) + list(range(8256, 10304))
    cols = np.asarray(cols)
    assert cols.shape[0] == 10304 and np.unique(cols).shape[0] == 10304
    lw = np.stack([np.asarray(inp["lru_wa"][0], f32), np.asarray(inp["lru_wi"][0], f32)], axis=1)
    lw = np.ascontiguousarray(lw.transpose(3, 0, 1, 2, 4).reshape(128, 32 * 128))
    wall = np.concatenate([
        _kmajor(w_in[:, cols]), _kmajor(np.asarray(inp["w_br_ssd"][0], f32)), _kmajor(np.asarray(inp["w_br_lru"][0], f32)),
        _kmajor(np.asarray(inp["w_out"][0], f32)), _kmajor(np.asarray(inp["w_mlp1"][0], f32)),
        _kmajor(np.asarray(inp["w_mlp2"][0], f32)), lw], axis=1)
    assert wall.shape == (128, W_TOT)
    cw = np.asarray(inp["ssd_conv_w"][0], f32)
    cb = np.asarray(inp["ssd_conv_b"][0], f32)
    pp = np.zeros((128, NPP), f32)
    for g in range(8):
        for q, ch0 in enumerate([256 * g, 256 * g + 128, 2048 + 128 * g, 3072 + 128 * g]):
            ccn = 4 * g + q
            pp[:, PP_CW + ccn * 4:PP_CW + ccn * 4 + 4] = cw[:, ch0:ch0 + 128].T
            pp[:, PP_CB + ccn] = cb[ch0:ch0 + 128]
    lcw = np.asarray(inp["lru_conv_w"][0], f32)
    for k in range(8):
        pp[:, PP_LCW + k * 4:PP_LCW + k * 4 + 4] = lcw[:, 128 * k:128 * k + 128].T
    pp[:, PP_LCB:PP_LCB + 8] = _cols128(np.asarray(inp["lru_conv_b"][0], f32))
    pp[:, PP_BA:PP_BA + 16] = _cols128(np.asarray(inp["lru_ba"][0], f32).reshape(-1))
    pp[:, PP_BI:PP_BI + 16] = _cols128(np.asarray(inp["lru_bi"][0], f32).reshape(-1))
    pp[:, PP_LAM:PP_LAM + 16] = _cols128(np.asarray(inp["lru_lambda"][0], f32).reshape(-1))
    pp[:, PP_BG:PP_BG + 16] = _cols128(np.asarray(inp["b_gate"][0], f32))
    pp[:, PP_NW:PP_NW + 16] = _cols128(np.asarray(inp["ssd_norm_w"][0], f32))
    pp[:, PP_B1:PP_B1 + 32] = _cols128(np.asarray(inp["b_mlp1"][0], f32))
    bm = np.asarray(inp["b_mod"][0], f32)
    pp[:, PP_BM:PP_BM + 48] = _cols128(bm)
    rb = np.concatenate([np.asarray(inp[k][0], f32).reshape(-1) for k in ("ln1_g", "ln1_b", "ln2_g", "ln2_b", "b_mlp2")]
                        + [bm[2048:3072], bm[5120:6144], np.asarray(inp["ssd_dt_bias"][0], f32).reshape(-1),
                           np.asarray(inp["ssd_a_log"][0], f32).reshape(-1), np.asarray(inp["ssd_d"][0], f32).reshape(-1)])
    assert rb.shape[0] == NRB
    wmod = _kmajor(np.asarray(inp["w_mod"][0], f32))
    return dict(wall=wall, pp=pp, rb=np.ascontiguousarray(rb), wmod=wmod, consts=_consts())


def core_inputs(inp, shared, b0, NB):
    f32 = np.float32
    cc = np.zeros((5, 1024), f32)
    cc[:NB] = np.asarray(inp["c"], f32)[b0:b0 + NB]
    cc[4] = np.asarray(inp["c_ctx"], f32)
    ccT = np.ascontiguousarray(cc.reshape(5, 8, 128).transpose(2, 1, 0).reshape(128, 40))
    d = dict(shared)
    d["xin"] = np.ascontiguousarray(np.asarray(inp["x"], f32)[b0:b0 + NB])
    d["ctxin"] = np.ascontiguousarray(np.asarray(inp["ctx"], f32)[b0:b0 + NB])
    d["ccT"] = ccT
    return d


def kernel(**inputs):
    NB = 4
    if "nc" not in _CACHE:
        _CACHE["nc"] = build_program(NB)
    nc = _CACHE["nc"]
    shared = prep_shared(inputs)
    in_maps = [core_inputs(inputs, shared, 4 * i, NB) for i in range(8)]
    res = run_bass_kernel_spmd(nc, in_maps, core_ids=list(range(8)))
    return np.concatenate([r["out"] for r in res.results], axis=0)
```

```python
import numpy as np
from concourse.bass_utils import run_bass_kernel_spmd
import concourse.bass as bass
import concourse.mybir as mybir

F32 = mybir.dt.float32
BF16 = mybir.dt.bfloat16
AF = mybir.ActivationFunctionType
ALU = mybir.AluOpType
AX = mybir.AxisListType


class TB:
    __slots__ = ("w", "rs", "name")

    def __init__(self, name=""):
        self.w = None
        self.rs = []
        self.name = name


class Op:
    __slots__ = ("eng", "fn", "deps", "idx", "eidx", "signal", "tok", "isdma")


COMPUTE = ("pe", "act", "dve", "pool")
ENGS = ("pe", "act", "dve", "pool", "sp")
EPOCH = 12000
NDMASEM = 12


class Prog:
    def __init__(self, nc, sem_stack):
        self.nc = nc
        self.ops = []
        self.ecount = {e: 0 for e in ENGS}
        self.last = {e: None for e in ENGS}
        self.sem_stack = sem_stack
        self.esems = {e: [] for e in COMPUTE}
        self.dsems = {}
        self.dma_count = {}
        self.dma_semval = {}
        self.dma_prev = {}
        self.seen = {e: {} for e in ENGS}
        self.tickets = {e: 0 for e in COMPUTE}
        self.emitted = 0
        self.out_tokens = []

    def _sem(self, name):
        return self.sem_stack.enter_context(self.nc.semaphore(name))

    def add(self, eng, fn, reads=(), writes=(), dma=False):
        op = Op()
        op.eng = eng
        op.fn = fn
        op.isdma = dma
        op.idx = len(self.ops)
        op.eidx = self.ecount[eng]
        self.ecount[eng] += 1
        op.signal = False
        op.tok = None
        deps = {}
        for r in reads:
            if r.w is not None:
                deps[r.w.idx] = r.w
        for w in writes:
            if w.w is not None:
                deps[w.w.idx] = w.w
            for o in w.rs:
                deps[o.idx] = o
        deps.pop(op.idx, None)
        best = {}
        out = []
        for d in deps.values():
            if d.isdma:
                out.append(d)
            else:
                b = best.get(d.eng)
                if b is None or d.idx > b.idx:
                    best[d.eng] = d
        for e, d in best.items():
            if e == eng and not dma:
                if eng == "pe":
                    continue
                if op.eidx - d.eidx > 1:
                    continue
            out.append(d)
        for d in out:
            d.signal = True
        op.deps = out
        for r in reads:
            r.rs.append(op)
        for w in writes:
            w.w = op
            w.rs = []
        self.ops.append(op)
        self.last[eng] = op
        return op

    def dma(self, eng, out, in_, reads=(), writes=(), is_output=False):
        op = self.add(eng, lambda e: e.dma_start(out=out, in_=in_), reads, writes, dma=True)
        op.signal = True
        if is_output:
            self.out_tokens.append(op)
        return op

    def barrier(self):
        lasts = [self.last[e] for e in ENGS if self.last[e] is not None]
        dmas = [o for o in self.ops[self.emitted:] if o.isdma]
        for e in ENGS:
            op = Op()
            op.eng = e
            op.fn = None
            op.isdma = False
            op.idx = len(self.ops)
            op.eidx = self.ecount[e]
            op.signal = False
            op.tok = None
            op.deps = []
            for d in lasts + dmas:
                if d.eng == e and not d.isdma:
                    continue
                if d not in op.deps:
                    d.signal = True
                    op.deps.append(d)
            self.ops.append(op)

    def finalize(self):
        nc = self.nc
        ops = self.ops[self.emitted:]
        self.emitted = len(self.ops)
        for op in ops:
            if not op.signal:
                continue
            if op.isdma:
                e = op.eng
                if e not in self.dsems:
                    self.dsems[e] = [self._sem(f"dq_{e}_{i}") for i in range(NDMASEM)]
                    self.dma_count[e] = 0
                    self.dma_semval[e] = [0] * NDMASEM
                    self.dma_prev[e] = [None] * NDMASEM
                k = self.dma_count[e] % NDMASEM
                self.dma_count[e] += 1
                self.dma_semval[e][k] += 16
                op.tok = (self.dsems[e][k], self.dma_semval[e][k], k)
            else:
                e = op.eng
                t = self.tickets[e]
                self.tickets[e] += 1
                ep, v = divmod(t, EPOCH)
                while len(self.esems[e]) <= ep:
                    self.esems[e].append(self._sem(f"es_{e}_{len(self.esems[e])}"))
                op.tok = (self.esems[e][ep], v + 1, None)
        per = {e: [o for o in ops if o.eng == e] for e in ENGS}
        outs = list(self.out_tokens)
        self.out_tokens = []

        def emit(ename, eng):
            seen = self.seen[ename]

            def wait(tok):
                sem, val = tok[0], tok[1]
                key = id(sem)
                if seen.get(key, 0) >= val:
                    return
                eng.wait_ge(sem, val)
                seen[key] = val

            for op in per[ename]:
                for d in op.deps:
                    if d.tok is not None:
                        wait(d.tok)
                if op.isdma:
                    sem, val, k = op.tok
                    if val > 16:
                        wait((sem, val - 16))
                if op.fn is None:
                    continue
                inst = op.fn(eng)
                if op.signal:
                    inst.then_inc(op.tok[0], 16 if op.isdma else 1)
            if ename == "sp":
                for o in outs:
                    wait(o.tok)

        with nc.Block() as block:
            if per["pe"]:
                @block.tensor
                def _(e):
                    emit("pe", e)
            if per["act"]:
                @block.scalar
                def _(e):
                    emit("act", e)
            if per["dve"]:
                @block.vector
                def _(e):
                    emit("dve", e)
            if per["pool"]:
                @block.gpsimd
                def _(e):
                    emit("pool", e)
            if per["sp"] or outs:
                @block.sync
                def _(e):
                    emit("sp", e)

import numpy as np
from contextlib import ExitStack

D = 1024
SEQ = 2048
CTX = 256
TT = 256
NTILE = SEQ // TT
ALPHA = 2.0 ** 0.25
O_XBC, O_Z, O_DT, O_LRU, O_LG, O_GS, O_GL, NIN = 0, 4096, 6144, 6208, 7232, 8256, 9280, 10304
W_IN, W_BS, W_BL, W_O, W_M1, W_M2, W_LR, W_TOT = 0, 82432, 98816, 107008, 115200, 147968, 180736, 184832
CBLK = 4864
NCBLK = W_TOT // CBLK
PP_CW, PP_CB, PP_LCW, PP_LCB, PP_BA, PP_BI, PP_LAM, PP_BG, PP_NW, PP_B1, PP_BM, NPP = 0, 128, 160, 192, 200, 216, 232, 248, 264, 280, 312, 360
RB_L1G, RB_L1B, RB_L2G, RB_L2B, RB_B2, RB_BM2, RB_BM5, RB_DTB, RB_ALOG, RB_D, NRB = 0, 1024, 2048, 3072, 4096, 5120, 6144, 7168, 7232, 7296, 7328
NCONST = 768


class _Stop(Exception):
    pass


def build_program(NB, debug=None):
    nc = bass.Bass("TRN2", target_bir_lowering=False)
    es = ExitStack()
    with es:
        return _build(nc, es, NB, debug)


def _build(nc, es, NB, debug):
    xin = nc.dram_tensor("xin", [NB, SEQ, D], F32, kind="ExternalInput").ap()
    ctxin = nc.dram_tensor("ctxin", [NB, CTX, D], F32, kind="ExternalInput").ap()
    ccT = nc.dram_tensor("ccT", [128, 8 * 5], F32, kind="ExternalInput").ap()
    wmod = nc.dram_tensor("wmod", [128, 8 * 6144], F32, kind="ExternalInput").ap()
    wall = nc.dram_tensor("wall", [128, W_TOT], F32, kind="ExternalInput").ap()
    ppd = nc.dram_tensor("pp", [128, NPP], F32, kind="ExternalInput").ap()
    rbd = nc.dram_tensor("rb", [NRB], F32, kind="ExternalInput").ap()
    cstd = nc.dram_tensor("consts", [128, NCONST], F32, kind="ExternalInput").ap()
    outd = nc.dram_tensor("out", [NB, SEQ, D], F32, kind="ExternalOutput").ap()
    wbf = nc.dram_tensor("wbf", [128, W_TOT], BF16, kind="Internal").ap()
    gsc = nc.dram_tensor("gsc", [NB * 2 * 128, 1024], F32, kind="Internal").ap()
    hbb = nc.dram_tensor("hbb", [NTILE * 128, 2048], F32, kind="Internal").ap()

    P = Prog(nc, es)
    dbg = {}
    if debug:
        dbg["modF"] = nc.dram_tensor("d_modF", [128, 160], F32, kind="ExternalOutput").ap()
        dbg["gsc"] = nc.dram_tensor("d_gsc", [NB * 2 * 128, 1024], F32, kind="ExternalOutput").ap()
        dbg["hTf"] = nc.dram_tensor("d_hTf", [128, 2048], F32, kind="ExternalOutput").ap()
        dbg["hTb"] = nc.dram_tensor("d_hTb", [128, 2048], F32, kind="ExternalOutput").ap()
        dbg["lruc"] = nc.dram_tensor("d_lruc", [128, 16], F32, kind="ExternalOutput").ap()
        dbg["hT"] = nc.dram_tensor("d_hT", [128, 8 * TT], BF16, kind="ExternalOutput").ap()
        dbg["xs"] = nc.dram_tensor("d_xs", [128, 2048], F32, kind="ExternalOutput").ap()
        dbg["unT"] = nc.dram_tensor("d_unT", [128, 16 * TT], BF16, kind="ExternalOutput").ap()
        dbg["ylg"] = nc.dram_tensor("d_ylg", [128, 8 * TT], BF16, kind="ExternalOutput").ap()
        dbg["merged"] = nc.dram_tensor("d_merged", [128, 8 * TT], BF16, kind="ExternalOutput").ap()
        dbg["xbc"] = nc.dram_tensor("d_xbc", [128, 4 * TT], BF16, kind="ExternalOutput").ap()
        dbg["dtt"] = nc.dram_tensor("d_dtt", [128, 128], F32, kind="ExternalOutput").ap()
        dbg["eall"] = nc.dram_tensor("d_eall", [128, 384], F32, kind="ExternalOutput").ap()
        dbg["xtok"] = nc.dram_tensor("d_xtok", [128, 512], BF16, kind="ExternalOutput").ap()
        dbg["btok"] = nc.dram_tensor("d_btok", [128, 256], BF16, kind="ExternalOutput").ap()

    def sb(name, shape, dt=F32):
        return es.enter_context(nc.sbuf_tensor(name, shape, dt))

    def act(out, in_, func, reads, writes, bias=None, scale=None, accum=None):
        kw = {}
        if bias is not None:
            kw["bias"] = bias
        if scale is not None:
            kw["scale"] = scale
        if accum is not None:
            kw["accum_out"] = accum
        return P.add("act", lambda e: e.activation(out=out, in_=in_, func=func, **kw), reads, writes)

    def tt(eng, out, in0, in1, op, reads, writes):
        return P.add(eng, lambda e: e.tensor_tensor(out=out, in0=in0, in1=in1, op=op), reads, writes)

    def ts(eng, out, in0, s1, s2, op0, op1, reads, writes):
        if op1 is None:
            return P.add(eng, lambda e: e.tensor_scalar(out=out, in0=in0, scalar1=s1, scalar2=None, op0=op0), reads, writes)
        return P.add(eng, lambda e: e.tensor_scalar(out=out, in0=in0, scalar1=s1, scalar2=s2, op0=op0, op1=op1), reads, writes)

    def stt(eng, out, in0, scalar, in1, op0, op1, reads, writes):
        return P.add(eng, lambda e: e.scalar_tensor_tensor(out=out, in0=in0, scalar=scalar, in1=in1, op0=op0, op1=op1), reads, writes)

    def stt_p(out, in0, scalar, in1, op0, op1, reads, writes):
        return stt("dve", out, in0, scalar, in1, op0, op1, reads, writes)

    def cp(eng, out, in_, reads, writes):
        if eng == "act":
            return P.add("act", lambda e: e.activation(out=out, in_=in_, func=AF.Identity), reads, writes)
        return P.add(eng, lambda e: e.tensor_copy(out=out, in_=in_), reads, writes)

    def affine(eng, out, in_, scale_ap, bias_ap, reads, writes):
        if eng == "act":
            return act(out, in_, AF.Identity, reads, writes, bias=bias_ap, scale=scale_ap)
        return ts(eng, out, in_, scale_ap, bias_ap, ALU.mult, ALU.add, reads, writes)

    def mm(out, lhsT, rhs, start, stop, reads, writes):
        return P.add("pe", lambda e: e.matmul(out, lhsT, rhs, start=start, stop=stop), reads, writes)

    def trp(out, in_, ident, reads, writes):
        return P.add("pe", lambda e: e.transpose(out, in_, ident), reads, writes)

    def memset(eng, ap, val, writes):
        return P.add(eng, lambda e: e.memset(ap, val), (), writes)

    psum_t = [es.enter_context(nc.psum_tensor(f"ps{i}", [128, 512], F32)) for i in range(8)]
    psum_tb = [TB(f"ps{i}") for i in range(8)]
    rot = {"i": 0, "n": 8}

    def psum():
        i = rot["i"] % rot["n"]
        rot["i"] += 1
        return psum_t[i], psum_tb[i]

    def psum_bf():
        t, b = psum()
        return t[:].bitcast(BF16), b

    pp = sb("pp_s", [128, NPP])
    pp_tb = TB("pp")
    der = sb("der", [128, 256])
    der_tb = TB("der")
    D_HCW, D_HCB, D_HBA, D_HBI, D_C1H, D_HBG = 0, 128, 160, 176, 192, 208
    modF = sb("modF", [128, 4 * 8 * 5])
    modF_tb = TB("modF")
    lnt = sb("lnt", [128, 4 * 1024])
    lnt_tb = TB("lnt")
    misc = sb("misc", [128, 160])
    misc_tb = TB("misc")
    cst = sb("cst", [128, NCONST])
    cst_tb = TB("cst")
    cstb = sb("cstb", [128, NCONST], BF16)
    cstb_tb = TB("cstb")
    LE, GT, GE, LT, ONES, IDN = [cst[:, i * 128:(i + 1) * 128] for i in range(6)]
    LEb, GTb, GEb, LTb, ONESb, IDNb = [cstb[:, i * 128:(i + 1) * 128] for i in range(6)]
    lruw = sb("lruw", [128, 32 * 128], BF16)
    lruw_tb = TB("lruw")
    b2row = sb("b2row", [1, 1024], BF16)
    b2row_tb = TB("b2row")

    P.dma("sp", pp[:], ppd[:, :], writes=[pp_tb])
    P.dma("sp", cst[:], cstd[:, :], writes=[cst_tb])
    for i in range(4):
        P.dma("sp", lnt[:, i * 1024:(i + 1) * 1024], rbd[i * 1024:(i + 1) * 1024].partition_broadcast(128), writes=[lnt_tb])
    P.dma("sp", misc[:, 0:160], rbd[RB_DTB:RB_DTB + 160].partition_broadcast(128), writes=[misc_tb])
    cp("dve", cstb[:], cst[:], [cst_tb], [cstb_tb])
    act(misc[:, 64:128], misc[:, 64:128], AF.Exp, [misc_tb], [misc_tb])
    ts("dve", misc[:, 64:128], misc[:, 64:128], -1.0, None, ALU.mult, None, [misc_tb], [misc_tb])
    ts("dve", der[:, D_HCW:D_HCW + 160], pp[:, PP_CW:PP_CW + 160], 0.5, None, ALU.mult, None, [pp_tb], [der_tb])
    ts("dve", der[:, D_HBA:D_HBA + 32], pp[:, PP_BA:PP_BA + 32], 0.5, None, ALU.mult, None, [pp_tb], [der_tb])
    ts("dve", der[:, D_HBG:D_HBG + 16], pp[:, PP_BG:PP_BG + 16], 0.5, None, ALU.mult, None, [pp_tb], [der_tb])
    act(der[:, D_C1H:D_C1H + 16], pp[:, PP_LAM:PP_LAM + 16], AF.Exp, [pp_tb], [der_tb], scale=-1.0)
    act(der[:, D_C1H:D_C1H + 16], der[:, D_C1H:D_C1H + 16], AF.Ln, [der_tb], [der_tb], bias=1.0)
    ts("dve", der[:, D_C1H:D_C1H + 16], der[:, D_C1H:D_C1H + 16], -4.0, None, ALU.mult, None, [der_tb], [der_tb])

    wbf_tbs = [TB(f"wbf{i}") for i in range(NCBLK)]
    with ExitStack() as es0:
        def sb0(name, shape, dt=F32):
            return es0.enter_context(nc.sbuf_tensor(name, shape, dt))
        wf = [sb0(f"wf{i}", [128, CBLK]) for i in range(2)]
        wb = [sb0(f"wb{i}", [128, CBLK], BF16) for i in range(2)]
        wf_tb = [TB() for _ in range(2)]
        wb_tb = [[TB() for _ in range(3)] for _ in range(2)]
        cuts = [0, 1664, 3264, 4864]
        for i in range(NCBLK):
            u = i % 2
            P.dma("sp", wf[u][:], wall[:, i * CBLK:(i + 1) * CBLK], writes=[wf_tb[u]])
            for j, eng in enumerate(("act", "dve", "pool")):
                a, b = cuts[j], cuts[j + 1]
                cp(eng, wb[u][:, a:b], wf[u][:, a:b], [wf_tb[u]], [wb_tb[u][j]])
            P.dma("pool", wbf[:, i * CBLK:(i + 1) * CBLK], wb[u][:], reads=wb_tb[u], writes=[wbf_tbs[i]])

        cc = sb0("cc", [128, 40])
        sct = sb0("sct", [128, 40])
        scT = sb0("scT", [128, 40])
        cc_tb, sct_tb, scT_tb = TB(), TB(), TB()
        P.dma("sp", cc[:], ccT[:, :], writes=[cc_tb])
        act(sct[:], cc[:], AF.Tanh, [cc_tb], [sct_tb], scale=0.5)
        stt("dve", sct[:], sct[:], 1.0, cc[:], ALU.add, ALU.mult, [sct_tb, cc_tb], [sct_tb])
        ts("dve", scT[:], sct[:], 0.5, None, ALU.mult, None, [sct_tb], [scT_tb])
        scT3 = scT[:].rearrange("p (k r) -> p k r", r=5)
        scB = sb0("scB", [128, 8 * NB * 128])
        scB_tb = TB()
        for kc in range(8):
            for j in range(NB):
                o = (kc * NB + j) * 128
                cp("pool", scB[:, o:o + 128], scT3[:, kc, j:j + 1].to_broadcast([128, 128]), [scT_tb], [scB_tb])
        wm = [sb0(f"wm{i}", [128, 8 * 1024]) for i in range(2)]
        wm_tb = [TB() for _ in range(2)]
        rbg = sb0("rbg", [128, 3 * 1024])
        rbg_tb = TB()
        P.dma("sp", rbg[:], rbd[RB_B2:RB_B2 + 3072].partition_broadcast(128), writes=[rbg_tb])
        b2f = sb0("b2f", [1, 1024])
        b2f_tb = TB()
        P.dma("sp", b2f[:], rbd[RB_B2:RB_B2 + 1024].partition_broadcast(1), writes=[b2f_tb])
        cp("dve", b2row[:], b2f[:], [b2f_tb], [b2row_tb])
        gst = sb0("gst", [128, 2 * 1024])
        gst_tb = [TB(), TB()]
        wmod3 = wmod.rearrange("p (k n) -> p k n", k=8)
        modF4 = modF[:].rearrange("p (m q r) -> p m q r", m=4, q=8)
        fm_index = {0: 0, 1: 1, 3: 2, 4: 3}
        for bi, mi in enumerate(range(6)):
            u = bi % 2
            w3 = wm[u][:].rearrange("p (k n) -> p k n", k=8)
            P.dma("sp", w3, wmod3[:, :, mi * 1024:(mi + 1) * 1024], writes=[wm_tb[u]])
            if mi in fm_index:
                m = fm_index[mi]
                for dq in range(8):
                    pt, ptb = psum()
                    for kc in range(8):
                        mm(pt[:, 0:5], w3[:, kc, dq * 128:(dq + 1) * 128], scT3[:, kc, :], kc == 0, kc == 7, [wm_tb[u], scT_tb], [ptb])
                    col = PP_BM + mi * 8 + dq
                    ts("dve", modF4[:, m, dq, :], pt[:, 0:5], pp[:, col:col + 1], 1.0 if mi in (1, 4) else 0.0,
                       ALU.add, ALU.add, [ptb, pp_tb], [modF_tb])
            else:
                gi = 0 if mi == 2 else 1
                for j in range(NB):
                    for half in range(2):
                        pt, ptb = psum()
                        for kc in range(8):
                            o = (kc * NB + j) * 128
                            mm(pt[:, 0:512], scB[:, o:o + 128], w3[:, kc, half * 512:(half + 1) * 512], kc == 0, kc == 7,
                               [wm_tb[u], scB_tb], [ptb])
                        gs = gst[:, gi * 1024 + half * 512: gi * 1024 + (half + 1) * 512]
                        bo = (1 + gi) * 1024 + half * 512
                        tt("dve", gs, pt[:, 0:512], rbg[:, bo:bo + 512], ALU.add, [ptb, rbg_tb], [gst_tb[gi]])
                    if gi == 0:
                        ts("dve", gst[:, 0:1024], gst[:, 0:1024], 0.5, None, ALU.mult, None, [gst_tb[0]], [gst_tb[0]])
                    r0 = (j * 2 + gi) * 128
                    P.dma("pool", gsc[r0:r0 + 128, :], gst[:, gi * 1024:(gi + 1) * 1024], reads=[gst_tb[gi]], writes=[TB()])
        if debug:
            P.dma("sp", dbg["modF"][:, :], modF[:], reads=[modF_tb], writes=[TB()], is_output=True)
        P.barrier()
        P.finalize()
    if debug:
        P.dma("sp", dbg["gsc"][:, :], gsc[:, :], writes=[TB()], is_output=True)
    if debug == "p0":
        P.finalize()
        return nc

    G = sb("G", [128, 2 * 1024])
    G_tb = TB("G")
    hTf = sb("hTf", [128, 2048])
    hTb = sb("hTb", [128, 2048])
    hTf_tb = [TB(f"hTf{g}") for g in range(8)]
    hTb_tb = [TB(f"hTb{g}") for g in range(8)]
    lruc = sb("lruc", [128, 16])
    lruc_tb = [TB(f"lruc{i}") for i in range(16)]
    lrub = sb("lrub", [128, NTILE * 8])
    lrub_tb = TB("lrub")
    junk = sb("junk", [128, 256])
    junk_tb = TB("junk")
    xs = sb("xs", [128, 2 * 1024])
    xs_tb = [TB("xs0"), TB("xs1")]
    hT = sb("hT", [128, 8 * TT], BF16)
    hT_tb = [TB("hT0"), TB("hT1")]
    h1T = sb("h1T", [128, 8 * TT], BF16)
    h1T_tb = [TB("h1T0"), TB("h1T1")]
    NRING = 4
    ring = [sb(f"ring{i}", [128, 4096], BF16) for i in range(NRING)]
    ring_tb = [TB(f"ring{i}") for i in range(NRING)]
    ringi = {"i": 0}
    st6 = sb("st6", [128, 2 * 12])
    mv = sb("mv", [128, 2 * 4])
    st_tb = [TB(), TB()]
    _xn = sb("xn0", [128, 1024], BF16)
    xn = [_xn, _xn]
    _xntb = TB()
    xn_tb = [_xntb, _xntb]
    dtv = sb("dtv", [128, 2 * 64])
    dta = sb("dta", [128, 2 * 64])
    dtt = sb("dtt", [128, 2 * 64])
    dtA = sb("dtA", [128, 2 * 64])
    dtw = sb("dtw", [128, 2 * 64])
    eall = sb("eall", [128, 2 * 192])
    dt_tb = [TB("dt0"), TB("dt1")]
    usb = [sb(f"usb{i}", [128, TT]) for i in range(4)]
    cacc = [sb(f"cacc{i}", [128, TT]) for i in range(4)]
    cth = [sb(f"cth{i}", [128, TT]) for i in range(4)]
    usb_tb = [TB() for _ in range(4)]
    cacc_tb = [TB() for _ in range(4)]
    cth_tb = [TB() for _ in range(4)]
    xbc = [[sb(f"xbc{r}_{q}", [128, TT], BF16) for q in range(4)] for r in range(2)]
    xbc_tb = [[TB() for q in range(4)] for r in range(2)]
    xtok = sb("xtok", [128, 2 * 256], BF16)
    btok = sb("btok", [128, 2 * 128], BF16)
    xtok_tb = [TB(), TB()]
    btok_tb = [TB(), TB()]
    NV = 2
    ver = [[sb(f"ver{r}_{v}", [128, 256], BF16) for v in range(5)] for r in range(NV)]
    ver_tb = [[TB() for v in range(5)] for r in range(NV)]
    hsb = sb("hsb", [128, 2 * 256], BF16)
    hsb_tb = [TB(), TB()]
    hsf = [sb(f"hsf{i}", [128, 256], BF16) for i in range(2)]
    hsf_tb = [TB(), TB()]
    Rt = [[sb(f"R{r}_{d}", [128, 512], BF16) for d in range(2)] for r in range(2)]
    Rt_tb = [[TB() for d in range(2)] for r in range(2)]
    Lt = [[sb(f"L{r}_{d}", [128, 512], BF16) for d in range(2)] for r in range(2)]
    Lt_tb = [[TB() for d in range(2)] for r in range(2)]
    Mt = [[sb(f"M{r}_{d}", [128, 512], BF16) for d in range(2)] for r in range(2)]
    Mt_tb = [[TB() for d in range(2)] for r in range(2)]
    CBm = [[sb(f"CB{r}_{d}", [128, 128], BF16) for d in range(2)] for r in range(2)]
    CBm_tb = [[TB() for d in range(2)] for r in range(2)]
    t12 = [[sb(f"t12_{r}_{d}", [128, 256]) for d in range(2)] for r in range(2)]
    t12_tb = [[TB() for d in range(2)] for r in range(2)]
    yy = [sb(f"yy{r}", [128, 256]) for r in range(2)]
    yy_tb = [TB(), TB()]
    tz = [sb(f"tz{r}", [128, 256]) for r in range(2)]
    tz_tb = [TB(), TB()]
    szt = [sb(f"sz{r}", [128, 256]) for r in range(2)]
    sz_tb = [TB(), TB()]
    ut = [sb(f"ut{r}", [128, 256]) for r in range(2)]
    ut_tb = [TB(), TB()]
    ssq = sb("ssq", [128, 4])
    ssq_tb = [TB(), TB()]
    unb = [sb(f"unb{r}", [128, 256], BF16) for r in range(2)]
    unb_tb = [TB(), TB()]
    unT = sb("unT", [128, 16 * TT], BF16)
    unT_tb = [[TB() for c in range(2)] for k in range(16)]
    ubf = [sb(f"ubf_{r}", [128, TT], BF16) for r in range(2)]
    ubf_tb = [TB(), TB()]
    ltmp = [sb(f"ltmp{i}", [128, TT]) for i in range(8)]
    ltmp_tb = [TB() for _ in range(8)]
    hfw = sb("hfw", [128, TT])
    hfw_tb = TB()
    ysum = sb("ysum", [128, TT])
    ysum_tb = TB()
    gtmp = [sb(f"gtmp{i}", [128, TT]) for i in range(4)]
    gtmp_tb = [TB() for _ in range(4)]
    ylg = sb("ylg", [128, 8 * TT], BF16)
    ylg_tb = [TB() for _ in range(8)]
    merged = sb("merged", [128, 8 * TT], BF16)
    merged_tb = [TB() for _ in range(8)]
    tg = [gtmp[0], gtmp[1]]
    tg_tb = [gtmp_tb[0], gtmp_tb[1]]
    m12 = [gtmp[2], gtmp[3]]
    m12_tb = [gtmp_tb[2], gtmp_tb[3]]
    _rt = sb("rtmp0", [128, 512])
    _rtb = TB()
    rtmp = [_rt, _rt]
    rtmp_tb = [_rtb, _rtb]
    hid = [sb(f"hid{i}", [128, 8 * TT], BF16) for i in range(2)]
    hid_tb = [[TB() for f in range(8)] for i in range(2)]
    rr = [sb(f"rr{i}", [128, TT], BF16) for i in range(2)]
    rr_tb = [TB(), TB()]

    print("SBUF bytes remaining:", nc.sbuf_bytes_remaining)
    wbf_in = wbf[:, W_IN:W_IN + 8 * NIN].rearrange("p (k n) -> p k n", k=8)
    wbf_bs = wbf[:, W_BS:W_BS + 16 * 1024].rearrange("p (k n) -> p k n", k=16)
    wbf_bl = wbf[:, W_BL:W_BL + 8 * 1024].rearrange("p (k n) -> p k n", k=8)
    wbf_o = wbf[:, W_O:W_O + 8 * 1024].rearrange("p (k n) -> p k n", k=8)
    wbf_m1 = wbf[:, W_M1:W_M1 + 8 * 4096].rearrange("p (k n) -> p k n", k=8)
    wbf_m2 = wbf[:, W_M2:W_M2 + 32 * 1024].rearrange("p (k n) -> p k n", k=32)

    def wload(dram3, k0, k1, c0, c1):
        i = ringi["i"] % NRING
        ringi["i"] += 1
        nk, ncol = k1 - k0, c1 - c0
        assert nk * ncol <= 4096
        v = ring[i][:, 0:nk * ncol].rearrange("p (k n) -> p k n", k=nk)
        P.dma("sp", v, dram3[:, k0:k1, c0:c1], reads=wbf_tbs, writes=[ring_tb[i]])
        return v, ring_tb[i]

    def wload_multi(specs):
        i = ringi["i"] % NRING
        ringi["i"] += 1
        off = 0
        views = []
        for (dram3, k0, k1, c0, c1) in specs:
            nk, ncol = k1 - k0, c1 - c0
            v = ring[i][:, off:off + nk * ncol].rearrange("p (k n) -> p k n", k=nk)
            off += nk * ncol
            assert off <= 4096
            P.dma("sp", v, dram3[:, k0:k1, c0:c1], reads=wbf_tbs, writes=[ring_tb[i]])
            views.append(v)
        return views, ring_tb[i]

    P.dma("sp", lruw[:], wbf[:, W_LR:W_LR + 4096], reads=wbf_tbs, writes=[lruw_tb])
    lruw3 = lruw[:].rearrange("p (i d) -> p i d", d=128)
    modF4 = modF[:].rearrange("p (m q r) -> p m q r", m=4, q=8)
    hT3 = hT[:].rearrange("p (k t) -> p k t", k=8)
    h1T3 = h1T[:].rearrange("p (k t) -> p k t", k=8)
    xs3 = xs[:].rearrange("p (c d) -> p c d", c=2)
    unT3 = unT[:].rearrange("p (k t) -> p k t", k=16)
    ylg3 = ylg[:].rearrange("p (k t) -> p k t", k=8)
    merged3 = merged[:].rearrange("p (k t) -> p k t", k=8)
    xtok3 = xtok[:].rearrange("p (c n) -> p c n", c=2)
    btok3 = btok[:].rearrange("p (c n) -> p c n", c=2)
    hsb3 = hsb[:].rearrange("p (c n) -> p c n", c=2)
    EPS = 1e-6

    def ln_stats(c, src_ap, reads):
        s3 = st6[:, 12 * c:12 * c + 12]
        P.add("dve", lambda e: e.bn_stats(out=s3[:, 0:6], in_=src_ap[:, 0:512]), reads, [st_tb[c]])
        P.add("dve", lambda e: e.bn_stats(out=s3[:, 6:12], in_=src_ap[:, 512:1024]), reads, [st_tb[c]])
        P.add("dve", lambda e: e.bn_aggr(out=mv[:, 4 * c:4 * c + 2], in_=s3), [st_tb[c]], [st_tb[c]])
        ts("dve", mv[:, 4 * c + 3:4 * c + 4], mv[:, 4 * c + 1:4 * c + 2], EPS, None, ALU.add, None, [st_tb[c]], [st_tb[c]])
        act(mv[:, 4 * c + 3:4 * c + 4], mv[:, 4 * c + 3:4 * c + 4], AF.Ln, [st_tb[c]], [st_tb[c]])
        act(mv[:, 4 * c + 2:4 * c + 3], mv[:, 4 * c + 3:4 * c + 4], AF.Exp, [st_tb[c]], [st_tb[c]], scale=-0.5)

    def to_hT(c, src_ap, src_reads, dst3, dst_tb, m_shift, m_scale, r):
        ln_stats(c, src_ap, src_reads)
        ts("dve", xn[c][:], src_ap, mv[:, 4 * c:4 * c + 1], mv[:, 4 * c + 2:4 * c + 3], ALU.subtract, ALU.mult,
           src_reads + [st_tb[c]], [xn_tb[c]])
        pt, ptb = psum_bf()
        for kc in range(8):
            trp(pt[:, kc * 128:(kc + 1) * 128], xn[c][:, kc * 128:(kc + 1) * 128], IDNb, [xn_tb[c], cstb_tb], [ptb])
        for kc in range(8):
            eng = "act" if kc % 2 == 0 else "dve"
            affine(eng, dst3[:, kc, c * 128:(c + 1) * 128], pt[:, kc * 128:(kc + 1) * 128],
                   modF4[:, m_scale, kc, r:r + 1], modF4[:, m_shift, kc, r:r + 1], [ptb, modF_tb], [dst_tb[c]])

    def conv(eng, i, u_ap, wcol, bcol, L, reads):
        a3 = cacc[i][:].rearrange("p (r l) -> p r l", l=L)
        u3 = u_ap.rearrange("p (r l) -> p r l", l=L)
        t3 = cth[i][:].rearrange("p (r l) -> p r l", l=L)
        ts(eng, cacc[i][:], u_ap, wcol(2), bcol, ALU.mult, ALU.add, reads, [cacc_tb[i]])
        for (k, dst, srcs) in ((1, slice(1, L), slice(0, L - 1)), (0, slice(2, L), slice(0, L - 2)), (3, slice(0, L - 1), slice(1, L))):
            if eng == "dve":
                stt(eng, a3[:, :, dst], u3[:, :, srcs], wcol(k), a3[:, :, dst], ALU.mult, ALU.add, reads + [cacc_tb[i]], [cacc_tb[i]])
            else:
                ts(eng, t3[:, :, dst], u3[:, :, srcs], wcol(k), None, ALU.mult, None, reads, [cth_tb[i]])
                tt(eng, a3[:, :, dst], a3[:, :, dst], t3[:, :, dst], ALU.add, [cacc_tb[i], cth_tb[i]], [cacc_tb[i]])

    cnt = {"g": 0, "v": 0, "c": 0, "k": 0}

    def tile(b, kind, ti):
        full = kind == "full"
        isctx = kind == "ctx"
        rot["n"] = 8

        def chk(stage):
            if debug == f"{kind}{ti}_{stage}":
                raise _Stop()
        L = 256 if isctx else 64
        r = 4 if isctx else b
        src = ctxin[b] if isctx else xin[b, ti * TT:(ti + 1) * TT, :]
        if kind == "bwd":
            P.dma("pool", hbb[ti * 128:(ti + 1) * 128, :], hTb[:], reads=hTb_tb, writes=[hbb_tbs[ti]])
            cp("pool", lrub[:, ti * 8:(ti + 1) * 8], lruc[:, 8:16], lruc_tb[8:16], [lrub_tb])
        if full:
            P.dma("pool", hTb[:], hbb[ti * 128:(ti + 1) * 128, :], reads=[hbb_tbs[ti]], writes=hTb_tb)
        for c in range(2):
            P.dma("pool", xs3[:, c, :], src[c * 128:(c + 1) * 128, :], writes=[xs_tb[c]])
            to_hT(c, xs3[:, c, :], [xs_tb[c]], hT3, hT_tb, 0, 1, r)
        if debug == "s1":
            raise _Stop()
        wdt, wdt_tb = wload(wbf_in, 0, 8, O_DT, O_DT + 64)
        for c in range(2):
            pt, ptb = psum()
            for kc in range(8):
                mm(pt[:, 0:64], hT3[:, kc, c * 128:(c + 1) * 128], wdt[:, kc, :], kc == 0, kc == 7, [hT_tb[c], wdt_tb], [ptb])
            s = slice(c * 64, (c + 1) * 64)
            tt("dve", dtv[:, s], pt[:, 0:64], misc[:, 0:64], ALU.add, [ptb, misc_tb], [dt_tb[c]])
            stt("dve", dta[:, s], dtv[:, s], -1.0, dtv[:, s], ALU.mult, ALU.max, [dt_tb[c]], [dt_tb[c]])
            act(dta[:, s], dta[:, s], AF.Exp, [dt_tb[c]], [dt_tb[c]], scale=-1.0)
            act(dta[:, s], dta[:, s], AF.Ln, [dt_tb[c]], [dt_tb[c]], bias=1.0)
            stt("dve", dtt[:, s], dtv[:, s], 0.0, dta[:, s], ALU.max, ALU.add, [dt_tb[c]], [dt_tb[c]])
            tt("dve", dtA[:, s], dtt[:, s], misc[:, 64:128], ALU.mult, [dt_tb[c], misc_tb], [dt_tb[c]])
            p2, p2b = psum()
            o = c * 64
            mm(p2[:, 0:32], LE, dtA[:, o:o + 32], True, True, [dt_tb[c], cst_tb], [p2b])
            mm(p2[:, 32:64], GE, dtA[:, o + 32:o + 64], True, True, [dt_tb[c], cst_tb], [p2b])
            mm(p2[:, 64:96], GT, dtA[:, o:o + 32], True, True, [dt_tb[c], cst_tb], [p2b])
            mm(p2[:, 96:128], LT, dtA[:, o + 32:o + 64], True, True, [dt_tb[c], cst_tb], [p2b])
            mm(p2[:, 128:192], ONES, dtA[:, o:o + 64], True, True, [dt_tb[c], cst_tb], [p2b])
            act(eall[:, c * 192:(c + 1) * 192], p2[:, 0:192], AF.Exp, [p2b], [dt_tb[c]])
            tt("dve", dtw[:, s], dtt[:, s], eall[:, c * 192 + 64:c * 192 + 128], ALU.mult, [dt_tb[c]], [dt_tb[c]])

        if debug == "dt":
            raise _Stop()

        def bc4(ap2, col0):
            return ap2[:, col0:col0 + 4].unsqueeze(2).to_broadcast([128, 4, 64])

        lstS = {}

        def ssdA(g):
            rb_ = cnt["g"] % 2
            cnt["g"] += 1
            nq = 4 if full else 3
            wx, wx_tb = wload(wbf_in, 0, 8, O_XBC + g * 512, O_XBC + g * 512 + nq * 128)
            if full:
                if g % 2 == 0:
                    lstS[("wz", g)] = wload(wbf_in, 0, 8, O_Z + g * 256, O_Z + g * 256 + 512)
                    lstS[("wz", g + 1)] = lstS[("wz", g)]
            for q in range(nq):
                pt, ptb = psum()
                for kc in range(8):
                    mm(pt[:, 0:TT], wx[:, kc, q * 128:(q + 1) * 128], hT3[:, kc, :], kc == 0, kc == 7, hT_tb + [wx_tb], [ptb])
                cp("act", usb[q][:], pt[:, 0:TT], [ptb], [usb_tb[q]])
            for q in range(nq):
                ccn = 4 * g + q
                eng = "dve"
                conv(eng, q, usb[q][:], lambda k, ccn=ccn: der[:, D_HCW + ccn * 4 + k:D_HCW + ccn * 4 + k + 1],
                     der[:, D_HCB + ccn:D_HCB + ccn + 1], L, [usb_tb[q], der_tb])
            for q in range(nq):
                eng = "dve"
                act(cth[q][:], cacc[q][:], AF.Tanh, [cacc_tb[q]], [cth_tb[q]])
                if eng == "dve":
                    stt(eng, xbc[rb_][q][:], cth[q][:], 1.0, cacc[q][:], ALU.add, ALU.mult, [cth_tb[q], cacc_tb[q]], [xbc_tb[rb_][q]])
                else:
                    ts(eng, cth[q][:], cth[q][:], 1.0, None, ALU.add, None, [cth_tb[q]], [cth_tb[q]])
                    tt(eng, xbc[rb_][q][:], cth[q][:], cacc[q][:], ALU.mult, [cth_tb[q], cacc_tb[q]], [xbc_tb[rb_][q]])
            if debug == "ssd_a":
                raise _Stop()
            return rb_

        def ssdB(g, rb_):
            wz, wz_tb = lstS.get(("wz", g), (None, None))
            for c in range(2):
                pt, ptb = psum_bf()
                for q in range(3):
                    trp(pt[:, q * 128:(q + 1) * 128], xbc[rb_][q][:, c * 128:(c + 1) * 128], IDNb, [xbc_tb[rb_][q], cstb_tb], [ptb])
                if debug != "ssd_b1":
                    ts("dve", xtok3[:, c, :], pt[:, 0:256], 1.0, None, ALU.mult, None, [ptb], [xtok_tb[c]])
                if debug not in ("ssd_b1", "ssd_b2"):
                    ts("dve", btok3[:, c, :], pt[:, 256:384], 1.0, None, ALU.mult, None, [ptb], [btok_tb[c]])
            if debug in ("ssd_b", "ssd_b1", "ssd_b2"):
                raise _Stop()
            hgf = hTf[:, g * 256:(g + 1) * 256]
            hgb = hTb[:, g * 256:(g + 1) * 256]
            hgf3 = hgf.rearrange("p (h q) -> p h q", q=64)
            hgb3 = hgb.rearrange("p (h q) -> p h q", q=64)

            def versions(c, which):
                rv = cnt["v"] % NV
                cnt["v"] += 1
                x3 = xtok3[:, c, :].rearrange("p (h q) -> p h q", q=64)
                srcs = {0: (dtt, c * 64 + 4 * g), 1: (dtt, c * 64 + 32 + 4 * g), 2: (dtw, c * 64 + 4 * g),
                        3: (dtw, c * 64 + 32 + 4 * g), 4: (misc, 128 + 4 * g)}
                for v in which:
                    t_, col = srcs[v]
                    eng = "dve"
                    tt(eng, ver[rv][v][:].rearrange("p (h q) -> p h q", q=64), x3, bc4(t_, col), ALU.mult,
                       [xtok_tb[c], dt_tb[c], misc_tb], [ver_tb[rv][v]])
                return rv

            def state_update(c, d, rv):
                pt, ptb = psum()
                mm(pt[:, 0:256], btok3[:, c, :], ver[rv][2 + d][:], True, True, [btok_tb[c], ver_tb[rv][2 + d]], [ptb])
                h3 = hgf3 if d == 0 else hgb3
                h2 = hgf if d == 0 else hgb
                htb = hTf_tb[g] if d == 0 else hTb_tb[g]
                dec = bc4(eall, c * 192 + 128 + d * 32 + 4 * g)
                tt("dve", h3, h3, dec, ALU.mult, [htb, dt_tb[c]], [htb])
                tt("dve", h2, h2, pt[:, 0:256], ALU.add, [htb, ptb], [htb])

            if not full:
                if debug == "ssd_c":
                    versions(0, [2])
                    raise _Stop()
                if isctx:
                    for c in range(2):
                        rv = versions(c, [2])
                        state_update(c, 0, rv)
                for c in (1, 0):
                    rv = versions(c, [3])
                    state_update(c, 1, rv)
                return

            for c in (1, 0):
                cp("act", hsb3[:, c, :], hgb, [hTb_tb[g]], [hsb_tb[c]])
                if c == 1:
                    rvb = versions(c, [3])
                    state_update(c, 1, rvb)
            chk("fa")
            for c in range(2):
                rc = cnt["c"] % 2
                cnt["c"] += 1
                rv = versions(c, [0, 1, 2, 4])
                cp("act", hsf[rc][:], hgf, [hTf_tb[g]], [hsf_tb[rc]])
                pz, pzb = psum()
                zc0 = (g % 2) * 256
                for kc in range(8):
                    mm(pz[:, 0:256], hT3[:, kc, c * 128:(c + 1) * 128], wz[:, kc, zc0:zc0 + 256], kc == 0, kc == 7,
                       [hT_tb[c], wz_tb], [pzb])
                pc, pcb = psum()
                mm(pc[:, 0:128], xbc[rb_][2][:, c * 128:(c + 1) * 128], xbc[rb_][3][:, c * 128:(c + 1) * 128], True, True,
                   [xbc_tb[rb_][2], xbc_tb[rb_][3]], [pcb])
                pi_, pib = psum()
                ct = xbc[rb_][3][:, c * 128:(c + 1) * 128]
                mm(pi_[:, 0:256], ct, hsf[rc][:], True, True, [xbc_tb[rb_][3], hsf_tb[rc]], [pib])
                mm(pi_[:, 256:512], ct, hsb3[:, c, :], True, True, [xbc_tb[rb_][3], hsb_tb[c]], [pib])
                segs = []
                for d in range(2):
                    msk = LEb if d == 0 else GEb
                    lh = GTb if d == 0 else LTb
                    col = c * 64 + d * 32 + 4 * g
                    tt("dve", Rt[rc][d][:].rearrange("p (h i) -> p h i", h=4),
                       dtA[:, col:col + 4].unsqueeze(2).to_broadcast([128, 4, 128]),
                       msk.unsqueeze(1).to_broadcast([128, 4, 128]), ALU.mult, [dt_tb[c], cstb_tb], [Rt_tb[rc][d]])
                    pt, ptb = psum()
                    mm(pt[:, 0:512], lh, Rt[rc][d][:], True, True, [Rt_tb[rc][d], cstb_tb], [ptb])
                    segs.append((pt, ptb))
                act(tz[rc][:], pz[:, 0:256], AF.Tanh, [pzb], [tz_tb[rc]], scale=0.5)
                for d in range(2):
                    act(Lt[rc][d][:], segs[d][0][:, 0:512], AF.Exp, [segs[d][1]], [Lt_tb[rc][d]])
                tt("dve", CBm[rc][0][:], pc[:, 0:128], LEb, ALU.mult, [pcb, cstb_tb], [CBm_tb[rc][0]])
                tt("dve", CBm[rc][1][:], pc[:, 0:128], GEb, ALU.mult, [pcb, cstb_tb], [CBm_tb[rc][1]])
                stt("dve", szt[rc][:], tz[rc][:], 1.0, pz[:, 0:256], ALU.add, ALU.mult, [tz_tb[rc], pzb], [sz_tb[rc]])
                for d in range(2):
                    tt("dve", t12[rc][d][:].rearrange("p (h q) -> p h q", q=64),
                       pi_[:, d * 256:(d + 1) * 256].rearrange("p (h q) -> p h q", q=64),
                       bc4(eall, c * 192 + d * 32 + 4 * g), ALU.mult, [pib, dt_tb[c]], [t12_tb[rc][d]])
                tt("dve", t12[rc][0][:], t12[rc][0][:], t12[rc][1][:], ALU.add, [t12_tb[rc][0], t12_tb[rc][1]], [t12_tb[rc][0]])
                for d in range(2):
                    tt("dve", Mt[rc][d][:].rearrange("p (h i) -> p h i", h=4), Lt[rc][d][:].rearrange("p (h i) -> p h i", h=4),
                       CBm[rc][d][:].unsqueeze(1).to_broadcast([128, 4, 128]), ALU.mult,
                       [Lt_tb[rc][d], CBm_tb[rc][d]], [Mt_tb[rc][d]])
                py, pyb = psum()
                for h in range(4):
                    hs = slice(h * 64, (h + 1) * 64)
                    ms = slice(h * 128, (h + 1) * 128)
                    mm(py[:, hs], Mt[rc][0][:, ms], ver[rv][0][:, hs], True, False, [Mt_tb[rc][0], ver_tb[rv][0]], [pyb])
                    mm(py[:, hs], Mt[rc][1][:, ms], ver[rv][1][:, hs], False, False, [Mt_tb[rc][1], ver_tb[rv][1]], [pyb])
                    mm(py[:, hs], IDNb, ver[rv][4][:, hs], False, True, [cstb_tb, ver_tb[rv][4]], [pyb])
                state_update(c, 0, rv)
                tt("dve", yy[rc][:], py[:, 0:256], t12[rc][0][:], ALU.add, [pyb, t12_tb[rc][0]], [yy_tb[rc]])
                tt("dve", ut[rc][:], yy[rc][:], szt[rc][:], ALU.mult, [yy_tb[rc], sz_tb[rc]], [ut_tb[rc]])
                act(junk[:, 0:256], ut[rc][:], AF.Square, [ut_tb[rc]], [junk_tb, ssq_tb[rc]], accum=ssq[:, 2 * rc:2 * rc + 1])
                ts("dve", ssq[:, 2 * rc + 1:2 * rc + 2], ssq[:, 2 * rc:2 * rc + 1], 1024.0 * 1e-5, None, ALU.add, None,
                   [ssq_tb[rc]], [ssq_tb[rc]])
                act(ssq[:, 2 * rc + 1:2 * rc + 2], ssq[:, 2 * rc + 1:2 * rc + 2], AF.Ln, [ssq_tb[rc]], [ssq_tb[rc]])
                act(ssq[:, 2 * rc + 1:2 * rc + 2], ssq[:, 2 * rc + 1:2 * rc + 2], AF.Exp, [ssq_tb[rc]], [ssq_tb[rc]], scale=-0.5)
                ts("dve", unb[rc][:], ut[rc][:], ssq[:, 2 * rc + 1:2 * rc + 2], 16.0, ALU.mult, ALU.mult,
                   [ut_tb[rc], ssq_tb[rc]], [unb_tb[rc]])
                pt, ptb = psum_bf()
                for m in range(2):
                    trp(pt[:, m * 128:(m + 1) * 128], unb[rc][:, m * 128:(m + 1) * 128], IDNb, [unb_tb[rc], cstb_tb], [ptb])
                for m in range(2):
                    k = 2 * g + m
                    ts("dve", unT3[:, k, c * 128:(c + 1) * 128], pt[:, m * 128:(m + 1) * 128],
                       pp[:, PP_NW + k:PP_NW + k + 1], None, ALU.mult, None, [ptb, pp_tb], [unT_tb[k][c]])
                chk("fj")
                if g == 0 and c == 1:
                    chk("fk")
                if g == 1 and c == 1:
                    chk("fl")
                if g == 3 and c == 1:
                    chk("fm")


        rbs = {0: ssdA(0)}
        for g in range(8):
            if g < 7:
                rbs[g + 1] = ssdA(g + 1)
            ssdB(g, rbs[g])
        if debug == "ssd":
            raise _Stop()
        chk("ssd")
        dirs = [1] if kind == "bwd" else [0, 1]
        lst = {}
        X, X2, IN, TG = range(4)
        TR, A_, TI, NA2, S_, IU, B_, H_ = range(8)

        def stageA(k):
            ru = cnt["k"] % 2
            cnt["k"] += 1
            lst[k] = ru
            if k % 4 == 0:
                lst["wl"] = wload(wbf_in, 0, 8, O_LRU + k * 128, O_LRU + k * 128 + 512)
                if full:
                    lst["wg"] = wload(wbf_in, 0, 8, O_LG + k * 128, O_LG + k * 128 + 512)
            wl, wl_tb = lst["wl"]
            kk = (k % 4) * 128
            pt, ptb = psum()
            for kc in range(8):
                mm(pt[:, 0:TT], wl[:, kc, kk:kk + 128], hT3[:, kc, :], kc == 0, kc == 7, hT_tb + [wl_tb], [ptb])
            cp("act", usb[ru][:], pt[:, 0:TT], [ptb], [usb_tb[ru]])
            conv("dve", ru, usb[ru][:], lambda t_, k=k: pp[:, PP_LCW + k * 4 + t_:PP_LCW + k * 4 + t_ + 1],
                 pp[:, PP_LCB + k:PP_LCB + k + 1], L, [usb_tb[ru], pp_tb])
            cp("act", ubf[ru][:], cacc[ru][:], [cacc_tb[ru]], [ubf_tb[ru]])
            if full:
                wg, wg_tb = lst["wg"]
                pg, pgb = psum()
                for kc in range(8):
                    mm(pg[:, 0:TT], wg[:, kc, kk:kk + 128], hT3[:, kc, :], kc == 0, kc == 7, hT_tb + [wg_tb], [pgb])
                cp("act", gtmp[X][:], pg[:, 0:TT], [pgb], [gtmp_tb[X]])
                act(gtmp[X2][:], pg[:, 0:TT], AF.Square, [pgb], [gtmp_tb[X2]])
                ts("dve", gtmp[X2][:], gtmp[X2][:], 0.044715, 1.0, ALU.mult, ALU.add, [gtmp_tb[X2]], [gtmp_tb[X2]])
                tt("dve", gtmp[IN][:], gtmp[X2][:], gtmp[X][:], ALU.mult, [gtmp_tb[X2], gtmp_tb[X]], [gtmp_tb[IN]])
                act(gtmp[TG][:], gtmp[IN][:], AF.Tanh, [gtmp_tb[IN]], [gtmp_tb[TG]], scale=0.7978845608028654)
                stt("dve", ylg3[:, k, :], gtmp[TG][:], 1.0, gtmp[X][:], ALU.add, ALU.mult, [gtmp_tb[TG], gtmp_tb[X]], [ylg_tb[k]])

        def stageB(k):
            ru = lst[k]
            uu = cacc[ru]
            uu_tb = cacc_tb[ru]
            for d in dirs:
                pr, prb = psum()
                mm(pr[:, 0:TT], lruw3[:, (d * 2 + 0) * 8 + k, :], ubf[ru][:], True, True, [lruw_tb, ubf_tb[ru]], [prb])
                mm(pr[:, 256:256 + TT], lruw3[:, (d * 2 + 1) * 8 + k, :], ubf[ru][:], True, True, [lruw_tb, ubf_tb[ru]], [prb])
                idx = d * 8 + k
                act(ltmp[TR][:], pr[:, 0:TT], AF.Tanh, [prb, der_tb], [ltmp_tb[TR]], scale=0.5, bias=der[:, D_HBA + idx:D_HBA + idx + 1])
                aout = ltmp[A_][:] if d == 0 else ltmp[A_][:, ::-1]
                act(aout, ltmp[TR][:], AF.Exp, [ltmp_tb[TR], der_tb], [ltmp_tb[A_]],
                    scale=der[:, D_C1H + idx:D_C1H + idx + 1], bias=der[:, D_C1H + idx:D_C1H + idx + 1])
                act(ltmp[TI][:], pr[:, 256:256 + TT], AF.Tanh, [prb, der_tb], [ltmp_tb[TI]], scale=0.5,
                    bias=der[:, D_HBI + idx:D_HBI + idx + 1])
                stt("dve", ltmp[NA2][:], ltmp[A_][:], -1.0, ltmp[A_][:], ALU.mult, ALU.mult, [ltmp_tb[A_]], [ltmp_tb[NA2]])
                ts("dve", ltmp[S_][:], ltmp[NA2][:], 1.0, 1e-30, ALU.add, ALU.max, [ltmp_tb[NA2]], [ltmp_tb[S_]])
                act(ltmp[S_][:], ltmp[S_][:], AF.Ln, [ltmp_tb[S_]], [ltmp_tb[S_]])
                act(ltmp[S_][:], ltmp[S_][:], AF.Exp, [ltmp_tb[S_]], [ltmp_tb[S_]], scale=0.5)
                stt_p(ltmp[IU][:], ltmp[TI][:], 1.0, uu[:], ALU.add, ALU.mult, [ltmp_tb[TI], uu_tb], [ltmp_tb[IU]])
                iu_in = ltmp[IU][:] if d == 0 else ltmp[IU][:, ::-1]
                stt("dve", ltmp[B_][:], ltmp[S_][:], 0.5, iu_in, ALU.mult, ALU.mult, [ltmp_tb[S_], ltmp_tb[IU]], [ltmp_tb[B_]])
                if d == 0 or not full:
                    init = lruc[:, idx:idx + 1]
                    init_tb = lruc_tb[idx]
                else:
                    init = lrub[:, ti * 8 + k:ti * 8 + k + 1]
                    init_tb = lrub_tb
                hout = hfw if (d == 0 and full) else ltmp[H_]
                hout_tb = hfw_tb if (d == 0 and full) else ltmp_tb[H_]
                P.add("dve", lambda e, hout=hout, init=init: e.tensor_tensor_scan(
                    out=hout[:], data0=ltmp[A_][:], data1=ltmp[B_][:], initial=init, op0=ALU.mult, op1=ALU.add),
                    [ltmp_tb[A_], ltmp_tb[B_], init_tb], [hout_tb])
                if d == 0 or not full:
                    cp("pool", lruc[:, idx:idx + 1], hout[:, TT - 1:TT], [hout_tb], [lruc_tb[idx]])
                if full and d == 1:
                    tt("dve", ysum[:], hfw[:], ltmp[H_][:, ::-1], ALU.add, [hfw_tb, ltmp_tb[H_]], [ysum_tb])
            if full:
                stt("dve", ylg3[:, k, :], ysum[:], 0.5, ylg3[:, k, :], ALU.mult, ALU.mult, [ysum_tb, ylg_tb[k]], [ylg_tb[k]])

        stageA(0)
        for k in range(8):
            if k < 7:
                stageA(k + 1)
            stageB(k)
        chk("lru")
        if not full:
            return

        all_unT = [unT_tb[k][c] for k in range(16) for c in range(2)]
        for dq in range(8):
            if dq % 2 == 0:
                wbs, wbs_tb = wload(wbf_bs, 0, 16, dq * 128, dq * 128 + 256)
                (wgs, wgl), wgs_tb = wload_multi([(wbf_in, 0, 8, O_GS + dq * 128, O_GS + dq * 128 + 256),
                                                  (wbf_in, 0, 8, O_GL + dq * 128, O_GL + dq * 128 + 256)])
                wgl_tb = wgs_tb
                wbl, wbl_tb = wload(wbf_bl, 0, 8, dq * 128, dq * 128 + 256)
            pb, pbb = psum()
            o2 = (dq % 2) * 128
            o4 = (dq % 4) * 128
            for kc in range(16):
                mm(pb[:, 0:TT], wbs[:, kc, o2:o2 + 128], unT3[:, kc, :], kc == 0, kc == 15, all_unT + [wbs_tb], [pbb])
            for kc in range(8):
                mm(pb[:, 256:256 + TT], wbl[:, kc, o2:o2 + 128], ylg3[:, kc, :], kc == 0, kc == 7, ylg_tb + [wbl_tb], [pbb])
            pgt, pgtb = psum()
            for kc in range(8):
                mm(pgt[:, 0:TT], wgs[:, kc, o2:o2 + 128], hT3[:, kc, :], kc == 0, kc == 7, hT_tb + [wgs_tb], [pgtb])
            for kc in range(8):
                mm(pgt[:, 256:256 + TT], wgl[:, kc, o2:o2 + 128], hT3[:, kc, :], kc == 0, kc == 7, hT_tb + [wgl_tb], [pgtb])
            act(tg[0][:], pgt[:, 0:TT], AF.Tanh, [pgtb, der_tb], [tg_tb[0]], scale=0.5, bias=der[:, D_HBG + dq:D_HBG + dq + 1])
            act(tg[1][:], pgt[:, 256:256 + TT], AF.Tanh, [pgtb, der_tb], [tg_tb[1]], scale=0.5,
                bias=der[:, D_HBG + 8 + dq:D_HBG + 8 + dq + 1])
            stt("dve", m12[0][:], tg[0][:], 1.0, pb[:, 0:TT], ALU.add, ALU.mult, [tg_tb[0], pbb], [m12_tb[0]])
            stt("dve", m12[1][:], tg[1][:], 1.0, pb[:, 256:256 + TT], ALU.add, ALU.mult, [tg_tb[1], pbb], [m12_tb[1]])
            tt("dve", merged3[:, dq, :], m12[0][:], m12[1][:], ALU.add, [m12_tb[0], m12_tb[1]], [merged_tb[dq]])

        chk("s8")
        rot["n"] = 4
        wo = [wload(wbf_o, 0, 8, half * 512, (half + 1) * 512) for half in range(2)]
        for c in range(2):
            for half in range(2):
                pm, pmb = psum_t[4 + half], psum_tb[4 + half]
                for kc in range(8):
                    mm(pm[:, 0:512], merged3[:, kc, c * 128:(c + 1) * 128], wo[half][0][:, kc, :], kc == 0, kc == 7,
                       merged_tb + [wo[half][1]], [pmb])
                xsl = xs3[:, c, half * 512:(half + 1) * 512]
                tt("dve", rtmp[half][:], pm[:, 0:512], G[:, half * 512:(half + 1) * 512], ALU.mult, [pmb, G_tb], [rtmp_tb[half]])
                stt_p(xsl, xsl, ALPHA, rtmp[half][:], ALU.mult, ALU.add, [xs_tb[c], rtmp_tb[half]], [xs_tb[c]])
            x2 = xs3[:, c, :]
            ln_stats(c, x2, [xs_tb[c]])
            ts("dve", x2, x2, mv[:, 4 * c:4 * c + 1], mv[:, 4 * c + 2:4 * c + 3], ALU.subtract, ALU.mult, [xs_tb[c], st_tb[c]], [xs_tb[c]])
            tt("dve", x2, x2, lnt[:, 0:1024], ALU.mult, [xs_tb[c], lnt_tb], [xs_tb[c]])
            tt("dve", x2, x2, lnt[:, 1024:2048], ALU.add, [xs_tb[c], lnt_tb], [xs_tb[c]])
            to_hT(c, x2, [xs_tb[c]], h1T3, h1T_tb, 2, 3, r)

        chk("s9")
        for q in range(4):
            hq = hid[q % 2]
            hq3 = hq[:].rearrange("p (f t) -> p f t", f=8)
            for fq in range(8):
                f = 8 * q + fq
                if f % 4 == 0:
                    w1, w1_tb = wload(wbf_m1, 0, 8, f * 128, f * 128 + 512)
                o4 = (f % 4) * 128
                ph, phb = psum()
                for kc in range(8):
                    mm(ph[:, 0:TT], w1[:, kc, o4:o4 + 128], h1T3[:, kc, :], kc == 0, kc == 7, h1T_tb + [w1_tb], [phb])
                act(rr[f % 2][:], ph[:, 0:TT], AF.Relu, [phb, pp_tb], [rr_tb[f % 2]], bias=pp[:, PP_B1 + f:PP_B1 + f + 1])
                if f % 2 == 0:
                    act(hq3[:, fq, :], rr[f % 2][:], AF.Square, [rr_tb[f % 2]], [hid_tb[q % 2][fq]])
                else:
                    tt("dve", hq3[:, fq, :], rr[f % 2][:], rr[f % 2][:], ALU.mult, [rr_tb[f % 2]], [hid_tb[q % 2][fq]])
            for half in range(2):
                w2, w2_tb = wload(wbf_m2, 8 * q, 8 * q + 8, half * 512, (half + 1) * 512)
                for c in range(2):
                    pa, pab = psum_t[4 + 2 * c + half], psum_tb[4 + 2 * c + half]
                    for fq in range(8):
                        mm(pa[:, 0:512], hq3[:, fq, c * 128:(c + 1) * 128], w2[:, fq, :], q == 0 and fq == 0, False,
                           [hid_tb[q % 2][fq], w2_tb], [pab])
        for c in range(2):
            for half in range(2):
                pa, pab = psum_t[4 + 2 * c + half], psum_tb[4 + 2 * c + half]
                mm(pa[:, 0:512], ONESb[0:1, :], b2row[0:1, half * 512:(half + 1) * 512], False, True, [cstb_tb, b2row_tb], [pab])
                xsl = xs3[:, c, half * 512:(half + 1) * 512]
                tt("dve", rtmp[half][:], pa[:, 0:512], G[:, 1024 + half * 512:1024 + (half + 1) * 512], ALU.mult,
                   [pab, G_tb], [rtmp_tb[half]])
                stt_p(xsl, xsl, ALPHA, rtmp[half][:], ALU.mult, ALU.add, [xs_tb[c], rtmp_tb[half]], [xs_tb[c]])
            x2 = xs3[:, c, :]
            ln_stats(c, x2, [xs_tb[c]])
            ts("dve", x2, x2, mv[:, 4 * c:4 * c + 1], mv[:, 4 * c + 2:4 * c + 3], ALU.subtract, ALU.mult, [xs_tb[c], st_tb[c]], [xs_tb[c]])
            tt("dve", x2, x2, lnt[:, 2048:3072], ALU.mult, [xs_tb[c], lnt_tb], [xs_tb[c]])
            tt("dve", x2, x2, lnt[:, 3072:4096], ALU.add, [xs_tb[c], lnt_tb], [xs_tb[c]])
            P.dma("pool", outd[b, ti * TT + c * 128: ti * TT + (c + 1) * 128, :], x2, reads=[xs_tb[c]], writes=[TB()], is_output=True)

    hbb_tbs = [TB(f"hbb{i}") for i in range(NTILE)]
    try:
        for b in range(NB):
            for gi in range(2):
                r0 = (b * 2 + gi) * 128
                P.dma("sp", G[:, gi * 1024:(gi + 1) * 1024], gsc[r0:r0 + 128, :], writes=[G_tb])
            memset("pool", hTf[:], 0.0, hTf_tb)
            memset("pool", hTb[:], 0.0, hTb_tb)
            memset("pool", lruc[:], 0.0, lruc_tb)
            tile(b, "ctx", 0)
            if debug in ("ctx", "s1", "dt", "ssd", "ssd_a", "ssd_b", "ssd_c", "ssd_b1", "ssd_b2"):
                raise _Stop()
            for ti in reversed(range(NTILE)):
                tile(b, "bwd", ti)
            if debug == "bwdall":
                raise _Stop()
            for ti in range(NTILE):
                tile(b, "full", ti)
    except _Stop:
        pass
    if debug:
        P.dma("sp", dbg["hTf"][:, :], hTf[:], reads=hTf_tb, writes=[TB()], is_output=True)
        P.dma("sp", dbg["hTb"][:, :], hTb[:], reads=hTb_tb, writes=[TB()], is_output=True)
        P.dma("sp", dbg["lruc"][:, :], lruc[:], reads=lruc_tb, writes=[TB()], is_output=True)
        P.dma("sp", dbg["hT"][:, :], hT[:], reads=hT_tb, writes=[TB()], is_output=True)
        P.dma("sp", dbg["xs"][:, :], xs[:], reads=xs_tb, writes=[TB()], is_output=True)
        P.dma("sp", dbg["unT"][:, :], unT[:], reads=[unT_tb[k][c] for k in range(16) for c in range(2)], writes=[TB()], is_output=True)
        P.dma("sp", dbg["ylg"][:, :], ylg[:], reads=ylg_tb, writes=[TB()], is_output=True)
        P.dma("sp", dbg["merged"][:, :], merged[:], reads=merged_tb, writes=[TB()], is_output=True)
        for q in range(4):
            P.dma("sp", dbg["xbc"][:, q * TT:(q + 1) * TT], xbc[1][q][:], reads=[xbc_tb[1][q]], writes=[TB()], is_output=True)
        P.dma("sp", dbg["dtt"][:, :], dtt[:], reads=dt_tb, writes=[TB()], is_output=True)
        P.dma("sp", dbg["eall"][:, :], eall[:], reads=dt_tb, writes=[TB()], is_output=True)
        P.dma("sp", dbg["xtok"][:, :], xtok[:], reads=xtok_tb, writes=[TB()], is_output=True)
        P.dma("sp", dbg["btok"][:, :], btok[:], reads=btok_tb, writes=[TB()], is_output=True)
    P.finalize()
    return nc

_CACHE = {}


def _consts():
    k = np.arange(128)[:, None]
    i = np.arange(128)[None, :]
    mats = [(k <= i), (k > i), (k >= i), (k < i), np.ones((128, 128), bool), (k == i)]
    return np.concatenate([m.astype(np.float32) for m in mats], axis=1)


def _kmajor(w):
    K, N = w.shape
    return np.ascontiguousarray(w.reshape(K // 128, 128, N).transpose(1, 0, 2).reshape(128, (K // 128) * N))


def _cols128(v):
    return np.ascontiguousarray(v.reshape(-1, 128).T)


def prep_shared(inp):
    f32 = np.float32
    w_in = np.asarray(inp["w_in"][0], f32)
    cols = []
    for g in range(8):
        cols += list(range(256 * g, 256 * g + 256))
        cols += list(range(2048 + 128 * g, 2048 + 128 * g + 128))
        cols += list(range(4160 + 128 * g, 4160 + 128 * g + 128))
    cols += list(range(5184, 7232)) + list(range(3072, 3136)) + list(range(3136, 4160))
    cols += list(range(7232, 8256)) + list(range(8256, 10304))
    cols = np.asarray(cols)
    assert cols.shape[0] == 10304 and np.unique(cols).shape[0] == 10304
    lw = np.stack([np.asarray(inp["lru_wa"][0], f32), np.asarray(inp["lru_wi"][0], f32)], axis=1)
    lw = np.ascontiguousarray(lw.transpose(3, 0, 1, 2, 4).reshape(128, 32 * 128))
    wall = np.concatenate([
        _kmajor(w_in[:, cols]), _kmajor(np.asarray(inp["w_br_ssd"][0], f32)), _kmajor(np.asarray(inp["w_br_lru"][0], f32)),
        _kmajor(np.asarray(inp["w_out"][0], f32)), _kmajor(np.asarray(inp["w_mlp1"][0], f32)),
        _kmajor(np.asarray(inp["w_mlp2"][0], f32)), lw], axis=1)
    assert wall.shape == (128, W_TOT)
    cw = np.asarray(inp["ssd_conv_w"][0], f32)
    cb = np.asarray(inp["ssd_conv_b"][0], f32)
    pp = np.zeros((128, NPP), f32)
    for g in range(8):
        for q, ch0 in enumerate([256 * g, 256 * g + 128, 2048 + 128 * g, 3072 + 128 * g]):
            ccn = 4 * g + q
            pp[:, PP_CW + ccn * 4:PP_CW + ccn * 4 + 4] = cw[:, ch0:ch0 + 128].T
            pp[:, PP_CB + ccn] = cb[ch0:ch0 + 128]
    lcw = np.asarray(inp["lru_conv_w"][0], f32)
    for k in range(8):
        pp[:, PP_LCW + k * 4:PP_LCW + k * 4 + 4] = lcw[:, 128 * k:128 * k + 128].T
    pp[:, PP_LCB:PP_LCB + 8] = _cols128(np.asarray(inp["lru_conv_b"][0], f32))
    pp[:, PP_BA:PP_BA + 16] = _cols128(np.asarray(inp["lru_ba"][0], f32).reshape(-1))
    pp[:, PP_BI:PP_BI + 16] = _cols128(np.asarray(inp["lru_bi"][0], f32).reshape(-1))
    pp[:, PP_LAM:PP_LAM + 16] = _cols128(np.asarray(inp["lru_lambda"][0], f32).reshape(-1))
    pp[:, PP_BG:PP_BG + 16] = _cols128(np.asarray(inp["b_gate"][0], f32))
    pp[:, PP_NW:PP_NW + 16] = _cols128(np.asarray(inp["ssd_norm_w"][0], f32))
    pp[:, PP_B1:PP_B1 + 32] = _cols128(np.asarray(inp["b_mlp1"][0], f32))
    bm = np.asarray(inp["b_mod"][0], f32)
    pp[:, PP_BM:PP_BM + 48] = _cols128(bm)
    rb = np.concatenate([np.asarray(inp[k][0], f32).reshape(-1) for k in ("ln1_g", "ln1_b", "ln2_g", "ln2_b", "b_mlp2")]
                        + [bm[2048:3072], bm[5120:6144], np.asarray(inp["ssd_dt_bias"][0], f32).reshape(-1),
                           np.asarray(inp["ssd_a_log"][0], f32).reshape(-1), np.asarray(inp["ssd_d"][0], f32).reshape(-1)])
    assert rb.shape[0] == NRB
    wmod = _kmajor(np.asarray(inp["w_mod"][0], f32))
    return dict(wall=wall, pp=pp, rb=np.ascontiguousarray(rb), wmod=wmod, consts=_consts())


def core_inputs(inp, shared, b0, NB):
    f32 = np.float32
    cc = np.zeros((5, 1024), f32)
    cc[:NB] = np.asarray(inp["c"], f32)[b0:b0 + NB]
    cc[4] = np.asarray(inp["c_ctx"], f32)
    ccT = np.ascontiguousarray(cc.reshape(5, 8, 128).transpose(2, 1, 0).reshape(128, 40))
    d = dict(shared)
    d["xin"] = np.ascontiguousarray(np.asarray(inp["x"], f32)[b0:b0 + NB])
    d["ctxin"] = np.ascontiguousarray(np.asarray(inp["ctx"], f32)[b0:b0 + NB])
    d["ccT"] = ccT
    return d


def kernel(**inputs):
    NB = 4
    if "nc" not in _CACHE:
        _CACHE["nc"] = build_program(NB)
    nc = _CACHE["nc"]
    shared = prep_shared(inputs)
    in_maps = [core_inputs(inputs, shared, 4 * i, NB) for i in range(8)]
    res = run_bass_kernel_spmd(nc, in_maps, core_ids=list(range(8)))
    return np.concatenate([r["out"] for r in res.results], axis=0)
```
